# Optimizing a Trainium2 kernel written in Bass

```python
import math
import jax, jax.numpy as jnp
from jax import lax
import numpy as np

D_MODEL = 1024
BATCH = 4
SEQ = 4096
DEPTH = 1
DEC_BATCH = 128
DEC_SEQ = 8
PAST_LEN = 8192
PAGE_SIZE = 128

N_Q_HEADS = 8
N_KV_HEADS = 2
GROUP = N_Q_HEADS // N_KV_HEADS
HEAD_DIM = 64
WINDOW = 128
N_BUCKETS = 32
MAX_DISTANCE = 128
ATTN_WIDTH = N_Q_HEADS * HEAD_DIM
KV_WIDTH = N_KV_HEADS * HEAD_DIM
HG_HEADS = 4
HG_KDIM = 128
HG_VDIM = 128
HG_WIDTH = HG_HEADS * HG_KDIM
HG_VWIDTH = HG_HEADS * HG_VDIM
HG_CHUNK = 64
D_FF = 2816
EPS = 1e-6
F_TINY = 1e-30
SPLIT_SIZES = (ATTN_WIDTH, KV_WIDTH, KV_WIDTH, HG_WIDTH, HG_WIDTH, HG_VWIDTH, HG_VWIDTH, D_MODEL, D_MODEL)
IN_WIDTH = sum(SPLIT_SIZES)

kernel_name = "hybrid_swa_sink_hgrn2_macaron_step"


def _rmsnorm(x, g):
    xf = x.astype(jnp.float32)
    y = xf * lax.rsqrt(jnp.mean(xf * xf, axis=-1, keepdims=True) + EPS)
    return (y * g.astype(jnp.float32)).astype(x.dtype)


def _swiglu(x, w1, w3, w2):
    return (jax.nn.silu(x @ w1) * (x @ w3)) @ w2


def _t5_bucket(dist):
    max_exact = N_BUCKETS // 2
    d = np.maximum(dist, 0)
    large = max_exact + (np.log(np.maximum(d, 1) / max_exact) / np.log(MAX_DISTANCE / max_exact)
                         * (N_BUCKETS - max_exact)).astype(np.int32)
    large = np.minimum(large, N_BUCKETS - 1)
    return np.where(d < max_exact, d, large).astype(np.int32)


def _rel_bias(table, dist):
    b = table.astype(jnp.float32)[_t5_bucket(dist)]
    return jnp.transpose(b, (2, 0, 1)).reshape(N_KV_HEADS, GROUP, dist.shape[0], dist.shape[1])


def _sink_softmax(s, sinks):
    sk = sinks.astype(jnp.float32).reshape(N_KV_HEADS, GROUP, 1, 1)
    m = jnp.maximum(jnp.max(s, axis=-1, keepdims=True), sk)
    e = jnp.exp(s - m)
    return e / (jnp.sum(e, axis=-1, keepdims=True) + jnp.exp(sk - m))


def _swa_prompt(q, k, v, table, sinks):
    B, L = q.shape[:2]
    nb = L // WINDOW
    qb = q.reshape(B, nb, WINDOW, N_KV_HEADS, GROUP, HEAD_DIM)
    kb = k.reshape(B, nb, WINDOW, N_KV_HEADS, HEAD_DIM)
    vb = v.reshape(B, nb, WINDOW, N_KV_HEADS, HEAD_DIM)
    kk = jnp.concatenate([jnp.concatenate([jnp.zeros_like(kb[:, :1]), kb[:, :-1]], axis=1), kb], axis=2)
    vv = jnp.concatenate([jnp.concatenate([jnp.zeros_like(vb[:, :1]), vb[:, :-1]], axis=1), vb], axis=2)
    s = jnp.einsum('bnqkgd,bnjkd->bnkgqj', qb, kk).astype(jnp.float32) * (HEAD_DIM ** -0.5)
    dist = np.arange(WINDOW)[:, None] + WINDOW - np.arange(2 * WINDOW)[None, :]
    valid = (dist >= 0) & (dist <= WINDOW)
    blk_valid = valid[None] & ((np.arange(nb)[:, None, None] > 0) | (np.arange(2 * WINDOW) >= WINDOW)[None, None, :])
    s = jnp.where(blk_valid[None, :, None, None], s + _rel_bias(table, dist)[None, None], -jnp.inf)
    p = _sink_softmax(s, sinks)
    o = jnp.einsum('bnkgqj,bnjkd->bnqkgd', p.astype(v.dtype), vv)
    return o.reshape(B, L, ATTN_WIDTH)


def _swa_sample(q, k, v, win_k, win_v, table, sinks):
    Bd, Ld = q.shape[:2]
    kk = jnp.concatenate([win_k.astype(k.dtype), k], axis=1)
    vv = jnp.concatenate([win_v.astype(v.dtype), v], axis=1)
    qg = q.reshape(Bd, Ld, N_KV_HEADS, GROUP, HEAD_DIM)
    s = jnp.einsum('bqkgd,bjkd->bkgqj', qg, kk).astype(jnp.float32) * (HEAD_DIM ** -0.5)
    dist = np.arange(Ld)[:, None] + WINDOW - np.arange(WINDOW + Ld)[None, :]
    valid = (dist >= 0) & (dist <= WINDOW)
    s = jnp.where(valid, s + _rel_bias(table, dist)[None], -jnp.inf)
    p = _sink_softmax(s, sinks)
    o = jnp.einsum('bkgqj,bjkd->bqkgd', p.astype(v.dtype), vv)
    return o.reshape(Bd, Ld, ATTN_WIDTH), kk[:, -WINDOW:], vv[:, -WINDOW:]


def _hgrn2(q, f_logit, v, lb, S0):
    B, L = q.shape[:2]
    C = math.gcd(L, HG_CHUNK)
    n = L // C
    f = lb + (1.0 - lb) * jax.nn.sigmoid(f_logit.astype(jnp.float32))
    log_f = jnp.log(jnp.maximum(f, F_TINY))
    k = 1.0 - f
    qf = jax.nn.silu(q.astype(jnp.float32))
    vf = v.astype(jnp.float32)

    def chunks(t, d):
        return t.reshape(B, n, C, HG_HEADS, d).transpose(1, 0, 3, 2, 4)

    tri = jnp.tril(jnp.ones((C, C), dtype=bool))[:, :, None]

    def step(S, inp):
        qc, kc, vc, lc = inp
        G = jnp.cumsum(lc, axis=2)
        inter = jnp.einsum('bhtk,bhkv->bhtv', qc * jnp.exp(G), S)
        diff = G[:, :, :, None, :] - G[:, :, None, :, :]
        decay = jnp.exp(jnp.where(tri, diff, -jnp.inf))
        att = jnp.einsum('bhtk,bhtsk->bhts', qc, decay * kc[:, :, None, :, :])
        out = inter + jnp.einsum('bhts,bhsv->bhtv', att, vc)
        G_last = G[:, :, -1:, :]
        S = jnp.exp(G_last[:, :, 0, :, None]) * S + jnp.einsum('bhsk,bhsv->bhkv', kc * jnp.exp(G_last - G), vc)
        return S, out

    S, o = lax.scan(step, S0.astype(jnp.float32),
                    (chunks(qf, HG_KDIM), chunks(k, HG_KDIM), chunks(vf, HG_VDIM), chunks(log_f, HG_KDIM)))
    o = o.transpose(1, 0, 3, 2, 4).reshape(B, L, HG_HEADS, HG_VDIM)
    return o, S


def _layer(x, win_k, win_v, S0, lb, table, w):
    B, L = x.shape[:2]
    h = x + 0.5 * _swiglu(_rmsnorm(x, w['ffn1_norm']), w['ffn1_w1'], w['ffn1_w3'], w['ffn1_w2'])
    u = _rmsnorm(h, w['mix_norm'])
    proj = u @ w['w_in']
    pts = np.cumsum(np.array(SPLIT_SIZES[:-1])).tolist()
    qa, ka, va, qr, fr, ir, gr, gate_a, gate_r = jnp.split(proj, pts, axis=-1)
    qa = _rmsnorm(qa.reshape(B, L, N_Q_HEADS, HEAD_DIM), w['q_norm'])
    ka = _rmsnorm(ka.reshape(B, L, N_KV_HEADS, HEAD_DIM), w['k_norm'])
    va = va.reshape(B, L, N_KV_HEADS, HEAD_DIM)
    if win_k is None:
        att = _swa_prompt(qa, ka, va, table, w['sinks'])
        new_k, new_v = ka[:, -WINDOW:], va[:, -WINDOW:]
    else:
        att, new_k, new_v = _swa_sample(qa, ka, va, win_k, win_v, table, w['sinks'])
    o_r, S = _hgrn2(qr, fr, ir, lb, S0)
    o_r = (_rmsnorm(o_r, w['hg_norm']).reshape(B, L, HG_VWIDTH).astype(x.dtype)) * jax.nn.silu(gr)
    merged = jax.nn.sigmoid(gate_a) * (att @ w['w_up_attn']) + jax.nn.sigmoid(gate_r) * (o_r @ w['w_up_hgrn'])
    h = h + merged @ w['w_out']
    y = h + 0.5 * _swiglu(_rmsnorm(h, w['ffn2_norm']), w['ffn2_w1'], w['ffn2_w3'], w['ffn2_w2'])
    return y, new_k, new_v, S.astype(x.dtype)


def setup_inputs(seed: int = 0) -> dict:
    key = jax.random.key(seed)
    ks = iter(jax.random.split(key, 32))

    def nrm(shape, scale):
        return jax.random.normal(next(ks), shape, jnp.float32) * scale

    def gain(shape):
        return 1.0 + nrm(shape, 0.02)

    return {
        "x_prompt": nrm((BATCH, SEQ, D_MODEL), 1.0),
        "x_sample": nrm((DEC_BATCH, DEC_SEQ, D_MODEL), 1.0),
        "cache_win_k": nrm((DEPTH, DEC_BATCH, WINDOW, N_KV_HEADS, HEAD_DIM), 1.0),
        "cache_win_v": nrm((DEPTH, DEC_BATCH, WINDOW, N_KV_HEADS, HEAD_DIM), 1.0),
        "state_hgrn": nrm((DEPTH, DEC_BATCH, HG_HEADS, HG_KDIM, HG_VDIM), 0.5),
        "ffn1_norm": gain((DEPTH, D_MODEL)),
        "ffn1_w1": nrm((DEPTH, D_MODEL, D_FF), D_MODEL ** -0.5),
        "ffn1_w3": nrm((DEPTH, D_MODEL, D_FF), D_MODEL ** -0.5),
        "ffn1_w2": nrm((DEPTH, D_FF, D_MODEL), D_FF ** -0.5),
        "mix_norm": gain((DEPTH, D_MODEL)),
        "w_in": nrm((DEPTH, D_MODEL, IN_WIDTH), D_MODEL ** -0.5),
        "q_norm": gain((DEPTH, HEAD_DIM)),
        "k_norm": gain((DEPTH, HEAD_DIM)),
        "sinks": nrm((DEPTH, N_Q_HEADS), 0.5),
        "rel_bias_table": nrm((N_BUCKETS, N_Q_HEADS), 0.5),
        "hgrn_lb_logits": nrm((DEPTH + 1, HG_WIDTH), 0.5),
        "hg_norm": gain((DEPTH, HG_VDIM)),
        "w_up_attn": nrm((DEPTH, ATTN_WIDTH, D_MODEL), ATTN_WIDTH ** -0.5),
        "w_up_hgrn": nrm((DEPTH, HG_VWIDTH, D_MODEL), HG_VWIDTH ** -0.5),
        "w_out": nrm((DEPTH, D_MODEL, D_MODEL), D_MODEL ** -0.5),
        "ffn2_norm": gain((DEPTH, D_MODEL)),
        "ffn2_w1": nrm((DEPTH, D_MODEL, D_FF), D_MODEL ** -0.5),
        "ffn2_w3": nrm((DEPTH, D_MODEL, D_FF), D_MODEL ** -0.5),
        "ffn2_w2": nrm((DEPTH, D_FF, D_MODEL), D_FF ** -0.5),
    }


def reference(x_prompt, x_sample, cache_win_k, cache_win_v, state_hgrn,
              ffn1_norm, ffn1_w1, ffn1_w3, ffn1_w2, mix_norm, w_in, q_norm, k_norm, sinks,
              rel_bias_table, hgrn_lb_logits, hg_norm, w_up_attn, w_up_hgrn, w_out,
              ffn2_norm, ffn2_w1, ffn2_w3, ffn2_w2):
    lb_all = jnp.cumsum(jax.nn.softmax(hgrn_lb_logits.astype(jnp.float32), axis=0), axis=0)
    yp, ys = x_prompt, x_sample
    pk, pv, ps, sk, sv, ss = [], [], [], [], [], []
    for l in range(DEPTH):
        w = {
            'ffn1_norm': ffn1_norm[l], 'ffn1_w1': ffn1_w1[l], 'ffn1_w3': ffn1_w3[l], 'ffn1_w2': ffn1_w2[l],
            'mix_norm': mix_norm[l], 'w_in': w_in[l], 'q_norm': q_norm[l], 'k_norm': k_norm[l],
            'sinks': sinks[l], 'hg_norm': hg_norm[l], 'w_up_attn': w_up_attn[l], 'w_up_hgrn': w_up_hgrn[l],
            'w_out': w_out[l], 'ffn2_norm': ffn2_norm[l], 'ffn2_w1': ffn2_w1[l], 'ffn2_w3': ffn2_w3[l],
            'ffn2_w2': ffn2_w2[l],
        }
        S0 = jnp.zeros((yp.shape[0], HG_HEADS, HG_KDIM, HG_VDIM), jnp.float32)
        yp, k1, v1, s1 = _layer(yp, None, None, S0, lb_all[l], rel_bias_table, w)
        ys, k2, v2, s2 = _layer(ys, cache_win_k[l], cache_win_v[l], state_hgrn[l], lb_all[l], rel_bias_table, w)
        pk.append(k1); pv.append(v1); ps.append(s1)
        sk.append(k2); sv.append(v2); ss.append(s2)
    return (yp, ys, jnp.stack(pk), jnp.stack(pv), jnp.stack(ps), jnp.stack(sk), jnp.stack(sv), jnp.stack(ss))
```

```python
import bisect
import contextlib
import numpy as np
import concourse.bass as bass
import concourse.mybir as mybir
from concourse.bass_utils import run_bass_kernel_spmd

F32 = mybir.dt.float32
BF16 = mybir.dt.bfloat16
AF = mybir.ActivationFunctionType
ALU = mybir.AluOpType
ENGS = ("pe", "act", "dve", "pool", "sp")
import os as _os
SAME_ENGINE_NOSYNC = ("pe", "sp") if _os.environ.get("SAMESYNC") else ("pe", "sp", "act", "dve")

D = 1024
DFF = 2816
NFF = 22
EPS = 1e-6
NEG = -30000.0
C_QA, C_KA, C_VA, C_QR, C_FR, C_IR, C_GR, C_GA, C_GTR = 0, 512, 640, 768, 1280, 1792, 2304, 2816, 3840


class _Node:
    __slots__ = ("ch", "w", "r")

    def __init__(self):
        self.ch = {}
        self.w = None
        self.r = {}


class Prog:
    def __init__(self, nc):
        self.nc = nc
        self.ops = []
        self.root = _Node()

    def _walk(self, key):
        node = self.root
        path = [node]
        for k in key:
            nxt = node.ch.get(k)
            if nxt is None:
                nxt = _Node()
                node.ch[k] = nxt
            node = nxt
            path.append(node)
        return path, node

    def _subtree(self, node, out):
        for c in node.ch.values():
            out.append(c)
            self._subtree(c, out)

    def _deps_for(self, idx, reads, writes, rkey):
        deps = set()
        for key in reads:
            path, node = self._walk(key)
            rel = list(path)
            self._subtree(node, rel)
            for n in rel:
                if n.w is not None:
                    deps.add(n.w)
            node.r[rkey] = idx
        for key in writes:
            path, node = self._walk(key)
            rel = list(path)
            self._subtree(node, rel)
            for n in rel:
                if n.w is not None:
                    deps.add(n.w)
                deps.update(n.r.values())
            sub = []
            self._subtree(node, sub)
            for n in sub:
                n.w = None
                n.r = {}
            node.w = idx
            node.r = {}
        deps.discard(idx)
        return deps

    def add(self, eng, fn, reads=(), writes=(), dma=None):
        idx = len(self.ops)
        reads = [tuple(k) if isinstance(k, (tuple, list)) else (k,) for k in reads]
        writes = [tuple(k) if isinstance(k, (tuple, list)) else (k,) for k in writes]
        rkey = ("dma", dma, idx) if dma is not None else eng
        deps = self._deps_for(idx, reads, writes, rkey)
        best = {}
        red = set()
        for d in deps:
            p = self.ops[d]
            if p["dma"] is not None:
                red.add(d)
            elif d > best.get(p["eng"], -1):
                best[p["eng"]] = d
        red.update(best.values())
        self.ops.append(dict(eng=eng, fn=fn, deps=red, dma=dma, idx=idx))
        return idx

    def emit(self):
        nc = self.nc
        ops = self.ops

        def stream(o):
            return ("dma", o["dma"]) if o["dma"] is not None else ("eng", o["eng"])

        def skip(p, o):
            if p["dma"] is not None:
                return False
            if p["eng"] == o["eng"]:
                if o["dma"] is None and p["eng"] in SAME_ENGINE_NOSYNC:
                    return True
                if o["dma"] is not None and p["eng"] == "sp":
                    return True
            return False

        need_inc = [False] * len(ops)
        for o in ops:
            for d in o["deps"]:
                p = ops[d]
                if p["dma"] is None and not skip(p, o):
                    need_inc[d] = True
        cnt = {}
        val = [0] * len(ops)
        for o in ops:
            s = stream(o)
            if o["dma"] is not None:
                cnt[s] = cnt.get(s, 0) + 16
                val[o["idx"]] = cnt[s]
            elif need_inc[o["idx"]]:
                cnt[s] = cnt.get(s, 0) + 1
                val[o["idx"]] = cnt[s]
        dma_prefix = {}
        for o in ops:
            if o["dma"] is not None:
                dma_prefix.setdefault(o["dma"], []).append((o["idx"], val[o["idx"]]))
        dma_idx_lists = {g: [a for a, _ in l] for g, l in dma_prefix.items()}

        def dma_target(group, consumer_idx):
            k = bisect.bisect_left(dma_idx_lists[group], consumer_idx)
            return dma_prefix[group][k - 1][1]

        streams = sorted(set(stream(o) for o in ops))
        per_eng = {e: [] for e in ENGS}
        for o in ops:
            per_eng[o["eng"]].append(o)
        waited = {e: {} for e in ENGS}
        for o in ops:
            need = {}
            for d in o["deps"]:
                p = ops[d]
                if skip(p, o):
                    continue
                s = stream(p)
                v = val[d] if p["dma"] is None else dma_target(p["dma"], o["idx"])
                if v > need.get(s, 0):
                    need[s] = v
            w = []
            wd = waited[o["eng"]]
            for s, v in need.items():
                if wd.get(s, 0) >= v:
                    continue
                wd[s] = v
                w.append((s, v))
            o["waits"] = w
            o["inc"] = (stream(o), 16 if o["dma"] is not None else 1) if (
                o["dma"] is not None or need_inc[o["idx"]]) else None
        final = dict(cnt)
        with contextlib.ExitStack() as es:
            sems = {}
            for s in streams:
                sems[s] = es.enter_context(nc.semaphore("s_%s_%s" % s))
            block = es.enter_context(nc.Block())
            handles = {"pe": "tensor", "act": "scalar", "dve": "vector",
                       "pool": "gpsimd", "sp": "sync"}

            def make(engname):
                my_ops = per_eng[engname]

                def body(eng):
                    for o in my_ops:
                        for (s, v) in o["waits"]:
                            eng.wait_ge(sems[s], v)
                        ins = o["fn"](eng)
                        if o["inc"] is not None:
                            ins.then_inc(sems[o["inc"][0]], o["inc"][1])
                    if engname == "sp":
                        for s, c in final.items():
                            eng.wait_ge(sems[s], c)
                return body

            for engname in ENGS:
                getattr(block, handles[engname])(make(engname))
        return len(ops)


def build_program(NTP=32, TB=4):
    nc = bass.Bass("TRN2", target_bir_lowering=False)

    def din(name, shape):
        return nc.dram_tensor(name, shape, F32, kind="ExternalInput").ap()

    def dout(name, shape):
        return nc.dram_tensor(name, shape, F32, kind="ExternalOutput").ap()

    xp = din("xp", [NTP * 128, D]); xpre = din("xpre", [NTP * 128, D]); xs = din("xs", [128, D]); pmkd = din("pmask", [128, 1])
    ck = din("ck", [16, 128, 128]); cv = din("cv", [16, 128, 128]); s0 = din("s0", [16, 4, 128, 128])
    w1a = din("w1a", [D, DFF]); w3a = din("w3a", [D, DFF]); w2a = din("w2a", [DFF, D])
    w1b = din("w1b", [D, DFF]); w3b = din("w3b", [D, DFF]); w2b = din("w2b", [DFF, D])
    win = din("win", [D, 4864]); wupa = din("wupa", [512, D]); wupr = din("wupr", [512, D]); wout = din("wout", [D, D])
    g1d = din("g1", [128, 8]); gmd = din("gm", [128, 8]); g2d = din("g2", [128, 8])
    qnd = din("qn", [128, 1]); knd = din("kn", [128, 1]); bdd = din("bd", [128, 128]); hgnd = din("hgn", [128, 1])
    sinkd = din("sinks", [128, 8]); lbad = din("lba", [128, 4]); lbbd = din("lbb", [128, 4]); tabd = din("tab", [32, 8])
    identd = din("ident", [128, 128]); ohd = din("oh", [33, 383]); mchd = din("mchunk", [128, 128])
    msd = din("msamp", [128, 128]); bmd = din("bmask", [128, 128]); rstpd = din("rstp", [128, 128])
    rstsd = din("rsts", [128, 128]); selmd = din("selm", [128, 16]); pmd = din("pm", [128, 2])
    yp = dout("yp", [NTP * 128, D]); ys = dout("ys", [128, D])
    pk = dout("pk", [128, 128]); pv = dout("pv", [128, 128]); pS = dout("pS", [4, 128, 128])
    sk = dout("sk", [16, 128, 128]); sv = dout("sv", [16, 128, 128]); sS = dout("sS", [16, 4, 128, 128])
    scr = nc.dram_tensor("scr", [8, 128, 383], F32, kind="Internal")
    wsrc = dict(w1a=w1a, w3a=w3a, w2a=w2a, win=win, wupa=wupa, wupr=wupr, wout=wout, w1b=w1b, w3b=w3b, w2b=w2b)
    wbf = {k: nc.dram_tensor(k + "_bf", [128, (v.shape[0] // 128) * v.shape[1]], BF16, kind="Internal") for k, v in wsrc.items()}

    TM = TB * 128
    NSLOT = 6
    with contextlib.ExitStack() as es:
        def sb(name, shape, dt=F32):
            return es.enter_context(nc.sbuf_tensor("sb_" + name, shape, dt))

        def pst(name, shape, dt=F32):
            return es.enter_context(nc.psum_tensor(name, shape, dt))

        hT = sb("hT", [128, 8, TM]); xnT = sb("xnT", [128, 8, TM], BF16); actT = sb("actT", [128, NFF, TM], BF16)
        wsl = [sb("wsl%d" % i, [128, 4096], BF16) for i in range(NSLOT)]
        xst = [sb("xst%d" % i, [128, D]) for i in range(2)]
        rstd = sb("rstd", [128, TM])
        def m1v(nn):
            return actT[:, 8 + 2 * nn:10 + 2 * nn, :].rearrange("p c t -> p (c t)").bitcast(F32)

        def m1k(nn):
            return [("actT", 8 + 2 * nn), ("actT", 9 + 2 * nn)]
        fs = [sb("fs%d" % i, [128, TM]) for i in range(5)]
        fo = sb("fo", [128, TM]); sqb = sb("sqb", [128, TM], BF16)
        qT = sb("qT", [128, 4, TM], BF16); kT = sb("kT", [128, 2, TM], BF16)
        khalo = sb("khalo", [128, 2, 128], BF16); vhalo = sb("vhalo", [128, 2, 128], BF16)
        vdup = sb("vdup", [128, TB, 2, 128], BF16)
        attT = sb("attT", [128, 4, TM], BF16); orT = sb("orT", [128, 4, TM], BF16)
        qt = sb("qt", [128, TM], BF16); kt = sb("kt", [128, TM], BF16); kht = sb("kht", [128, TM], BF16)
        khtok = [sb("khtok%d" % i, [128, 128], BF16) for i in range(2)]
        AT = sb("AT", [128, 128], BF16)
        vr = sb("vr", [128, TB, 512], BF16)
        pT = [sb("pT%d" % i, [128, 512], BF16) for i in range(2)]
        sbf = [sb("sbf%d" % i, [128, 512]) for i in range(2)]
        rd = sb("rd", [128, 512])
        biasO = sb("biasO", [128, 8, 128]); biasP = sb("biasP", [128, 8, 128])
        biasS = sb("biasS", [128, 8, 128]); biasC = sb("biasC", [128, 8, 128])
        S = sb("S", [128, 4, 128]); Sb = sb("Sb", [128, 4, 128], BF16)
        kcT = sb("kcT", [128, 2, 16, 128], BF16); vcdup = sb("vcdup", [128, 16, 2, 128], BF16)
        S0 = sb("S0", [128, 16, 128]); cst = S0; S0b = sb("S0b", [128, 16, 128], BF16); Vm = sb("Vm", [128, 16, 128], BF16)
        kout = sb("kout", [128, 128]); vout = sb("vout", [128, 128]); bdf = sb("bdf", [128, 128]); bdb = sb("bdb", [128, 128], BF16)
        ident = sb("ident", [128, 128]); identb = sb("identb", [128, 128], BF16)
        onesb = sb("onesb", [128, 128], BF16); onesf = sb("onesf", [128, 128])
        g1 = sb("g1s", [128, 8]); gm = sb("gms", [128, 8]); g2 = sb("g2s", [128, 8])
        qn = sb("qns", [128, 1]); kn = sb("kns", [128, 1]); hgn = sb("hgns", [128, 1])
        esink = sb("esink", [128, 8]); lb = sb("lb", [128, 4]); oml = sb("oml", [128, 4]); lbt = sb("lbt", [128, 4])
        tabs = sb("tabs", [33, 8]); ohs = rstd[0:33, 0:383]; tabrep = sb("tabrep", [33, 128]); cb = fo
        mch = sb("mch", [128, 128]); msm = sb("msm", [128, 128]); bmk = sb("bmk", [128, 128])
        pmk = sb("pmk_s", [128, 1]); rstp = sb("rstp_s", [128, 128]); rsts = sb("rsts_s", [128, 128]); selm = sb("selm_s", [128, 16]); pm = sb("pm_s", [128, 2])
        ps = [pst("ps%d" % i, [128, 512]) for i in range(7)]
        psb = pst("psb", [128, 1024], BF16)

        P = Prog(nc)

        def A(eng, reads, writes, fn, dma=None):
            writes = list(writes)
            for k in reads:
                k0 = k[0] if isinstance(k, (tuple, list)) else k
                if k0 in ("ps", "psb"):
                    writes.append(k)
            P.add(eng, fn, reads, writes, dma)

        small = [(ident, identd, "ident"),  (mch, mchd, "mch"), (msm, msd, "msm"), (bmk, bmd, "bmk"),
                 (rstp, rstpd, "rstp"), (rsts, rstsd, "rsts"), (selm, selmd, "selm"), (pm, pmd, "pm"),
                 (g1, g1d, "g1"), (gm, gmd, "gm"), (g2, g2d, "g2"), (bdf, bdd, "bdf"), (qn, qnd, "qn"), (kn, knd, "kn"), (hgn, hgnd, "hgn"),
                 (esink, sinkd, "esink"), (pmk, pmkd, "pmk"), (lb, lbad, "lb"), (lbt, lbbd, "lbt")]
        for (t, d, key) in small:
            A("sp", [], [key], lambda e, t=t, d=d: e.dma_start(out=t[:], in_=d), dma="c")
        A("sp", [], ["rstd"], lambda e: e.dma_start(out=ohs, in_=ohd), dma="c")
        A("dve", [], ["tabs"], lambda e: e.memset(tabs[:], NEG))
        A("sp", [], ["tabs"], lambda e: e.dma_start(out=tabs[0:32, :], in_=tabd), dma="c")
        A("dve", [], ["onesb"], lambda e: e.memset(onesb[:], 1.0))
        A("dve", [], ["onesf"], lambda e: e.memset(onesf[:], 1.0))
        A("dve", [], ["S"], lambda e: e.memset(S[:], 0.0))
        A("dve", [], ["Sb"], lambda e: e.memset(Sb[:], 0.0))
        A("act", ["bdf"], ["bdb"], lambda e: e.activation(out=bdb[:], in_=bdf[:], func=AF.Copy))
        A("dve", [], ["Vm"], lambda e: e.memset(Vm[:], 0.0))
        A("act", ["ident"], ["identb"], lambda e: e.activation(out=identb[:], in_=ident[:], func=AF.Copy))
        A("dve", ["lb", "lbt"], ["lbt"], lambda e: e.tensor_tensor(out=lbt[:], in0=lb[:], in1=lbt[:], op=ALU.subtract))
        A("act", ["lbt"], ["lb"], lambda e: e.activation(out=lb[:], in_=lbt[:], func=AF.Sigmoid))
        A("dve", ["lb"], ["oml"], lambda e: e.tensor_scalar(out=oml[:], in0=lb[:], scalar1=-1.0, scalar2=1.0, op0=ALU.mult, op1=ALU.add))
        A("act", ["esink"], ["esink"], lambda e: e.activation(out=esink[:], in_=esink[:], func=AF.Exp))
        ffn_g = [(c0, min(512, DFF - c0)) for c0 in range(0, DFF, 512)]
        win_g = ([(C_QA, 256), (C_QA + 256, 256), (C_KA, 128), (C_VA, 128)] + [(C_QR + h * 128, 128) for h in range(4)]
                 + [(C_FR + h * 128, 128) for h in range(4)] + [(C_IR, 512)] + [(C_GR + h * 128, 128) for h in range(4)]
                 + [(C_GA, 512), (C_GA + 512, 512), (C_GTR, 512), (C_GTR + 512, 512)])
        GT = dict(w1a=ffn_g, w3a=ffn_g, w1b=ffn_g, w3b=ffn_g, w2a=[(c0, 256) for c0 in range(0, D, 256)], w2b=[(c0, 256) for c0 in range(0, D, 256)],
                  win=win_g, wupa=[(0, 512), (512, 512)], wupr=[(0, 512), (512, 512)], wout=[(0, 512), (512, 512)])
        KCW = {k: v.shape[0] // 128 for k, v in wsrc.items()}

        def conv_jobs(name):
            return [(name, gi) for gi in range(len(GT[name]))]

        def emit_conv(job):
            name, gi = job
            c0, w = GT[name][gi]
            kcw = KCW[name]
            srcf = wsrc[name][:, c0:c0 + w].rearrange("(k p) n -> p k n", p=128)
            dstf = wbf[name].ap()[:, kcw * c0:kcw * (c0 + w)].rearrange("p (k n) -> p k n", k=kcw)
            A("pool", [], [("wb", name, gi)], lambda e, srcf=srcf, dstf=dstf: e.dma_start(out=dstf, in_=srcf), dma="cv_%s_%d" % (name, gi))

        j1, j3 = conv_jobs("w1a"), conv_jobs("w3a")
        emit_conv(j1[0])
        emit_conv(j3[0])
        pending_conv = []
        for a_, b_ in zip(j1[1:], j3[1:]):
            pending_conv += [a_, b_]
        pending_conv += conv_jobs("w2a")
        for nm in ("win", "wupa", "wupr", "wout"):
            pending_conv += conv_jobs(nm)
        jb1, jb3 = conv_jobs("w1b"), conv_jobs("w3b")
        for a_, b_ in zip(jb1, jb3):
            pending_conv += [a_, b_]
        pending_conv += conv_jobs("w2b")
        for h in range(8):
            A("dve", ["onesf", "tabs"], ["tabrep"], lambda e, h=h: e.tensor_scalar(out=tabrep[:], in0=onesf[0:33, :], scalar1=tabs[:, h:h + 1], scalar2=None, op0=ALU.mult))
            pbk = h % 2
            cbh, cbk = (fo, "fo") if h % 2 == 0 else (fs[4], "fs4")
            A("pe", ["tabrep", "rstd"], [("ps", pbk)], lambda e, pbk=pbk: e.matmul(ps[pbk][:, 0:383], lhsT=tabrep[:], rhs=ohs, start=True, stop=True))
            A("act", [("ps", pbk)], [cbk], lambda e, pbk=pbk, cbh=cbh: e.activation(out=cbh[:, 0:383], in_=ps[pbk][:, 0:383], func=AF.Copy))
            A("act", [cbk], [("scr", h)], lambda e, h=h, cbh=cbh: e.dma_start(out=scr.ap()[h], in_=cbh[:, 0:383]), dma="b1%d" % (h % 2))
        for h in range(8):
            so = bass.AP(tensor=scr, offset=h * 128 * 383 + 127, ap=[[382, 128], [1, 128]])
            sp_ = bass.AP(tensor=scr, offset=h * 128 * 383 + 255, ap=[[382, 128], [1, 128]])
            A("act", [("scr", h)], [("biasO", h)], lambda e, h=h, so=so: e.dma_start(out=biasO[:, h, :], in_=so), dma="b2")
            A("act", [("scr", h)], [("biasP", h)], lambda e, h=h, sp_=sp_: e.dma_start(out=biasP[:, h, :], in_=sp_), dma="b2")
        A("dve", ["biasO", "bmk"], ["biasS"], lambda e: e.tensor_tensor(out=biasS[:], in0=biasO[:], in1=bmk[:].unsqueeze(1).to_broadcast([128, 8, 128]), op=ALU.add))
        A("dve", ["biasP"], ["biasC"], lambda e: e.tensor_copy(out=biasC[:].rearrange("p h (b q) -> p h b q", b=16), in_=biasP[:, :, 0:8].unsqueeze(2).to_broadcast([128, 8, 16, 8])))

        wctr = [0]

        def wload(wname, kc, c0, n, nsl=None, k0=0):
            gi = GT[wname].index((c0, n))
            need = [k for k, jb in enumerate(pending_conv) if jb == (wname, gi)]
            if need:
                for jb in pending_conv[:need[-1] + 1]:
                    emit_conv(jb)
                del pending_conv[:need[-1] + 1]
            s = wctr[0] % (nsl or NSLOT)
            wctr[0] += 1
            view = wsl[s][:, 0:kc * n].rearrange("p (k n) -> p k n", k=kc)
            base = KCW[wname] * c0 + k0 * n
            src = wbf[wname].ap()[:, base:base + kc * n]
            A("pool", [("wb", wname, gi)], [("w", s)], lambda e: e.dma_start(out=wsl[s][:, 0:kc * n], in_=src), dma="w%d" % s)
            if pending_conv:
                emit_conv(pending_conv.pop(0))
            return s, view

        def rmsnorm(T, gain, gkey, stats_done=False):
            for c in ([] if stats_done else range(8)):
                A("act", [("hT", c)], [("xnT", c)], lambda e, c=c: e.activation(out=xnT[:, c, 0:T], in_=hT[:, c, 0:T], func=AF.Square))
            for c in ([] if stats_done else range(8)):
                A("pe", [("xnT", c), "onesb"], [("ps", 0)], lambda e, c=c: e.matmul(ps[0][:, 0:T], lhsT=onesb[:], rhs=xnT[:, c, 0:T], start=(c == 0), stop=(c == 7)))
            A("act", [("ps", 0)], ["rstd"], lambda e: e.activation(out=rstd[:, 0:T], in_=ps[0][:, 0:T], func=AF.Ln, bias=EPS, scale=1.0 / D))
            A("act", ["rstd"], ["rstd"], lambda e: e.activation(out=rstd[:, 0:T], in_=rstd[:, 0:T], func=AF.Exp, scale=-0.5))
            for c in range(8):
                A("dve", [("hT", c), "rstd", gkey], [("xnT", c)], lambda e, c=c: e.scalar_tensor_tensor(out=xnT[:, c, 0:T], in0=hT[:, c, 0:T], scalar=gain[:, c:c + 1], in1=rstd[:, 0:T], op0=ALU.mult, op1=ALU.mult))

        def proj_fm(T, s, view, c0, m, psi, kc=8):
            for c in range(kc):
                A("pe", [("w", s), ("xnT", c)], [("ps", psi)], lambda e, c=c: e.matmul(ps[psi][0:m, 0:T], lhsT=view[:, c, c0:c0 + m], rhs=xnT[:, c, 0:T], start=(c == 0), stop=(c == kc - 1)))

        def ffn(T, w1, w3, w2):
            for j0 in range(0, NFF, 4):
                gw = min(4, NFF - j0)
                s1, v1 = wload(w1, 8, j0 * 128, gw * 128)
                s3, v3 = wload(w3, 8, j0 * 128, gw * 128)
                for jj in range(gw):
                    j = j0 + jj
                    proj_fm(T, s1, v1, jj * 128, 128, 0)
                    proj_fm(T, s3, v3, jj * 128, 128, 1)
                    f = fs[j % 2]
                    fk = "fs%d" % (j % 2)
                    A("act", [("ps", 0)], [fk], lambda e, f=f: e.activation(out=f[:, 0:T], in_=ps[0][:, 0:T], func=AF.Silu))
                    A("dve", [("ps", 1), fk], [("actT", j)], lambda e, f=f, j=j: e.tensor_tensor(out=actT[:, j, 0:T], in0=ps[1][:, 0:T], in1=f[:, 0:T], op=ALU.mult))
            HK = NFF // 2
            for n2 in range(4):
                s2a, v2a = wload(w2, HK, n2 * 256, 256, k0=0)
                s2b, v2b = wload(w2, HK, n2 * 256, 256, k0=HK)
                for nn in range(2):
                    n = n2 * 2 + nn
                    pb = n % 2
                    for j in range(NFF):
                        s2, v2, jj = (s2a, v2a, j) if j < HK else (s2b, v2b, j - HK)
                        A("pe", [("w", s2), ("actT", j)], [("ps", pb)], lambda e, j=j, jj=jj, v2=v2, pb=pb, nn=nn: e.matmul(ps[pb][:, 0:T], lhsT=v2[:, jj, nn * 128:(nn + 1) * 128], rhs=actT[:, j, 0:T], start=(j == 0), stop=(j == NFF - 1)))
                    A("dve", [("ps", pb), ("hT", n)], [("hT", n)], lambda e, n=n, pb=pb: e.scalar_tensor_tensor(out=hT[:, n, 0:T], in0=ps[pb][:, 0:T], scalar=0.5, in1=hT[:, n, 0:T], op0=ALU.mult, op1=ALU.add))

        def qknorm(T, psi, gain, gkey, dst, dkey):
            A("act", [("ps", psi)], ["sqb"], lambda e: e.activation(out=sqb[:, 0:T], in_=ps[psi][:, 0:T], func=AF.Square))
            A("pe", ["sqb", "bdb"], [("ps", 6)], lambda e: e.matmul(ps[6][:, 0:T], lhsT=bdb[:], rhs=sqb[:, 0:T], start=True, stop=True))
            A("act", [("ps", 6)], ["fs4"], lambda e: e.activation(out=fs[4][:, 0:T], in_=ps[6][:, 0:T], func=AF.Ln, bias=EPS, scale=1.0 / 64))
            A("act", ["fs4"], ["fs4"], lambda e: e.activation(out=fs[4][:, 0:T], in_=fs[4][:, 0:T], func=AF.Exp, scale=-0.5))
            A("dve", [("ps", psi), "fs4", gkey], [dkey], lambda e: e.scalar_tensor_tensor(out=dst, in0=ps[psi][:, 0:T], scalar=gain[:, 0:1], in1=fs[4][:, 0:T], op0=ALU.mult, op1=ALU.mult))

        import os
        STAGE = float(os.environ.get('STAGE', '9'))
        SST = float(os.environ.get('SSTAGE', '9'))
        xctr = [0]

        def block(tiles, first_tile_of_seq, out_rows, xsrc, ydst, is_sample, last_prompt, prefix=False, prefix_last=False, halo_mask=False, after_x=None):
            nt = tiles
            T = nt * 128
            for i in range(nt):
                sl = xctr[0] % 2
                xctr[0] += 1
                A("sp", [], [("xst", sl)], lambda e, i=i, sl=sl: e.dma_start(out=xst[sl][:], in_=xsrc[i * 128:(i + 1) * 128, :]), dma="x%d" % sl)
                for c in range(8):
                    A("pe", [("xst", sl), "ident"], [("ps", 2 + c // 4)], lambda e, c=c, sl=sl: e.transpose(out=ps[2 + c // 4][:, (c % 4) * 128:(c % 4 + 1) * 128], in_=xst[sl][:, c * 128:(c + 1) * 128], identity=ident[:]))
                A("act", [("ps", 2)], [("hT", c_, i) for c_ in range(4)], lambda e, i=i: e.activation(out=hT[:, 0:4, i * 128:(i + 1) * 128], in_=ps[2][:, :].rearrange("p (c t) -> p c t", c=4), func=AF.Copy))
                A("dve", [("ps", 3)], [("hT", c_, i) for c_ in range(4, 8)], lambda e, i=i: e.tensor_copy(out=hT[:, 4:8, i * 128:(i + 1) * 128], in_=ps[3][:, :].rearrange("p (c t) -> p c t", c=4)))
                A("act", [("hT", c_, i) for c_ in range(8)], [("xnT", c_, i) for c_ in range(8)], lambda e, i=i: e.activation(out=xnT[:, :, i * 128:(i + 1) * 128], in_=hT[:, :, i * 128:(i + 1) * 128], func=AF.Square))
                for c in range(8):
                    A("pe", [("xnT", c, i), "onesb"], [("ps", 0)], lambda e, c=c, i=i: e.matmul(ps[0][:, i * 128:(i + 1) * 128], lhsT=onesb[:], rhs=xnT[:, c, i * 128:(i + 1) * 128], start=(c == 0), stop=(c == 7)))
            if after_x is not None:
                after_x()
            rmsnorm(T, g1, "g1", stats_done=True)
            ffn(T, "w1a", "w3a", "w2a")
            if STAGE <= 1:
                return
            rmsnorm(T, gm, "gm")
            if is_sample and SST <= 1.1:
                return
            do_kv = (not prefix) or prefix_last
            if do_kv:
                sv_, vv = wload("win", 8, C_VA, 128)
            for i in (range(nt) if do_kv else []):
                for c in range(8):
                    A("pe", [("w", sv_), ("xnT", c)], [("ps", 2)], lambda e, c=c, i=i: e.matmul(ps[2][:, 0:128], lhsT=xnT[:, c, i * 128:(i + 1) * 128], rhs=vv[:, c, 0:128], start=(c == 0), stop=(c == 7)))
                for dup in range(2):
                    A("act", [("ps", 2)], [("vdup", i)], lambda e, i=i, dup=dup: e.activation(out=vdup[:, i, :, dup * 64:(dup + 1) * 64], in_=ps[2][:, 0:128].rearrange("p (g d) -> p g d", g=2), func=AF.Copy))
                if is_sample and SST <= 1.2:
                    return
                if i == nt - 1 and (is_sample or last_prompt):
                    A("act", [("ps", 2)], ["vout"], lambda e: e.activation(out=vout[:], in_=ps[2][:, 0:128], func=AF.Copy))
                    if is_sample:
                        NB = int(os.environ.get('NB', '16'))
                        for b in range(NB):
                            A("sp", ["vout"], [], lambda e, b=b: e.dma_start(out=sv[b, 120:128, :], in_=vout[b * 8:(b + 1) * 8, :]), dma="o1")
                    else:
                        A("sp", ["vout"], [], lambda e: e.dma_start(out=pv, in_=vout[:]), dma="o1")
            if is_sample and SST <= 1.3:
                return
            want_k = is_sample or last_prompt
            if do_kv:
                sk_, kv = wload("win", 8, C_KA, 128)
            for g in (range(2) if do_kv else []):
                pad = Vm[:, 8 * g:8 * g + 8, :]
                for dup in range(2):
                    A("dve", [("w", sk_)], ["Vm"], lambda e, g=g, dup=dup, pad=pad: e.tensor_copy(out=pad[:, :, dup * 64:(dup + 1) * 64], in_=kv[:, :, g * 64:(g + 1) * 64]))
                for c in range(8):
                    A("pe", ["Vm", ("xnT", c)], [("ps", g)], lambda e, c=c, g=g, pad=pad: e.matmul(ps[g][:, 0:T], lhsT=pad[:, c, :], rhs=xnT[:, c, 0:T], start=(c == 0), stop=(c == 7)))
                qknorm(T, g, kn, "kn", kT[:, g, 0:T], ("kT", g))
                if want_k:
                    t0 = (nt - 1) * 128
                    A("dve", [("ps", g), "fs4", "kn"], [("fs3", g)], lambda e, g=g, t0=t0: e.scalar_tensor_tensor(out=fs[3][:, g * 128:(g + 1) * 128], in0=ps[g][:, t0:t0 + 128], scalar=kn[:, 0:1], in1=fs[4][:, t0:t0 + 128], op0=ALU.mult, op1=ALU.mult))
                    A("pe", [("fs3", g), "ident"], [("ps", 2)], lambda e, g=g: e.transpose(out=ps[2][:, g * 128:(g + 1) * 128], in_=fs[3][:, g * 128:(g + 1) * 128], identity=ident[:]))
                    A("act", [("ps", 2)], [("kout", g)], lambda e, g=g: e.activation(out=kout[:, g * 64:(g + 1) * 64], in_=ps[2][:, g * 128:g * 128 + 64], func=AF.Copy))
            if is_sample and SST <= 1.4:
                return
            if want_k:
                if is_sample:
                    for b in range(16):
                        A("sp", ["kout"], [], lambda e, b=b: e.dma_start(out=sk[b, 120:128, :], in_=kout[b * 8:(b + 1) * 8, :]), dma="o1")
                else:
                    A("sp", ["kout"], [], lambda e: e.dma_start(out=pk, in_=kout[:]), dma="o1")
            if STAGE <= 1.5:
                return
            QG = [qT, actT[:, 8:12, :]]
            QK = [["qT"], [("actT", 8), ("actT", 9), ("actT", 10), ("actT", 11)]]
            QK1 = [lambda hl: ("qT", hl), lambda hl: ("actT", 8 + hl)]
            for g in (range(2) if not prefix else []):
                sq_, qv = wload("win", 8, C_QA + g * 256, 256)
                for c2 in range(2):
                    for c in range(8):
                        A("pe", [("w", sq_), ("xnT", c)], [("ps", c2)], lambda e, c=c, c2=c2, qv=qv: e.matmul(ps[c2][:, 0:T], lhsT=qv[:, c, c2 * 128:(c2 + 1) * 128], rhs=xnT[:, c, 0:T], start=(c == 0), stop=(c == 7)))
                    qknorm(T, c2, qn, "qn", qt[:, 0:T], "qt")
                    for par in range(2):
                        hl = 2 * c2 + par
                        A("dve", ["qt", "pm"], [QK1[g](hl)], lambda e, hl=hl, par=par, g=g: e.tensor_scalar(out=QG[g][:, hl, 0:T], in0=qt[:, 0:T], scalar1=pm[:, par:par + 1], scalar2=None, op0=ALU.mult))
            for _once in ([0] if not prefix else []):
                b_own = biasS if is_sample else biasO
                b_prev = biasC if is_sample else biasP

                def att_set(sid):
                    if sid == 0:
                        return dict(sc=(2, 3), po=4, pd=ps[5], pdk=("ps", 5), P=[pT[0][:, :], pT[1][:, :]], Pk=[["pT0"], ["pT1"]],
                                    SB=[sbf[0][:, :], sbf[1][:, :]], SBk=[["sbf0"], ["sbf1"]], RD=rd[:, :], RDk=["rd"])
                    f32v = lambda c: actT[:, c:c + 2, :].rearrange("p c t -> p (c t)").bitcast(F32)
                    return dict(sc=(0, 1), po=6, pd=psb[:, :].bitcast(F32), pdk="psb", P=[actT[:, 0, :], actT[:, 1, :]], Pk=[[("actT", 0)], [("actT", 1)]],
                                SB=[f32v(2), f32v(4)], SBk=[[("actT", 2), ("actT", 3)], [("actT", 4), ("actT", 5)]], RD=f32v(6), RDk=[("actT", 6), ("actT", 7)])

                def att_scores(i, W, g):
                    tsl = slice(i * 128, (i + 1) * 128)
                    has_prev = is_sample or not (first_tile_of_seq and i == 0)
                    so, sp2 = W["sc"]
                    A("pe", [("kT", g)] + QK[g], [("ps", so)], lambda e: e.matmul(ps[so][:, :], lhsT=kT[:, g, tsl], rhs=QG[g][:, :, tsl], start=True, stop=True))
                    A("dve", [("ps", so), "biasO", "biasS"], W["SBk"][0], lambda e: e.scalar_tensor_tensor(out=W["SB"][0], in0=ps[so][:, :], scalar=0.125, in1=b_own[:, 4 * g:4 * g + 4, :].rearrange("p h q -> p (h q)"), op0=ALU.mult, op1=ALU.add))
                    A("act", W["SBk"][0], W["Pk"][0], lambda e: e.activation(out=W["P"][0], in_=W["SB"][0], func=AF.Exp))
                    if has_prev:
                        if is_sample:
                            for b in range(16):
                                A("pe", ["kcT"] + QK[g], [("ps", sp2)], lambda e, b=b: e.matmul(ps[sp2][:, :].rearrange("p (h b q) -> p h b q", h=4, b=16)[:, :, b, :], lhsT=kcT[:, g, b, :], rhs=QG[g][:, :, i * 128 + b * 8:i * 128 + b * 8 + 8], start=True, stop=True))
                        elif i > 0:
                            psl = slice((i - 1) * 128, i * 128)
                            A("pe", [("kT", g)] + QK[g], [("ps", sp2)], lambda e: e.matmul(ps[sp2][:, :], lhsT=kT[:, g, psl], rhs=QG[g][:, :, tsl], start=True, stop=True))
                        else:
                            A("pe", [("khalo", g)] + QK[g], [("ps", sp2)], lambda e: e.matmul(ps[sp2][:, :], lhsT=khalo[:, g, :], rhs=QG[g][:, :, tsl], start=True, stop=True))
                        A("dve", [("ps", sp2), "biasP", "biasC"], W["SBk"][1], lambda e: e.scalar_tensor_tensor(out=W["SB"][1], in0=ps[sp2][:, :], scalar=0.125, in1=b_prev[:, 4 * g:4 * g + 4, :].rearrange("p h q -> p (h q)"), op0=ALU.mult, op1=ALU.add))
                        if halo_mask and i == 0:
                            A("dve", W["SBk"][1] + ["pmk"], W["SBk"][1], lambda e: e.tensor_scalar(out=W["SB"][1], in0=W["SB"][1], scalar1=pmk[:, 0:1], scalar2=None, op0=ALU.add))
                        A("act", W["SBk"][1], W["Pk"][1], lambda e: e.activation(out=W["P"][1], in_=W["SB"][1], func=AF.Exp))

                def att_pv(i, W, g):
                    tsl = slice(i * 128, (i + 1) * 128)
                    has_prev = is_sample or not (first_tile_of_seq and i == 0)
                    po, pd, pdk = W["po"], W["pd"], W["pdk"]
                    P0, P1 = W["P"]
                    A("pe", [("vdup", i)] + W["Pk"][0], [("ps", po)], lambda e: e.matmul(ps[po][:, :], lhsT=vdup[:, i, g, :], rhs=P0, start=True, stop=not has_prev))
                    A("pe", ["onesb"] + W["Pk"][0], [pdk], lambda e: e.matmul(pd[:, :], lhsT=onesb[:], rhs=P0, start=True, stop=not has_prev))
                    if has_prev:
                        if is_sample:
                            for b in range(16):
                                A("pe", ["vcdup"] + W["Pk"][1], [("ps", po)], lambda e, b=b: e.matmul(ps[po][:, :].rearrange("p (h b q) -> p h b q", h=4, b=16)[:, :, b, :], lhsT=vcdup[:, b, g, :], rhs=P1.rearrange("p (h b q) -> p h b q", h=4, b=16)[:, :, b, :], start=False, stop=(b == 15)))
                        elif i > 0:
                            A("pe", [("vdup", i - 1)] + W["Pk"][1], [("ps", po)], lambda e: e.matmul(ps[po][:, :], lhsT=vdup[:, i - 1, g, :], rhs=P1, start=False, stop=True))
                        else:
                            A("pe", [("vhalo", g)] + W["Pk"][1], [("ps", po)], lambda e: e.matmul(ps[po][:, :], lhsT=vhalo[:, g, :], rhs=P1, start=False, stop=True))
                        A("pe", ["onesb"] + W["Pk"][1], [pdk], lambda e: e.matmul(pd[:, :], lhsT=onesb[:], rhs=P1, start=False, stop=True))
                    RD = W["RD"]
                    for hl in range(4):
                        h = 4 * g + hl
                        A("act", [pdk, "esink"], W["RDk"], lambda e, hl=hl, h=h: e.activation(out=RD[:, hl * 128:(hl + 1) * 128], in_=pd[:, hl * 128:(hl + 1) * 128], func=AF.Ln, bias=esink[:, h:h + 1]))
                    A("act", W["RDk"], W["RDk"], lambda e: e.activation(out=RD, in_=RD, func=AF.Exp, scale=-1.0))
                    for hl in range(4):
                        h = 4 * g + hl
                        hp = (h % 2) * 64
                        A("dve", [("ps", po)] + W["RDk"], [("attT", h // 2, h % 2, i)], lambda e, hl=hl, h=h, hp=hp: e.tensor_tensor(out=attT[hp:hp + 64, h // 2, tsl], in0=ps[po][hp:hp + 64, hl * 128:(hl + 1) * 128], in1=RD[hp:hp + 64, hl * 128:(hl + 1) * 128], op=ALU.mult))

                W2 = [att_set(0), att_set(1)]
                steps = [(g_, i_) for g_ in range(2) for i_ in range(nt)]
                for k in range(len(steps) + 1):
                    if k < len(steps):
                        att_scores(steps[k][1], W2[k % 2], steps[k][0])
                    if k >= 1:
                        att_pv(steps[k - 1][1], W2[(k - 1) % 2], steps[k - 1][0])
            if not is_sample and do_kv:
                A("act", ["kT"], ["khalo"], lambda e: e.activation(out=khalo[:, :, :], in_=kT[:, :, (nt - 1) * 128:nt * 128], func=AF.Copy))
                A("dve", [("vdup", nt - 1)], ["vhalo"], lambda e: e.tensor_copy(out=vhalo[:, :, :], in_=vdup[:, nt - 1, :, :]))
            if STAGE <= 2:
                return
            si_, iv = wload("win", 8, C_IR, 512)
            for i in range(nt):
                for c in range(8):
                    A("pe", [("w", si_), ("xnT", c)], [("ps", 2)], lambda e, c=c, i=i: e.matmul(ps[2][:, :], lhsT=xnT[:, c, i * 128:(i + 1) * 128], rhs=iv[:, c, :], start=(c == 0), stop=(c == 7)))
                A("act", [("ps", 2)], [("vr", i)], lambda e, i=i: e.activation(out=vr[:, i, :], in_=ps[2][:, :], func=AF.Copy))
            rst = rsts if is_sample else rstp
            msk = msm if is_sample else mch

            def hg_set(sid):
                if sid == 0:
                    F = [fs[0], fs[1], fs[2], fs[3]]
                    return dict(F=[f[:, 0:T] for f in F], Fk=[["fs0"], ["fs1"], ["fs2"], ["fs3"]], FO=fo[:, 0:T], FOk=["fo"],
                                QT=qt[:, 0:T], QTk=["qt"], KT=kt[:, 0:T], KTk=["kt"], KHT=kht[:, 0:T], KHTk=["kht"],
                                AT=AT[:, :], ATk=["AT"], KH=[khtok[0][:, :], khtok[1][:, :]], KHk=[[("khtok", 0)], [("khtok", 1)]],
                                po=4, pu=5)
                def f32v(c):
                    return actT[:, c:c + 2, :].rearrange("p c t -> p (c t)").bitcast(F32)[:, 0:T]
                return dict(F=[f32v(8), f32v(10), f32v(12), f32v(14)], Fk=[[("actT", 8), ("actT", 9)], [("actT", 10), ("actT", 11)], [("actT", 12), ("actT", 13)], [("actT", 14), ("actT", 15)]],
                            FO=f32v(16), FOk=[("actT", 16), ("actT", 17)],
                            QT=actT[:, 18, 0:T], QTk=[("actT", 18)], KT=actT[:, 19, 0:T], KTk=[("actT", 19)], KHT=actT[:, 20, 0:T], KHTk=[("actT", 20)],
                            AT=actT[:, 21, 0:128], ATk=[("actT", 21, 0)], KH=[actT[:, 21, 128:256], actT[:, 21, 256:384]], KHk=[[("actT", 21, 1)], [("actT", 21, 2)]],
                            po=6, pu=3)

            def hg_stage_a(hh, B):
                F, Fk = B["F"], B["Fk"]
                sf_, fv = wload("win", 8, C_FR + hh * 128, 128)
                if not prefix:
                    sqr_, qrv = wload("win", 8, C_QR + hh * 128, 128)
                pf = B["po"] if not is_sample else 0
                pq = B["pu"] if not is_sample else 1
                proj_fm(T, sf_, fv, 0, 128, pf)
                A("act", [("ps", pf)], Fk[1], lambda e: e.activation(out=F[1], in_=ps[pf][:, 0:T], func=AF.Sigmoid))
                A("dve", Fk[1] + ["lb", "oml"], Fk[1], lambda e, hh=hh: e.tensor_scalar(out=F[1], in0=F[1], scalar1=oml[:, hh:hh + 1], scalar2=lb[:, hh:hh + 1], op0=ALU.mult, op1=ALU.add))
                yield
                A("act", Fk[1], Fk[2], lambda e: e.activation(out=F[2], in_=F[1], func=AF.Ln))
                A("dve", Fk[1], Fk[1], lambda e: e.tensor_scalar(out=F[1], in0=F[1], scalar1=-1.0, scalar2=1.0, op0=ALU.mult, op1=ALU.add))
                for i in range(nt):
                    A("dve", Fk[2] + ["rstp", "rsts"], Fk[3], lambda e, i=i: e.tensor_tensor_scan(out=F[3][:, i * 128:(i + 1) * 128], data0=rst[:, :], data1=F[2][:, i * 128:(i + 1) * 128], initial=0.0, op0=ALU.mult, op1=ALU.add))
                yield
                A("act", Fk[3], Fk[2], lambda e: e.activation(out=F[2], in_=F[3], func=AF.Exp))
                A("act", Fk[3], Fk[0], lambda e: e.activation(out=F[0], in_=F[3], func=AF.Exp, scale=-1.0))
                yield
                if not prefix:
                    proj_fm(T, sqr_, qrv, 0, 128, pq)
                    A("act", [("ps", pq)], Fk[3], lambda e: e.activation(out=F[3], in_=ps[pq][:, 0:T], func=AF.Silu))
                    A("dve", Fk[3] + Fk[2], B["QTk"], lambda e: e.tensor_tensor(out=B["QT"], in0=F[3], in1=F[2], op=ALU.mult))
                A("dve", Fk[1] + Fk[0], Fk[0], lambda e: e.tensor_tensor(out=F[0], in0=F[1], in1=F[0], op=ALU.mult))
                yield
                if not prefix:
                    A("act", Fk[0], B["KTk"], lambda e: e.activation(out=B["KT"], in_=F[0], func=AF.Copy))
                glen = 8 if is_sample else 64
                for c0 in range(0, T, glen):
                    A("dve", Fk[0] + Fk[2], B["KHTk"], lambda e, c0=c0: e.tensor_scalar(out=B["KHT"][:, c0:c0 + glen], in0=F[0][:, c0:c0 + glen], scalar1=F[2][:, c0 + glen - 1:c0 + glen], scalar2=None, op0=ALU.mult))

            def hg_tile(hh, i, B):
                F, Fk = B["F"], B["Fk"]
                po, pu = B["po"], B["pu"]
                vsl = slice(hh * 128, (hh + 1) * 128)
                tsl = slice(i * 128, (i + 1) * 128)
                A("pe", B["KHTk"] + ["identb"], ["psb"], lambda e: e.transpose(out=psb[:, 0:128], in_=B["KHT"][:, tsl], identity=identb[:]))
                if not prefix:
                    A("pe", B["KTk"] + B["QTk"], [("ps", pu)], lambda e: e.matmul(ps[pu][:, 0:128], lhsT=B["KT"][:, tsl], rhs=B["QT"][:, tsl], start=True, stop=True))
                    A("dve", [("ps", pu), "mch", "msm"], B["ATk"], lambda e: e.tensor_tensor(out=B["AT"], in0=ps[pu][:, 0:128], in1=msk[:, :], op=ALU.mult))
                    A("pe", [("vr", i)] + B["ATk"], [("ps", po)], lambda e: e.matmul(ps[po][:, 0:128], lhsT=vr[:, i, vsl], rhs=B["AT"], start=True, stop=False))
                if not is_sample:
                    for ch in range(2):
                        A("dve", ["psb", "pm"], B["KHk"][ch], lambda e, ch=ch: e.tensor_scalar(out=B["KH"][ch], in0=psb[:, 0:128], scalar1=pm[:, ch:ch + 1], scalar2=None, op0=ALU.mult))
                    for ch in range(2):
                        cs = i * 128 + ch * 64
                        if not prefix:
                            A("pe", [("Sb", hh)] + B["QTk"], [("ps", po)], lambda e, cs=cs, ch=ch: e.matmul(ps[po][:, ch * 64:(ch + 1) * 64], lhsT=Sb[:, hh, :], rhs=B["QT"][:, cs:cs + 64], start=False, stop=(ch == 1)))
                        A("pe", B["KHk"][ch] + [("vr", i)], [("ps", pu)], lambda e, ch=ch: e.matmul(ps[pu][:, 128:256], lhsT=B["KH"][ch], rhs=vr[:, i, vsl], start=True, stop=True))
                        A("dve", [("ps", pu), ("S", hh)] + Fk[2], [("S", hh)], lambda e, cs=cs: e.scalar_tensor_tensor(out=S[:, hh, :], in0=S[:, hh, :], scalar=F[2][:, cs + 63:cs + 64], in1=ps[pu][:, 128:256], op0=ALU.mult, op1=ALU.add))
                        if (not prefix) or (prefix_last and i == nt - 1 and ch == 1):
                            A("act", [("S", hh)], [("Sb", hh)], lambda e: e.activation(out=Sb[:, hh, :], in_=S[:, hh, :], func=AF.Copy))
                else:
                    A("act", ["psb"], B["KHk"][0], lambda e: e.activation(out=B["KH"][0], in_=psb[:, 0:128], func=AF.Copy))
                    for b in range(16):
                        A("pe", B["S0bk"] + B["QTk"], [("ps", po)], lambda e, b=b: e.matmul(ps[po][:, b * 8:(b + 1) * 8], lhsT=B["S0b"][:, b, :], rhs=B["QT"][:, i * 128 + b * 8:i * 128 + b * 8 + 8], start=False, stop=(b == 15)))
                    for b in range(16):
                        A("dve", [("vr", i), "selm"], [("Vm", b)], lambda e, b=b: e.tensor_scalar(out=Vm[:, b, :], in0=vr[:, i, vsl], scalar1=selm[:, b:b + 1], scalar2=None, op0=ALU.mult))
                    for q4 in range(4):
                        A("pe", B["KHk"][0] + ["Vm"], [("ps", pu)], lambda e, q4=q4: e.matmul(ps[pu][:, :], lhsT=B["KH"][0], rhs=Vm[:, q4 * 4:(q4 + 1) * 4, :], start=True, stop=True))
                        for bb in range(4):
                            b = q4 * 4 + bb
                            col = i * 128 + b * 8 + 7
                            A("dve", [("ps", pu)] + B["S0k"] + Fk[2], B["S0k"], lambda e, b=b, bb=bb, col=col: e.scalar_tensor_tensor(out=B["S0"][:, b, :], in0=B["S0"][:, b, :], scalar=F[2][:, col:col + 1], in1=ps[pu][:, bb * 128:(bb + 1) * 128], op0=ALU.mult, op1=ALU.add))
                    A("sp", B["S0k"], [], lambda e: e.dma_start(out=sS[:, hh, :, :].rearrange("b k v -> k b v"), in_=B["S0"]), dma=B["S0g"])
                if not prefix:
                    A("act", [("ps", po)], B["FOk"], lambda e: e.activation(out=B["FO"][:, tsl], in_=ps[po][:, 0:128], func=AF.Copy))

            def hg_stage_c(hh, B):
                F, Fk = B["F"], B["Fk"]
                sg_, gv = wload("win", 8, C_GR + hh * 128, 128)
                pc, pg = B["po"], B["pu"]
                A("act", B["FOk"], B["KTk"], lambda e: e.activation(out=B["KT"], in_=B["FO"], func=AF.Square))
                A("pe", B["KTk"] + ["onesb"], [("ps", pc)], lambda e: e.matmul(ps[pc][:, 0:T], lhsT=onesb[:], rhs=B["KT"], start=True, stop=True))
                yield
                A("act", [("ps", pc)], Fk[1], lambda e: e.activation(out=F[1], in_=ps[pc][:, 0:T], func=AF.Ln, bias=EPS, scale=1.0 / 128))
                A("act", Fk[1], Fk[1], lambda e: e.activation(out=F[1], in_=F[1], func=AF.Exp, scale=-0.5))
                proj_fm(T, sg_, gv, 0, 128, pg)
                yield
                A("act", [("ps", pg)], Fk[3], lambda e: e.activation(out=F[3], in_=ps[pg][:, 0:T], func=AF.Silu))
                A("dve", B["FOk"] + Fk[1] + ["hgn"], B["FOk"], lambda e: e.scalar_tensor_tensor(out=B["FO"], in0=B["FO"], scalar=hgn[:, 0:1], in1=F[1], op0=ALU.mult, op1=ALU.mult))
                A("dve", B["FOk"] + Fk[3], [("orT", hh)], lambda e: e.tensor_tensor(out=orT[:, hh, 0:T], in0=B["FO"], in1=F[3], op=ALU.mult))

            def hg_prefix_a(hh, B):
                F, Fk = B["F"], B["Fk"]
                sf_, fv = wload("win", 8, C_FR + hh * 128, 128)
                proj_fm(T, sf_, fv, 0, 128, 0)
                A("act", [("ps", 0)], Fk[1], lambda e: e.activation(out=F[1], in_=ps[0][:, 0:T], func=AF.Sigmoid))
                A("dve", Fk[1] + ["lb", "oml"], Fk[1], lambda e: e.tensor_scalar(out=F[1], in0=F[1], scalar1=oml[:, hh:hh + 1], scalar2=lb[:, hh:hh + 1], op0=ALU.mult, op1=ALU.add))
                A("act", Fk[1], Fk[2], lambda e: e.activation(out=F[2], in_=F[1], func=AF.Ln))
                A("dve", Fk[1], Fk[1], lambda e: e.tensor_scalar(out=F[1], in0=F[1], scalar1=-1.0, scalar2=1.0, op0=ALU.mult, op1=ALU.add))
                A("dve", Fk[2] + [("actT", 0), ("actT", 1)], Fk[3], lambda e: e.tensor_tensor_scan(out=F[3], data0=onesT[:, 0:T], data1=F[2], initial=0.0, op0=ALU.mult, op1=ALU.add))
                A("act", Fk[3], Fk[0], lambda e: e.activation(out=F[0], in_=F[3], func=AF.Exp, scale=-1.0, bias=F[3][:, T - 1:T]))
                A("act", Fk[3], Fk[2], lambda e: e.activation(out=F[2][:, 0:1], in_=F[3][:, T - 1:T], func=AF.Exp))
                A("dve", Fk[1] + Fk[0], B["KHTk"], lambda e: e.tensor_tensor(out=B["KHT"], in0=F[1], in1=F[0], op=ALU.mult))

            def hg_prefix_b(hh, B):
                F, Fk = B["F"], B["Fk"]
                pu = B["pu"]
                vsl = slice(hh * 128, (hh + 1) * 128)
                for i in range(nt):
                    tsl = slice(i * 128, (i + 1) * 128)
                    A("pe", B["KHTk"] + ["identb"], ["psb"], lambda e, tsl=tsl: e.transpose(out=psb[:, 0:128], in_=B["KHT"][:, tsl], identity=identb[:]))
                    kh, khk = B["KH"][i % 2], B["KHk"][i % 2]
                    if i % 2 == 0:
                        A("act", ["psb"], khk, lambda e, kh=kh: e.activation(out=kh, in_=psb[:, 0:128], func=AF.Copy))
                    else:
                        A("dve", ["psb"], khk, lambda e, kh=kh: e.tensor_copy(out=kh, in_=psb[:, 0:128]))
                    A("pe", khk + [("vr", i)], [("ps", pu)], lambda e, i=i, kh=kh: e.matmul(ps[pu][:, 0:128], lhsT=kh, rhs=vr[:, i, vsl], start=(i == 0), stop=(i == nt - 1)))
                A("dve", [("ps", pu), ("S", hh)] + Fk[2], [("S", hh)], lambda e: e.scalar_tensor_tensor(out=S[:, hh, :], in0=S[:, hh, :], scalar=F[2][:, 0:1], in1=ps[pu][:, 0:128], op0=ALU.mult, op1=ALU.add))
                if prefix_last:
                    A("act", [("S", hh)], [("Sb", hh)], lambda e: e.activation(out=Sb[:, hh, :], in_=S[:, hh, :], func=AF.Copy))

            if prefix:
                onesT = actT[:, 0:2, :].rearrange("p c t -> p (c t)").bitcast(F32)
                A("dve", [], [("actT", 0), ("actT", 1)], lambda e: e.memset(onesT[:, 0:T], 1.0))
                B0, B1 = hg_set(0), hg_set(1)
                for h0 in (0, 2):
                    hg_prefix_a(h0, B0)
                    hg_prefix_a(h0 + 1, B1)
                    hg_prefix_b(h0, B0)
                    hg_prefix_b(h0 + 1, B1)
            elif is_sample:
                B0 = hg_set(0)
                S0alt = actT[:, 8:16, :].rearrange("p c t -> p (c t)").bitcast(F32).rearrange("p (b v) -> p b v", b=16)
                S0balt = actT[:, 16:20, :].rearrange("p c t -> p (c t)").rearrange("p (b v) -> p b v", b=16)
                Bh = [dict(B0, S0=S0[:, :, :], S0k=["S0"], S0b=S0b[:, :, :], S0bk=["S0b"], S0g="s0"),
                      dict(B0, S0=S0alt, S0k=[("actT", c) for c in range(8, 16)], S0b=S0balt, S0bk=[("actT", c) for c in range(16, 20)], S0g="s0b")]

                def s0_load(hh):
                    Bx = Bh[hh % 2]
                    A("sp", [], Bx["S0k"], lambda e: e.dma_start(out=Bx["S0"], in_=s0[:, hh, :, :].rearrange("b k v -> k b v")), dma=Bx["S0g"])
                    A("act", Bx["S0k"], Bx["S0bk"], lambda e: e.activation(out=Bx["S0b"], in_=Bx["S0"], func=AF.Copy))

                s0_load(0)
                for hh in range(4):
                    for _ in hg_stage_a(hh, Bh[hh % 2]):
                        pass
                    if hh + 1 < 4:
                        s0_load(hh + 1)
                    for i in range(nt):
                        hg_tile(hh, i, Bh[hh % 2])
                    for _ in hg_stage_c(hh, Bh[hh % 2]):
                        pass
            else:
                B0, B1 = hg_set(0), hg_set(1)
                def zipgen(*gens):
                    gens = list(gens)
                    while gens:
                        for gn in list(gens):
                            try:
                                next(gn)
                            except StopIteration:
                                gens.remove(gn)

                for h0 in (0, 2):
                    zipgen(hg_stage_a(h0, B0), hg_stage_a(h0 + 1, B1))
                    for i in range(nt):
                        hg_tile(h0, i, B0)
                        hg_tile(h0 + 1, i, B1)
                    zipgen(hg_stage_c(h0, B0), hg_stage_c(h0 + 1, B1))
            if STAGE <= 3 or prefix:
                return
            for n4 in range(2):
                sga_, gav = wload("win", 8, C_GA + n4 * 512, 512)
                sA_, wA4 = wload("wupa", 4, n4 * 512, 512)
                for nn in range(4):
                    n = n4 * 4 + nn
                    p2 = 2 * (nn % 2)
                    nsl = slice(nn * 128, (nn + 1) * 128)
                    fsg, fsgk = fs[p2], "fs%d" % p2
                    for c in range(4):
                        A("pe", [("w", sA_), ("attT", c)], [("ps", p2)], lambda e, c=c, nsl=nsl, wA4=wA4, p2=p2: e.matmul(ps[p2][:, 0:T], lhsT=wA4[:, c, nsl], rhs=attT[:, c, 0:T], start=(c == 0), stop=(c == 3)))
                    proj_fm(T, sga_, gav, nn * 128, 128, p2 + 1)
                    A("act", [("ps", p2 + 1)], [fsgk], lambda e, fsg=fsg, p2=p2: e.activation(out=fsg[:, 0:T], in_=ps[p2 + 1][:, 0:T], func=AF.Sigmoid))
                    A("dve", [("ps", p2), fsgk], m1k(nn), lambda e, nn=nn, fsg=fsg, p2=p2: e.tensor_tensor(out=m1v(nn)[:, 0:T], in0=ps[p2][:, 0:T], in1=fsg[:, 0:T], op=ALU.mult))
                sgr_, grv = wload("win", 8, C_GTR + n4 * 512, 512)
                sR_, wR4 = wload("wupr", 4, n4 * 512, 512)
                for nn in range(4):
                    n = n4 * 4 + nn
                    p2 = 2 * (nn % 2)
                    nsl = slice(nn * 128, (nn + 1) * 128)
                    fsg, fsgk = fs[p2 + 1], "fs%d" % (p2 + 1)
                    for c in range(4):
                        A("pe", [("w", sR_), ("orT", c)], [("ps", p2)], lambda e, c=c, nsl=nsl, wR4=wR4, p2=p2: e.matmul(ps[p2][:, 0:T], lhsT=wR4[:, c, nsl], rhs=orT[:, c, 0:T], start=(c == 0), stop=(c == 3)))
                    proj_fm(T, sgr_, grv, nn * 128, 128, p2 + 1)
                    A("act", [("ps", p2 + 1)], [fsgk], lambda e, fsg=fsg, p2=p2: e.activation(out=fsg[:, 0:T], in_=ps[p2 + 1][:, 0:T], func=AF.Sigmoid))
                    A("dve", [("ps", p2), fsgk], ["fs4"], lambda e, fsg=fsg, p2=p2: e.tensor_tensor(out=fs[4][:, 0:T], in0=ps[p2][:, 0:T], in1=fsg[:, 0:T], op=ALU.mult))
                    A("dve", m1k(nn) + ["fs4"], [("actT", n)], lambda e, n=n, nn=nn: e.tensor_tensor(out=actT[:, n, 0:T], in0=m1v(nn)[:, 0:T], in1=fs[4][:, 0:T], op=ALU.add))
            for n4 in range(2):
                so_, ov = wload("wout", 8, n4 * 512, 512)
                for nn in range(4):
                    n = n4 * 4 + nn
                    pb = n % 2
                    for c in range(8):
                        A("pe", [("w", so_), ("actT", c)], [("ps", pb)], lambda e, c=c, nn=nn, pb=pb, ov=ov: e.matmul(ps[pb][:, 0:T], lhsT=ov[:, c, nn * 128:(nn + 1) * 128], rhs=actT[:, c, 0:T], start=(c == 0), stop=(c == 7)))
                    A("dve", [("ps", pb), ("hT", n)], [("hT", n)], lambda e, n=n, pb=pb: e.tensor_tensor(out=hT[:, n, 0:T], in0=ps[pb][:, 0:T], in1=hT[:, n, 0:T], op=ALU.add))
            rmsnorm(T, g2, "g2")
            ffn(T, "w1b", "w3b", "w2b")
            for i in range(nt):
                sl = xctr[0] % 2
                xctr[0] += 1
                for c in range(8):
                    A("pe", [("hT", c), "ident"], [("ps", 2 + c // 4)], lambda e, c=c, i=i: e.transpose(out=ps[2 + c // 4][:, (c % 4) * 128:(c % 4 + 1) * 128], in_=hT[:, c, i * 128:(i + 1) * 128], identity=ident[:]))
                A("act", [("ps", 2)], [("xst", sl, 0)], lambda e, sl=sl: e.activation(out=xst[sl][:, 0:512], in_=ps[2][:, :], func=AF.Copy))
                A("dve", [("ps", 3)], [("xst", sl, 1)], lambda e, sl=sl: e.tensor_copy(out=xst[sl][:, 512:1024], in_=ps[3][:, :]))
                A("sp", [("xst", sl)], [], lambda e, i=i, sl=sl: e.dma_start(out=ydst[i * 128:(i + 1) * 128, :], in_=xst[sl][:]), dma="x%d" % sl)

        def sample_prep():
            if True:
                A("sp", [], ["S0"], lambda e: e.dma_start(out=cst[:], in_=ck.rearrange("b j c -> j b c")), dma="cs")
            A("sp", ["S0"], [], lambda e: e.dma_start(out=sk.rearrange("b j c -> j b c")[0:120], in_=cst[8:128, :, :]), dma="o2")
            tmpv = Vm[:, :, :].rearrange("p b v -> p (b v)").bitcast(F32).rearrange("p (b g u d) -> p b g u d", b=4, g=2, u=2)
            for bq in range(4):
                for u in range(2):
                    A("act" if u == 0 else "dve", ["S0"], ["Vm"], (lambda e, bq=bq, u=u: e.activation(out=tmpv[:, :, :, u, :], in_=cst[:, 4 * bq:4 * bq + 4, :].rearrange("p b (g d) -> p b g d", g=2), func=AF.Copy)) if u == 0 else (lambda e, bq=bq, u=u: e.tensor_copy(out=tmpv[:, :, :, u, :], in_=cst[:, 4 * bq:4 * bq + 4, :].rearrange("p b (g d) -> p b g d", g=2))))
                for bb in range(4):
                    for g in range(2):
                        idx = bb * 2 + g
                        A("pe", ["Vm", "ident"], [("ps", 2 + idx // 4)], lambda e, bb=bb, g=g, idx=idx: e.transpose(out=ps[2 + idx // 4][:, (idx % 4) * 128:(idx % 4 + 1) * 128], in_=tmpv[:, bb, g, :, :].rearrange("p u d -> p (u d)"), identity=ident[:]))
                for half in range(2):
                    b0 = 4 * bq + 2 * half
                    A("act" if half == 0 else "dve", [("ps", 2 + half)], ["kcT"], (lambda e, b0=b0, half=half: e.activation(out=kcT[:, :, b0:b0 + 2, :], in_=ps[2 + half][:, :].rearrange("p (b g j) -> p g b j", b=2, g=2), func=AF.Copy)) if half == 0 else (lambda e, b0=b0, half=half: e.tensor_copy(out=kcT[:, :, b0:b0 + 2, :], in_=ps[2 + half][:, :].rearrange("p (b g j) -> p g b j", b=2, g=2))))
            A("sp", ["S0"], ["S0"], lambda e: e.dma_start(out=cst[:], in_=cv.rearrange("b j c -> j b c")), dma="cs")
            A("sp", ["S0"], [], lambda e: e.dma_start(out=sv.rearrange("b j c -> j b c")[0:120], in_=cst[8:128, :, :]), dma="o2")
            for dup in range(2):
                A("act", ["S0"], ["vcdup"], lambda e, dup=dup: e.activation(out=vcdup[:, :, :, dup * 64:(dup + 1) * 64], in_=cst[:].rearrange("j b (g d) -> j b g d", g=2), func=AF.Copy))

        nblk = NTP // TB
        for bi in range(nblk):
            rows = slice(bi * TB * 128, (bi + 1) * TB * 128)
            block(TB, False, None, xpre[rows, :], None, False, False, prefix=True, prefix_last=(bi == nblk - 1), after_x=(sample_prep if bi == 1 else None))
        for bi in range(nblk):
            rows = slice(bi * TB * 128, (bi + 1) * TB * 128)
            block(TB, False, None, xp[rows, :], yp[rows, :], False, bi == nblk - 1, halo_mask=(bi == 0))
        A("sp", [("S",)], [], lambda e: e.dma_start(out=pS.rearrange("h k v -> k h v"), in_=S[:]), dma="o2")
        import os
        if os.environ.get('NOSAMPLE'):
            nops = P.emit()
            return nc, nops
        if not os.environ.get('NOSBLOCK'):
            block(1, True, None, xs, ys, True, False)
        nops = P.emit()
    return nc, nops


def _t5_bucket(d):
    d = np.maximum(d, 0)
    large = 16 + (np.log(np.maximum(d, 1) / 16) / np.log(128 / 16) * 16).astype(np.int32)
    large = np.minimum(large, 31)
    return np.where(d < 16, d, large).astype(np.int32)


def _consts():
    c = {}
    c["ident"] = np.eye(128, dtype=np.float32)
    oh = np.zeros((33, 383), np.float32)
    for i in range(383):
        d = i - 127
        if 0 <= d <= 128:
            oh[int(_t5_bucket(np.array(d))), i] = 1.0
        else:
            oh[32, i] = 1.0
    c["oh"] = oh
    s = np.arange(128)[:, None]
    t = np.arange(128)[None, :]
    c["mchunk"] = ((s // 64 == t // 64) & (s <= t)).astype(np.float32)
    c["msamp"] = ((s // 8 == t // 8) & (s <= t)).astype(np.float32)
    c["bmask"] = np.where(s // 8 == t // 8, 0.0, NEG).astype(np.float32)
    c["rstp"] = np.broadcast_to((np.arange(128) % 64 != 0).astype(np.float32), (128, 128)).copy()
    c["rsts"] = np.broadcast_to((np.arange(128) % 8 != 0).astype(np.float32), (128, 128)).copy()
    c["selm"] = (np.arange(128)[:, None] // 8 == np.arange(16)[None, :]).astype(np.float32)
    pm = np.zeros((128, 2), np.float32)
    pm[:64, 0] = 1.0
    pm[64:, 1] = 1.0
    c["pm"] = pm
    c["bd"] = (np.arange(128)[:, None] // 64 == np.arange(128)[None, :] // 64).astype(np.float32)
    return c


_CACHE = {}


def kernel(x_prompt, x_sample, cache_win_k, cache_win_v, state_hgrn,
           ffn1_norm, ffn1_w1, ffn1_w3, ffn1_w2, mix_norm, w_in, q_norm, k_norm, sinks,
           rel_bias_table, hgrn_lb_logits, hg_norm, w_up_attn, w_up_hgrn, w_out,
           ffn2_norm, ffn2_w1, ffn2_w3, ffn2_w2):
    f = lambda a: np.ascontiguousarray(np.asarray(a, dtype=np.float32))
    NC = 8
    if "nc" not in _CACHE:
        _CACHE["nc"] = build_program(16, 4)[0]
    nc = _CACHE["nc"]
    x_prompt = f(x_prompt); x_sample = f(x_sample)
    ckf = f(cache_win_k)[0].reshape(128, 128, 128); cvf = f(cache_win_v)[0].reshape(128, 128, 128)
    s0f = f(state_hgrn)[0]
    g8 = lambda g: f(np.asarray(g)[0].reshape(8, 128).T)
    shared = dict(
        w1a=f(ffn1_w1)[0], w3a=f(ffn1_w3)[0], w2a=f(ffn1_w2)[0],
        w1b=f(ffn2_w1)[0], w3b=f(ffn2_w3)[0], w2b=f(ffn2_w2)[0],
        win=f(w_in)[0], wupa=f(w_up_attn)[0], wupr=f(w_up_hgrn)[0], wout=f(w_out)[0],
        g1=g8(ffn1_norm), gm=g8(mix_norm), g2=g8(ffn2_norm),
        qn=f(np.tile(np.asarray(q_norm)[0].reshape(64, 1), (2, 1))), kn=f(np.tile(np.asarray(k_norm)[0].reshape(64, 1), (2, 1))),
        hgn=f(np.asarray(hg_norm)[0].reshape(128, 1)),
        sinks=f(np.broadcast_to(np.asarray(sinks)[0].reshape(1, 8), (128, 8))),
        lba=f(np.asarray(hgrn_lb_logits)[0].reshape(4, 128).T), lbb=f(np.asarray(hgrn_lb_logits)[1].reshape(4, 128).T),
        tab=f(rel_bias_table),
    )
    shared.update(_consts())
    zeros_p = np.zeros((2048, D), np.float32)
    in_maps = []
    for c in range(NC):
        m = dict(shared)
        p, r = c // 2, c % 2
        m["xp"] = x_prompt[p, r * 2048:(r + 1) * 2048]
        m["xpre"] = x_prompt[p, 0:2048] if r == 1 else zeros_p
        m["pmask"] = np.full((128, 1), 0.0 if r == 1 else NEG, np.float32)
        m["xs"] = x_sample[16 * c:16 * (c + 1)].reshape(128, D)
        m["ck"] = ckf[16 * c:16 * (c + 1)]
        m["cv"] = cvf[16 * c:16 * (c + 1)]
        m["s0"] = s0f[16 * c:16 * (c + 1)]
        in_maps.append(m)
    res = run_bass_kernel_spmd(nc, in_maps, core_ids=list(range(NC))).results
    yp = np.stack([np.concatenate([res[2 * p]["yp"], res[2 * p + 1]["yp"]], 0) for p in range(4)]).astype(np.float32)
    ys = np.concatenate([res[c]["ys"].reshape(16, 8, D) for c in range(NC)]).astype(np.float32)
    pk = np.stack([res[2 * p + 1]["pk"].reshape(128, 2, 64) for p in range(4)])[None].astype(np.float32)
    pv = np.stack([res[2 * p + 1]["pv"].reshape(128, 2, 64) for p in range(4)])[None].astype(np.float32)
    pS = np.stack([res[2 * p + 1]["pS"] for p in range(4)])[None].astype(np.float32)
    sk = np.concatenate([res[c]["sk"].reshape(16, 128, 2, 64) for c in range(NC)])[None].astype(np.float32)
    sv = np.concatenate([res[c]["sv"].reshape(16, 128, 2, 64) for c in range(NC)])[None].astype(np.float32)
    sS = np.concatenate([res[c]["sS"] for c in range(NC)])[None].astype(np.float32)
    return (yp, ys, pk, pv, pS, sk, sv, sS)
```

```python
import bisect
import contextlib
import numpy as np
import concourse.bass as bass
import concourse.mybir as mybir
from concourse.bass_utils import run_bass_kernel_spmd

F32 = mybir.dt.float32
BF16 = mybir.dt.bfloat16
AF = mybir.ActivationFunctionType
ALU = mybir.AluOpType
ENGS = ("pe", "act", "dve", "pool", "sp")
import os as _os
SAME_ENGINE_NOSYNC = ("pe", "sp") if _os.environ.get("SAMESYNC") else ("pe", "sp", "act", "dve")

D = 1024
DFF = 2816
NFF = 22
EPS = 1e-6
NEG = -30000.0
C_QA, C_KA, C_VA, C_QR, C_FR, C_IR, C_GR, C_GA, C_GTR = 0, 512, 640, 768, 1280, 1792, 2304, 2816, 3840


class _Node:
    __slots__ = ("ch", "w", "r")

    def __init__(self):
        self.ch = {}
        self.w = None
        self.r = {}


class Prog:
    def __init__(self, nc):
        self.nc = nc
        self.ops = []
        self.root = _Node()

    def _walk(self, key):
        node = self.root
        path = [node]
        for k in key:
            nxt = node.ch.get(k)
            if nxt is None:
                nxt = _Node()
                node.ch[k] = nxt
            node = nxt
            path.append(node)
        return path, node

    def _subtree(self, node, out):
        for c in node.ch.values():
            out.append(c)
            self._subtree(c, out)

    def _deps_for(self, idx, reads, writes, rkey):
        deps = set()
        for key in reads:
            path, node = self._walk(key)
            rel = list(path)
            self._subtree(node, rel)
            for n in rel:
                if n.w is not None:
                    deps.add(n.w)
            node.r[rkey] = idx
        for key in writes:
            path, node = self._walk(key)
            rel = list(path)
            self._subtree(node, rel)
            for n in rel:
                if n.w is not None:
                    deps.add(n.w)
                deps.update(n.r.values())
            sub = []
            self._subtree(node, sub)
            for n in sub:
                n.w = None
                n.r = {}
            node.w = idx
            node.r = {}
        deps.discard(idx)
        return deps

    def add(self, eng, fn, reads=(), writes=(), dma=None):
        idx = len(self.ops)
        reads = [tuple(k) if isinstance(k, (tuple, list)) else (k,) for k in reads]
        writes = [tuple(k) if isinstance(k, (tuple, list)) else (k,) for k in writes]
        rkey = ("dma", dma, idx) if dma is not None else eng
        deps = self._deps_for(idx, reads, writes, rkey)
        best = {}
        red = set()
        for d in deps:
            p = self.ops[d]
            if p["dma"] is not None:
                red.add(d)
            elif d > best.get(p["eng"], -1):
                best[p["eng"]] = d
        red.update(best.values())
        self.ops.append(dict(eng=eng, fn=fn, deps=red, dma=dma, idx=idx))
        return idx

    def emit(self):
        nc = self.nc
        ops = self.ops

        def stream(o):
            return ("dma", o["dma"]) if o["dma"] is not None else ("eng", o["eng"])

        def skip(p, o):
            if p["dma"] is not None:
                return False
            if p["eng"] == o["eng"]:
                if o["dma"] is None and p["eng"] in SAME_ENGINE_NOSYNC:
                    return True
                if o["dma"] is not None and p["eng"] == "sp":
                    return True
            return False

        need_inc = [False] * len(ops)
        for o in ops:
            for d in o["deps"]:
                p = ops[d]
                if p["dma"] is None and not skip(p, o):
                    need_inc[d] = True
        cnt = {}
        val = [0] * len(ops)
        for o in ops:
            s = stream(o)
            if o["dma"] is not None:
                cnt[s] = cnt.get(s, 0) + 16
                val[o["idx"]] = cnt[s]
            elif need_inc[o["idx"]]:
                cnt[s] = cnt.get(s, 0) + 1
                val[o["idx"]] = cnt[s]
        dma_prefix = {}
        for o in ops:
            if o["dma"] is not None:
                dma_prefix.setdefault(o["dma"], []).append((o["idx"], val[o["idx"]]))
        dma_idx_lists = {g: [a for a, _ in l] for g, l in dma_prefix.items()}

        def dma_target(group, consumer_idx):
            k = bisect.bisect_left(dma_idx_lists[group], consumer_idx)
            return dma_prefix[group][k - 1][1]

        streams = sorted(set(stream(o) for o in ops))
        per_eng = {e: [] for e in ENGS}
        for o in ops:
            per_eng[o["eng"]].append(o)
        waited = {e: {} for e in ENGS}
        for o in ops:
            need = {}
            for d in o["deps"]:
                p = ops[d]
                if skip(p, o):
                    continue
                s = stream(p)
                v = val[d] if p["dma"] is None else dma_target(p["dma"], o["idx"])
                if v > need.get(s, 0):
                    need[s] = v
            w = []
            wd = waited[o["eng"]]
            for s, v in need.items():
                if wd.get(s, 0) >= v:
                    continue
                wd[s] = v
                w.append((s, v))
            o["waits"] = w
            o["inc"] = (stream(o), 16 if o["dma"] is not None else 1) if (
                o["dma"] is not None or need_inc[o["idx"]]) else None
        final = dict(cnt)
        with contextlib.ExitStack() as es:
            sems = {}
            for s in streams:
                sems[s] = es.enter_context(nc.semaphore("s_%s_%s" % s))
            block = es.enter_context(nc.Block())
            handles = {"pe": "tensor", "act": "scalar", "dve": "vector",
                       "pool": "gpsimd", "sp": "sync"}

            def make(engname):
                my_ops = per_eng[engname]

                def body(eng):
                    for o in my_ops:
                        for (s, v) in o["waits"]:
                            eng.wait_ge(sems[s], v)
                        ins = o["fn"](eng)
                        if o["inc"] is not None:
                            ins.then_inc(sems[o["inc"][0]], o["inc"][1])
                    if engname == "sp":
                        for s, c in final.items():
                            eng.wait_ge(sems[s], c)
                return body

            for engname in ENGS:
                getattr(block, handles[engname])(make(engname))
        return len(ops)


def build_program(NTP=32, TB=4):
    nc = bass.Bass("TRN2", target_bir_lowering=False)

    def din(name, shape):
        return nc.dram_tensor(name, shape, F32, kind="ExternalInput").ap()

    def dout(name, shape):
        return nc.dram_tensor(name, shape, F32, kind="ExternalOutput").ap()

    xp = din("xp", [NTP * 128, D]); xpre = din("xpre", [NTP * 128, D]); xs = din("xs", [128, D]); pmkd = din("pmask", [128, 1])
    ck = din("ck", [16, 128, 128]); cv = din("cv", [16, 128, 128]); s0 = din("s0", [16, 4, 128, 128])
    w1a = din("w1a", [D, DFF]); w3a = din("w3a", [D, DFF]); w2a = din("w2a", [DFF, D])
    w1b = din("w1b", [D, DFF]); w3b = din("w3b", [D, DFF]); w2b = din("w2b", [DFF, D])
    win = din("win", [D, 4864]); wupa = din("wupa", [512, D]); wupr = din("wupr", [512, D]); wout = din("wout", [D, D])
    g1d = din("g1", [128, 8]); gmd = din("gm", [128, 8]); g2d = din("g2", [128, 8])
    qnd = din("qn", [128, 1]); knd = din("kn", [128, 1]); bdd = din("bd", [128, 128]); hgnd = din("hgn", [128, 1])
    sinkd = din("sinks", [128, 8]); lbad = din("lba", [128, 4]); lbbd = din("lbb", [128, 4]); tabd = din("tab", [32, 8])
    identd = din("ident", [128, 128]); ohd = din("oh", [33, 383]); mchd = din("mchunk", [128, 128])
    msd = din("msamp", [128, 128]); bmd = din("bmask", [128, 128]); rstpd = din("rstp", [128, 128])
    rstsd = din("rsts", [128, 128]); selmd = din("selm", [128, 16]); pmd = din("pm", [128, 2])
    yp = dout("yp", [NTP * 128, D]); ys = dout("ys", [128, D])
    pk = dout("pk", [128, 128]); pv = dout("pv", [128, 128]); pS = dout("pS", [4, 128, 128])
    sk = dout("sk", [16, 128, 128]); sv = dout("sv", [16, 128, 128]); sS = dout("sS", [16, 4, 128, 128])
    scr = nc.dram_tensor("scr", [8, 128, 383], F32, kind="Internal")
    wsrc = dict(w1a=w1a, w3a=w3a, w2a=w2a, win=win, wupa=wupa, wupr=wupr, wout=wout, w1b=w1b, w3b=w3b, w2b=w2b)
    wbf = {k: nc.dram_tensor(k + "_bf", [128, (v.shape[0] // 128) * v.shape[1]], BF16, kind="Internal") for k, v in wsrc.items()}

    TM = TB * 128
    NSLOT = 6
    with contextlib.ExitStack() as es:
        def sb(name, shape, dt=F32):
            return es.enter_context(nc.sbuf_tensor("sb_" + name, shape, dt))

        def pst(name, shape, dt=F32):
            return es.enter_context(nc.psum_tensor(name, shape, dt))

        hT = sb("hT", [128, 8, TM]); xnT = sb("xnT", [128, 8, TM], BF16); actT = sb("actT", [128, NFF, TM], BF16)
        wsl = [sb("wsl%d" % i, [128, 4096], BF16) for i in range(NSLOT)]
        xst = [sb("xst%d" % i, [128, D]) for i in range(2)]
        rstd = sb("rstd", [128, TM])
        def m1v(nn):
            return actT[:, 8 + 2 * nn:10 + 2 * nn, :].rearrange("p c t -> p (c t)").bitcast(F32)

        def m1k(nn):
            return [("actT", 8 + 2 * nn), ("actT", 9 + 2 * nn)]
        fs = [sb("fs%d" % i, [128, TM]) for i in range(5)]
        fo = sb("fo", [128, TM]); sqb = sb("sqb", [128, TM], BF16)
        qT = sb("qT", [128, 4, TM], BF16); kT = sb("kT", [128, 2, TM], BF16)
        khalo = sb("khalo", [128, 2, 128], BF16); vhalo = sb("vhalo", [128, 2, 128], BF16)
        vdup = sb("vdup", [128, TB, 2, 128], BF16)
        attT = sb("attT", [128, 4, TM], BF16); orT = sb("orT", [128, 4, TM], BF16)
        qt = sb("qt", [128, TM], BF16); kt = sb("kt", [128, TM], BF16); kht = sb("kht", [128, TM], BF16)
        khtok = [sb("khtok%d" % i, [128, 128], BF16) for i in range(2)]
        AT = sb("AT", [128, 128], BF16)
        vr = sb("vr", [128, TB, 512], BF16)
        pT = [sb("pT%d" % i, [128, 512], BF16) for i in range(2)]
        sbf = [sb("sbf%d" % i, [128, 512]) for i in range(2)]
        rd = sb("rd", [128, 512])
        biasO = sb("biasO", [128, 8, 128]); biasP = sb("biasP", [128, 8, 128])
        biasS = sb("biasS", [128, 8, 128]); biasC = sb("biasC", [128, 8, 128])
        S = sb("S", [128, 4, 128]); Sb = sb("Sb", [128, 4, 128], BF16)
        kcT = sb("kcT", [128, 2, 16, 128], BF16); vcdup = sb("vcdup", [128, 16, 2, 128], BF16)
        S0 = sb("S0", [128, 16, 128]); cst = S0; S0b = sb("S0b", [128, 16, 128], BF16); Vm = sb("Vm", [128, 16, 128], BF16)
        kout = sb("kout", [128, 128]); vout = sb("vout", [128, 128]); bdf = sb("bdf", [128, 128]); bdb = sb("bdb", [128, 128], BF16)
        ident = sb("ident", [128, 128]); identb = sb("identb", [128, 128], BF16)
        onesb = sb("onesb", [128, 128], BF16); onesf = sb("onesf", [128, 128])
        g1 = sb("g1s", [128, 8]); gm = sb("gms", [128, 8]); g2 = sb("g2s", [128, 8])
        qn = sb("qns", [128, 1]); kn = sb("kns", [128, 1]); hgn = sb("hgns", [128, 1])
        esink = sb("esink", [128, 8]); lb = sb("lb", [128, 4]); oml = sb("oml", [128, 4]); lbt = sb("lbt", [128, 4])
        tabs = sb("tabs", [33, 8]); ohs = rstd[0:33, 0:383]; tabrep = sb("tabrep", [33, 128]); cb = fo
        mch = sb("mch", [128, 128]); msm = sb("msm", [128, 128]); bmk = sb("bmk", [128, 128])
        pmk = sb("pmk_s", [128, 1]); rstp = sb("rstp_s", [128, 128]); rsts = sb("rsts_s", [128, 128]); selm = sb("selm_s", [128, 16]); pm = sb("pm_s", [128, 2])
        ps = [pst("ps%d" % i, [128, 512]) for i in range(7)]
        psb = pst("psb", [128, 1024], BF16)

        P = Prog(nc)

        def A(eng, reads, writes, fn, dma=None):
            writes = list(writes)
            for k in reads:
                k0 = k[0] if isinstance(k, (tuple, list)) else k
                if k0 in ("ps", "psb"):
                    writes.append(k)
            P.add(eng, fn, reads, writes, dma)

        small = [(ident, identd, "ident"),  (mch, mchd, "mch"), (msm, msd, "msm"), (bmk, bmd, "bmk"),
                 (rstp, rstpd, "rstp"), (rsts, rstsd, "rsts"), (selm, selmd, "selm"), (pm, pmd, "pm"),
                 (g1, g1d, "g1"), (gm, gmd, "gm"), (g2, g2d, "g2"), (bdf, bdd, "bdf"), (qn, qnd, "qn"), (kn, knd, "kn"), (hgn, hgnd, "hgn"),
                 (esink, sinkd, "esink"), (pmk, pmkd, "pmk"), (lb, lbad, "lb"), (lbt, lbbd, "lbt")]
        for (t, d, key) in small:
            A("sp", [], [key], lambda e, t=t, d=d: e.dma_start(out=t[:], in_=d), dma="c")
        A("sp", [], ["rstd"], lambda e: e.dma_start(out=ohs, in_=ohd), dma="c")
        A("dve", [], ["tabs"], lambda e: e.memset(tabs[:], NEG))
        A("sp", [], ["tabs"], lambda e: e.dma_start(out=tabs[0:32, :], in_=tabd), dma="c")
        A("dve", [], ["onesb"], lambda e: e.memset(onesb[:], 1.0))
        A("dve", [], ["onesf"], lambda e: e.memset(onesf[:], 1.0))
        A("dve", [], ["S"], lambda e: e.memset(S[:], 0.0))
        A("dve", [], ["Sb"], lambda e: e.memset(Sb[:], 0.0))
        A("act", ["bdf"], ["bdb"], lambda e: e.activation(out=bdb[:], in_=bdf[:], func=AF.Copy))
        A("dve", [], ["Vm"], lambda e: e.memset(Vm[:], 0.0))
        A("act", ["ident"], ["identb"], lambda e: e.activation(out=identb[:], in_=ident[:], func=AF.Copy))
        A("dve", ["lb", "lbt"], ["lbt"], lambda e: e.tensor_tensor(out=lbt[:], in0=lb[:], in1=lbt[:], op=ALU.subtract))
        A("act", ["lbt"], ["lb"], lambda e: e.activation(out=lb[:], in_=lbt[:], func=AF.Sigmoid))
        A("dve", ["lb"], ["oml"], lambda e: e.tensor_scalar(out=oml[:], in0=lb[:], scalar1=-1.0, scalar2=1.0, op0=ALU.mult, op1=ALU.add))
        A("act", ["esink"], ["esink"], lambda e: e.activation(out=esink[:], in_=esink[:], func=AF.Exp))
        ffn_g = [(c0, min(512, DFF - c0)) for c0 in range(0, DFF, 512)]
        win_g = ([(C_QA, 256), (C_QA + 256, 256), (C_KA, 128), (C_VA, 128)] + [(C_QR + h * 128, 128) for h in range(4)]
                 + [(C_FR + h * 128, 128) for h in range(4)] + [(C_IR, 512)] + [(C_GR + h * 128, 128) for h in range(4)]
                 + [(C_GA, 512), (C_GA + 512, 512), (C_GTR, 512), (C_GTR + 512, 512)])
        GT = dict(w1a=ffn_g, w3a=ffn_g, w1b=ffn_g, w3b=ffn_g, w2a=[(c0, 256) for c0 in range(0, D, 256)], w2b=[(c0, 256) for c0 in range(0, D, 256)],
                  win=win_g, wupa=[(0, 512), (512, 512)], wupr=[(0, 512), (512, 512)], wout=[(0, 512), (512, 512)])
        KCW = {k: v.shape[0] // 128 for k, v in wsrc.items()}

        def conv_jobs(name):
            return [(name, gi) for gi in range(len(GT[name]))]

        def emit_conv(job):
            name, gi = job
            c0, w = GT[name][gi]
            kcw = KCW[name]
            srcf = wsrc[name][:, c0:c0 + w].rearrange("(k p) n -> p k n", p=128)
            dstf = wbf[name].ap()[:, kcw * c0:kcw * (c0 + w)].rearrange("p (k n) -> p k n", k=kcw)
            A("pool", [], [("wb", name, gi)], lambda e, srcf=srcf, dstf=dstf: e.dma_start(out=dstf, in_=srcf), dma="cv_%s_%d" % (name, gi))

        j1, j3 = conv_jobs("w1a"), conv_jobs("w3a")
        emit_conv(j1[0])
        emit_conv(j3[0])
        pending_conv = []
        for a_, b_ in zip(j1[1:], j3[1:]):
            pending_conv += [a_, b_]
        pending_conv += conv_jobs("w2a")
        for nm in ("win", "wupa", "wupr", "wout"):
            pending_conv += conv_jobs(nm)
        jb1, jb3 = conv_jobs("w1b"), conv_jobs("w3b")
        for a_, b_ in zip(jb1, jb3):
            pending_conv += [a_, b_]
        pending_conv += conv_jobs("w2b")
        for h in range(8):
            A("dve", ["onesf", "tabs"], ["tabrep"], lambda e, h=h: e.tensor_scalar(out=tabrep[:], in0=onesf[0:33, :], scalar1=tabs[:, h:h + 1], scalar2=None, op0=ALU.mult))
            pbk = h % 2
            cbh, cbk = (fo, "fo") if h % 2 == 0 else (fs[4], "fs4")
            A("pe", ["tabrep", "rstd"], [("ps", pbk)], lambda e, pbk=pbk: e.matmul(ps[pbk][:, 0:383], lhsT=tabrep[:], rhs=ohs, start=True, stop=True))
            A("act", [("ps", pbk)], [cbk], lambda e, pbk=pbk, cbh=cbh: e.activation(out=cbh[:, 0:383], in_=ps[pbk][:, 0:383], func=AF.Copy))
            A("act", [cbk], [("scr", h)], lambda e, h=h, cbh=cbh: e.dma_start(out=scr.ap()[h], in_=cbh[:, 0:383]), dma="b1%d" % (h % 2))
        for h in range(8):
            so = bass.AP(tensor=scr, offset=h * 128 * 383 + 127, ap=[[382, 128], [1, 128]])
            sp_ = bass.AP(tensor=scr, offset=h * 128 * 383 + 255, ap=[[382, 128], [1, 128]])
            A("act", [("scr", h)], [("biasO", h)], lambda e, h=h, so=so: e.dma_start(out=biasO[:, h, :], in_=so), dma="b2")
            A("act", [("scr", h)], [("biasP", h)], lambda e, h=h, sp_=sp_: e.dma_start(out=biasP[:, h, :], in_=sp_), dma="b2")
        A("dve", ["biasO", "bmk"], ["biasS"], lambda e: e.tensor_tensor(out=biasS[:], in0=biasO[:], in1=bmk[:].unsqueeze(1).to_broadcast([128, 8, 128]), op=ALU.add))
        A("dve", ["biasP"], ["biasC"], lambda e: e.tensor_copy(out=biasC[:].rearrange("p h (b q) -> p h b q", b=16), in_=biasP[:, :, 0:8].unsqueeze(2).to_broadcast([128, 8, 16, 8])))

        wctr = [0]

        def wload(wname, kc, c0, n, nsl=None, k0=0):
            gi = GT[wname].index((c0, n))
            need = [k for k, jb in enumerate(pending_conv) if jb == (wname, gi)]
            if need:
                for jb in pending_conv[:need[-1] + 1]:
                    emit_conv(jb)
                del pending_conv[:need[-1] + 1]
            s = wctr[0] % (nsl or NSLOT)
            wctr[0] += 1
            view = wsl[s][:, 0:kc * n].rearrange("p (k n) -> p k n", k=kc)
            base = KCW[wname] * c0 + k0 * n
            src = wbf[wname].ap()[:, base:base + kc * n]
            A("pool", [("wb", wname, gi)], [("w", s)], lambda e: e.dma_start(out=wsl[s][:, 0:kc * n], in_=src), dma="w%d" % s)
            if pending_conv:
                emit_conv(pending_conv.pop(0))
            return s, view

        def rmsnorm(T, gain, gkey):
            for c in range(8):
                A("act", [("hT", c)], [("xnT", c)], lambda e, c=c: e.activation(out=xnT[:, c, 0:T], in_=hT[:, c, 0:T], func=AF.Square))
            for c in range(8):
                A("pe", [("xnT", c), "onesb"], [("ps", 0)], lambda e, c=c: e.matmul(ps[0][:, 0:T], lhsT=onesb[:], rhs=xnT[:, c, 0:T], start=(c == 0), stop=(c == 7)))
            A("act", [("ps", 0)], ["rstd"], lambda e: e.activation(out=rstd[:, 0:T], in_=ps[0][:, 0:T], func=AF.Ln, bias=EPS, scale=1.0 / D))
            A("act", ["rstd"], ["rstd"], lambda e: e.activation(out=rstd[:, 0:T], in_=rstd[:, 0:T], func=AF.Exp, scale=-0.5))
            for c in range(8):
                A("dve", [("hT", c), "rstd", gkey], [("xnT", c)], lambda e, c=c: e.scalar_tensor_tensor(out=xnT[:, c, 0:T], in0=hT[:, c, 0:T], scalar=gain[:, c:c + 1], in1=rstd[:, 0:T], op0=ALU.mult, op1=ALU.mult))

        def proj_fm(T, s, view, c0, m, psi, kc=8):
            for c in range(kc):
                A("pe", [("w", s), ("xnT", c)], [("ps", psi)], lambda e, c=c: e.matmul(ps[psi][0:m, 0:T], lhsT=view[:, c, c0:c0 + m], rhs=xnT[:, c, 0:T], start=(c == 0), stop=(c == kc - 1)))

        def ffn(T, w1, w3, w2):
            for j0 in range(0, NFF, 4):
                gw = min(4, NFF - j0)
                s1, v1 = wload(w1, 8, j0 * 128, gw * 128)
                s3, v3 = wload(w3, 8, j0 * 128, gw * 128)
                for jj in range(gw):
                    j = j0 + jj
                    proj_fm(T, s1, v1, jj * 128, 128, 0)
                    proj_fm(T, s3, v3, jj * 128, 128, 1)
                    f = fs[j % 2]
                    fk = "fs%d" % (j % 2)
                    A("act", [("ps", 0)], [fk], lambda e, f=f: e.activation(out=f[:, 0:T], in_=ps[0][:, 0:T], func=AF.Silu))
                    A("dve", [("ps", 1), fk], [("actT", j)], lambda e, f=f, j=j: e.tensor_tensor(out=actT[:, j, 0:T], in0=ps[1][:, 0:T], in1=f[:, 0:T], op=ALU.mult))
            HK = NFF // 2
            for n2 in range(4):
                s2a, v2a = wload(w2, HK, n2 * 256, 256, k0=0)
                s2b, v2b = wload(w2, HK, n2 * 256, 256, k0=HK)
                for nn in range(2):
                    n = n2 * 2 + nn
                    pb = n % 2
                    for j in range(NFF):
                        s2, v2, jj = (s2a, v2a, j) if j < HK else (s2b, v2b, j - HK)
                        A("pe", [("w", s2), ("actT", j)], [("ps", pb)], lambda e, j=j, jj=jj, v2=v2, pb=pb, nn=nn: e.matmul(ps[pb][:, 0:T], lhsT=v2[:, jj, nn * 128:(nn + 1) * 128], rhs=actT[:, j, 0:T], start=(j == 0), stop=(j == NFF - 1)))
                    A("dve", [("ps", pb), ("hT", n)], [("hT", n)], lambda e, n=n, pb=pb: e.scalar_tensor_tensor(out=hT[:, n, 0:T], in0=ps[pb][:, 0:T], scalar=0.5, in1=hT[:, n, 0:T], op0=ALU.mult, op1=ALU.add))

        def qknorm(T, psi, gain, gkey, dst, dkey):
            A("act", [("ps", psi)], ["sqb"], lambda e: e.activation(out=sqb[:, 0:T], in_=ps[psi][:, 0:T], func=AF.Square))
            A("pe", ["sqb", "bdb"], [("ps", 6)], lambda e: e.matmul(ps[6][:, 0:T], lhsT=bdb[:], rhs=sqb[:, 0:T], start=True, stop=True))
            A("act", [("ps", 6)], ["fs4"], lambda e: e.activation(out=fs[4][:, 0:T], in_=ps[6][:, 0:T], func=AF.Ln, bias=EPS, scale=1.0 / 64))
            A("act", ["fs4"], ["fs4"], lambda e: e.activation(out=fs[4][:, 0:T], in_=fs[4][:, 0:T], func=AF.Exp, scale=-0.5))
            A("dve", [("ps", psi), "fs4", gkey], [dkey], lambda e: e.scalar_tensor_tensor(out=dst, in0=ps[psi][:, 0:T], scalar=gain[:, 0:1], in1=fs[4][:, 0:T], op0=ALU.mult, op1=ALU.mult))

        import os
        STAGE = float(os.environ.get('STAGE', '9'))
        SST = float(os.environ.get('SSTAGE', '9'))
        xctr = [0]

        def block(tiles, first_tile_of_seq, out_rows, xsrc, ydst, is_sample, last_prompt, prefix=False, prefix_last=False, halo_mask=False, after_x=None):
            nt = tiles
            T = nt * 128
            for i in range(nt):
                sl = xctr[0] % 2
                xctr[0] += 1
                A("sp", [], [("xst", sl)], lambda e, i=i, sl=sl: e.dma_start(out=xst[sl][:], in_=xsrc[i * 128:(i + 1) * 128, :]), dma="x%d" % sl)
                for c in range(8):
                    A("pe", [("xst", sl), "ident"], [("ps", 2 + c // 4)], lambda e, c=c, sl=sl: e.transpose(out=ps[2 + c // 4][:, (c % 4) * 128:(c % 4 + 1) * 128], in_=xst[sl][:, c * 128:(c + 1) * 128], identity=ident[:]))
                A("act", [("ps", 2)], [("hT", c_, i) for c_ in range(4)], lambda e, i=i: e.activation(out=hT[:, 0:4, i * 128:(i + 1) * 128], in_=ps[2][:, :].rearrange("p (c t) -> p c t", c=4), func=AF.Copy))
                A("dve", [("ps", 3)], [("hT", c_, i) for c_ in range(4, 8)], lambda e, i=i: e.tensor_copy(out=hT[:, 4:8, i * 128:(i + 1) * 128], in_=ps[3][:, :].rearrange("p (c t) -> p c t", c=4)))
            if after_x is not None:
                after_x()
            rmsnorm(T, g1, "g1")
            ffn(T, "w1a", "w3a", "w2a")
            if STAGE <= 1:
                return
            rmsnorm(T, gm, "gm")
            if is_sample and SST <= 1.1:
                return
            do_kv = (not prefix) or prefix_last
            if do_kv:
                sv_, vv = wload("win", 8, C_VA, 128)
            for i in (range(nt) if do_kv else []):
                for c in range(8):
                    A("pe", [("w", sv_), ("xnT", c)], [("ps", 2)], lambda e, c=c, i=i: e.matmul(ps[2][:, 0:128], lhsT=xnT[:, c, i * 128:(i + 1) * 128], rhs=vv[:, c, 0:128], start=(c == 0), stop=(c == 7)))
                for dup in range(2):
                    A("act", [("ps", 2)], [("vdup", i)], lambda e, i=i, dup=dup: e.activation(out=vdup[:, i, :, dup * 64:(dup + 1) * 64], in_=ps[2][:, 0:128].rearrange("p (g d) -> p g d", g=2), func=AF.Copy))
                if is_sample and SST <= 1.2:
                    return
                if i == nt - 1 and (is_sample or last_prompt):
                    A("act", [("ps", 2)], ["vout"], lambda e: e.activation(out=vout[:], in_=ps[2][:, 0:128], func=AF.Copy))
                    if is_sample:
                        NB = int(os.environ.get('NB', '16'))
                        for b in range(NB):
                            A("sp", ["vout"], [], lambda e, b=b: e.dma_start(out=sv[b, 120:128, :], in_=vout[b * 8:(b + 1) * 8, :]), dma="o1")
                    else:
                        A("sp", ["vout"], [], lambda e: e.dma_start(out=pv, in_=vout[:]), dma="o1")
            if is_sample and SST <= 1.3:
                return
            want_k = is_sample or last_prompt
            if do_kv:
                sk_, kv = wload("win", 8, C_KA, 128)
            for g in (range(2) if do_kv else []):
                pad = Vm[:, 8 * g:8 * g + 8, :]
                for dup in range(2):
                    A("dve", [("w", sk_)], ["Vm"], lambda e, g=g, dup=dup, pad=pad: e.tensor_copy(out=pad[:, :, dup * 64:(dup + 1) * 64], in_=kv[:, :, g * 64:(g + 1) * 64]))
                for c in range(8):
                    A("pe", ["Vm", ("xnT", c)], [("ps", g)], lambda e, c=c, g=g, pad=pad: e.matmul(ps[g][:, 0:T], lhsT=pad[:, c, :], rhs=xnT[:, c, 0:T], start=(c == 0), stop=(c == 7)))
                qknorm(T, g, kn, "kn", kT[:, g, 0:T], ("kT", g))
                if want_k:
                    t0 = (nt - 1) * 128
                    A("dve", [("ps", g), "fs4", "kn"], [("fs3", g)], lambda e, g=g, t0=t0: e.scalar_tensor_tensor(out=fs[3][:, g * 128:(g + 1) * 128], in0=ps[g][:, t0:t0 + 128], scalar=kn[:, 0:1], in1=fs[4][:, t0:t0 + 128], op0=ALU.mult, op1=ALU.mult))
                    A("pe", [("fs3", g), "ident"], [("ps", 2)], lambda e, g=g: e.transpose(out=ps[2][:, g * 128:(g + 1) * 128], in_=fs[3][:, g * 128:(g + 1) * 128], identity=ident[:]))
                    A("act", [("ps", 2)], [("kout", g)], lambda e, g=g: e.activation(out=kout[:, g * 64:(g + 1) * 64], in_=ps[2][:, g * 128:g * 128 + 64], func=AF.Copy))
            if is_sample and SST <= 1.4:
                return
            if want_k:
                if is_sample:
                    for b in range(16):
                        A("sp", ["kout"], [], lambda e, b=b: e.dma_start(out=sk[b, 120:128, :], in_=kout[b * 8:(b + 1) * 8, :]), dma="o1")
                else:
                    A("sp", ["kout"], [], lambda e: e.dma_start(out=pk, in_=kout[:]), dma="o1")
            if STAGE <= 1.5:
                return
            for g in (range(2) if not prefix else []):
                sq_, qv = wload("win", 8, C_QA + g * 256, 256)
                for c2 in range(2):
                    for c in range(8):
                        A("pe", [("w", sq_), ("xnT", c)], [("ps", c2)], lambda e, c=c, c2=c2, qv=qv: e.matmul(ps[c2][:, 0:T], lhsT=qv[:, c, c2 * 128:(c2 + 1) * 128], rhs=xnT[:, c, 0:T], start=(c == 0), stop=(c == 7)))
                    qknorm(T, c2, qn, "qn", qt[:, 0:T], "qt")
                    for par in range(2):
                        hl = 2 * c2 + par
                        A("dve", ["qt", "pm"], [("qT", hl)], lambda e, hl=hl, par=par: e.tensor_scalar(out=qT[:, hl, 0:T], in0=qt[:, 0:T], scalar1=pm[:, par:par + 1], scalar2=None, op0=ALU.mult))
                b_own = biasS if is_sample else biasO
                b_prev = biasC if is_sample else biasP

                def att_set(sid):
                    if sid == 0:
                        return dict(sc=(2, 3), po=4, pd=ps[5], pdk=("ps", 5), P=[pT[0][:, :], pT[1][:, :]], Pk=[["pT0"], ["pT1"]],
                                    SB=[sbf[0][:, :], sbf[1][:, :]], SBk=[["sbf0"], ["sbf1"]], RD=rd[:, :], RDk=["rd"])
                    f32v = lambda c: actT[:, c:c + 2, :].rearrange("p c t -> p (c t)").bitcast(F32)
                    return dict(sc=(0, 1), po=6, pd=psb[:, :].bitcast(F32), pdk="psb", P=[actT[:, 0, :], actT[:, 1, :]], Pk=[[("actT", 0)], [("actT", 1)]],
                                SB=[f32v(2), f32v(4)], SBk=[[("actT", 2), ("actT", 3)], [("actT", 4), ("actT", 5)]], RD=f32v(6), RDk=[("actT", 6), ("actT", 7)])

                def att_scores(i, W, g):
                    tsl = slice(i * 128, (i + 1) * 128)
                    has_prev = is_sample or not (first_tile_of_seq and i == 0)
                    so, sp2 = W["sc"]
                    A("pe", [("kT", g), "qT"], [("ps", so)], lambda e: e.matmul(ps[so][:, :], lhsT=kT[:, g, tsl], rhs=qT[:, :, tsl], start=True, stop=True))
                    A("dve", [("ps", so), "biasO", "biasS"], W["SBk"][0], lambda e: e.scalar_tensor_tensor(out=W["SB"][0], in0=ps[so][:, :], scalar=0.125, in1=b_own[:, 4 * g:4 * g + 4, :].rearrange("p h q -> p (h q)"), op0=ALU.mult, op1=ALU.add))
                    A("act", W["SBk"][0], W["Pk"][0], lambda e: e.activation(out=W["P"][0], in_=W["SB"][0], func=AF.Exp))
                    if has_prev:
                        if is_sample:
                            for b in range(16):
                                A("pe", ["kcT", "qT"], [("ps", sp2)], lambda e, b=b: e.matmul(ps[sp2][:, :].rearrange("p (h b q) -> p h b q", h=4, b=16)[:, :, b, :], lhsT=kcT[:, g, b, :], rhs=qT[:, :, i * 128 + b * 8:i * 128 + b * 8 + 8], start=True, stop=True))
                        elif i > 0:
                            psl = slice((i - 1) * 128, i * 128)
                            A("pe", [("kT", g), "qT"], [("ps", sp2)], lambda e: e.matmul(ps[sp2][:, :], lhsT=kT[:, g, psl], rhs=qT[:, :, tsl], start=True, stop=True))
                        else:
                            A("pe", [("khalo", g), "qT"], [("ps", sp2)], lambda e: e.matmul(ps[sp2][:, :], lhsT=khalo[:, g, :], rhs=qT[:, :, tsl], start=True, stop=True))
                        A("dve", [("ps", sp2), "biasP", "biasC"], W["SBk"][1], lambda e: e.scalar_tensor_tensor(out=W["SB"][1], in0=ps[sp2][:, :], scalar=0.125, in1=b_prev[:, 4 * g:4 * g + 4, :].rearrange("p h q -> p (h q)"), op0=ALU.mult, op1=ALU.add))
                        if halo_mask and i == 0:
                            A("dve", W["SBk"][1] + ["pmk"], W["SBk"][1], lambda e: e.tensor_scalar(out=W["SB"][1], in0=W["SB"][1], scalar1=pmk[:, 0:1], scalar2=None, op0=ALU.add))
                        A("act", W["SBk"][1], W["Pk"][1], lambda e: e.activation(out=W["P"][1], in_=W["SB"][1], func=AF.Exp))

                def att_pv(i, W, g):
                    tsl = slice(i * 128, (i + 1) * 128)
                    has_prev = is_sample or not (first_tile_of_seq and i == 0)
                    po, pd, pdk = W["po"], W["pd"], W["pdk"]
                    P0, P1 = W["P"]
                    A("pe", [("vdup", i)] + W["Pk"][0], [("ps", po)], lambda e: e.matmul(ps[po][:, :], lhsT=vdup[:, i, g, :], rhs=P0, start=True, stop=not has_prev))
                    A("pe", ["onesb"] + W["Pk"][0], [pdk], lambda e: e.matmul(pd[:, :], lhsT=onesb[:], rhs=P0, start=True, stop=not has_prev))
                    if has_prev:
                        if is_sample:
                            for b in range(16):
                                A("pe", ["vcdup"] + W["Pk"][1], [("ps", po)], lambda e, b=b: e.matmul(ps[po][:, :].rearrange("p (h b q) -> p h b q", h=4, b=16)[:, :, b, :], lhsT=vcdup[:, b, g, :], rhs=P1.rearrange("p (h b q) -> p h b q", h=4, b=16)[:, :, b, :], start=False, stop=(b == 15)))
                        elif i > 0:
                            A("pe", [("vdup", i - 1)] + W["Pk"][1], [("ps", po)], lambda e: e.matmul(ps[po][:, :], lhsT=vdup[:, i - 1, g, :], rhs=P1, start=False, stop=True))
                        else:
                            A("pe", [("vhalo", g)] + W["Pk"][1], [("ps", po)], lambda e: e.matmul(ps[po][:, :], lhsT=vhalo[:, g, :], rhs=P1, start=False, stop=True))
                        A("pe", ["onesb"] + W["Pk"][1], [pdk], lambda e: e.matmul(pd[:, :], lhsT=onesb[:], rhs=P1, start=False, stop=True))
                    RD = W["RD"]
                    for hl in range(4):
                        h = 4 * g + hl
                        A("act", [pdk, "esink"], W["RDk"], lambda e, hl=hl, h=h: e.activation(out=RD[:, hl * 128:(hl + 1) * 128], in_=pd[:, hl * 128:(hl + 1) * 128], func=AF.Ln, bias=esink[:, h:h + 1]))
                    A("act", W["RDk"], W["RDk"], lambda e: e.activation(out=RD, in_=RD, func=AF.Exp, scale=-1.0))
                    for hl in range(4):
                        h = 4 * g + hl
                        hp = (h % 2) * 64
                        A("dve", [("ps", po)] + W["RDk"], [("attT", h // 2, h % 2, i)], lambda e, hl=hl, h=h, hp=hp: e.tensor_tensor(out=attT[hp:hp + 64, h // 2, tsl], in0=ps[po][hp:hp + 64, hl * 128:(hl + 1) * 128], in1=RD[hp:hp + 64, hl * 128:(hl + 1) * 128], op=ALU.mult))

                W2 = [att_set(0), att_set(1)]
                for k in range(nt + 1):
                    if k < nt:
                        att_scores(k, W2[k % 2], g)
                    if k >= 1:
                        att_pv(k - 1, W2[(k - 1) % 2], g)
            if not is_sample and do_kv:
                A("act", ["kT"], ["khalo"], lambda e: e.activation(out=khalo[:, :, :], in_=kT[:, :, (nt - 1) * 128:nt * 128], func=AF.Copy))
                A("dve", [("vdup", nt - 1)], ["vhalo"], lambda e: e.tensor_copy(out=vhalo[:, :, :], in_=vdup[:, nt - 1, :, :]))
            if STAGE <= 2:
                return
            si_, iv = wload("win", 8, C_IR, 512)
            for i in range(nt):
                for c in range(8):
                    A("pe", [("w", si_), ("xnT", c)], [("ps", 2)], lambda e, c=c, i=i: e.matmul(ps[2][:, :], lhsT=xnT[:, c, i * 128:(i + 1) * 128], rhs=iv[:, c, :], start=(c == 0), stop=(c == 7)))
                A("act", [("ps", 2)], [("vr", i)], lambda e, i=i: e.activation(out=vr[:, i, :], in_=ps[2][:, :], func=AF.Copy))
            rst = rsts if is_sample else rstp
            msk = msm if is_sample else mch

            def hg_set(sid):
                if sid == 0:
                    F = [fs[0], fs[1], fs[2], fs[3]]
                    return dict(F=[f[:, 0:T] for f in F], Fk=[["fs0"], ["fs1"], ["fs2"], ["fs3"]], FO=fo[:, 0:T], FOk=["fo"],
                                QT=qt[:, 0:T], QTk=["qt"], KT=kt[:, 0:T], KTk=["kt"], KHT=kht[:, 0:T], KHTk=["kht"],
                                AT=AT[:, :], ATk=["AT"], KH=[khtok[0][:, :], khtok[1][:, :]], KHk=[[("khtok", 0)], [("khtok", 1)]],
                                po=4, pu=5)
                def f32v(c):
                    return actT[:, c:c + 2, :].rearrange("p c t -> p (c t)").bitcast(F32)[:, 0:T]
                return dict(F=[f32v(8), f32v(10), f32v(12), f32v(14)], Fk=[[("actT", 8), ("actT", 9)], [("actT", 10), ("actT", 11)], [("actT", 12), ("actT", 13)], [("actT", 14), ("actT", 15)]],
                            FO=f32v(16), FOk=[("actT", 16), ("actT", 17)],
                            QT=actT[:, 18, 0:T], QTk=[("actT", 18)], KT=actT[:, 19, 0:T], KTk=[("actT", 19)], KHT=actT[:, 20, 0:T], KHTk=[("actT", 20)],
                            AT=actT[:, 21, 0:128], ATk=[("actT", 21, 0)], KH=[actT[:, 21, 128:256], actT[:, 21, 256:384]], KHk=[[("actT", 21, 1)], [("actT", 21, 2)]],
                            po=6, pu=3)

            def hg_stage_a(hh, B):
                F, Fk = B["F"], B["Fk"]
                sf_, fv = wload("win", 8, C_FR + hh * 128, 128)
                if not prefix:
                    sqr_, qrv = wload("win", 8, C_QR + hh * 128, 128)
                pf = B["po"] if not is_sample else 0
                pq = B["pu"] if not is_sample else 1
                proj_fm(T, sf_, fv, 0, 128, pf)
                A("act", [("ps", pf)], Fk[1], lambda e: e.activation(out=F[1], in_=ps[pf][:, 0:T], func=AF.Sigmoid))
                A("dve", Fk[1] + ["lb", "oml"], Fk[1], lambda e, hh=hh: e.tensor_scalar(out=F[1], in0=F[1], scalar1=oml[:, hh:hh + 1], scalar2=lb[:, hh:hh + 1], op0=ALU.mult, op1=ALU.add))
                yield
                A("act", Fk[1], Fk[2], lambda e: e.activation(out=F[2], in_=F[1], func=AF.Ln))
                A("dve", Fk[1], Fk[1], lambda e: e.tensor_scalar(out=F[1], in0=F[1], scalar1=-1.0, scalar2=1.0, op0=ALU.mult, op1=ALU.add))
                for i in range(nt):
                    A("dve", Fk[2] + ["rstp", "rsts"], Fk[3], lambda e, i=i: e.tensor_tensor_scan(out=F[3][:, i * 128:(i + 1) * 128], data0=rst[:, :], data1=F[2][:, i * 128:(i + 1) * 128], initial=0.0, op0=ALU.mult, op1=ALU.add))
                yield
                A("act", Fk[3], Fk[2], lambda e: e.activation(out=F[2], in_=F[3], func=AF.Exp))
                A("act", Fk[3], Fk[0], lambda e: e.activation(out=F[0], in_=F[3], func=AF.Exp, scale=-1.0))
                yield
                if not prefix:
                    proj_fm(T, sqr_, qrv, 0, 128, pq)
                    A("act", [("ps", pq)], Fk[3], lambda e: e.activation(out=F[3], in_=ps[pq][:, 0:T], func=AF.Silu))
                    A("dve", Fk[3] + Fk[2], B["QTk"], lambda e: e.tensor_tensor(out=B["QT"], in0=F[3], in1=F[2], op=ALU.mult))
                A("dve", Fk[1] + Fk[0], Fk[0], lambda e: e.tensor_tensor(out=F[0], in0=F[1], in1=F[0], op=ALU.mult))
                yield
                if not prefix:
                    A("act", Fk[0], B["KTk"], lambda e: e.activation(out=B["KT"], in_=F[0], func=AF.Copy))
                glen = 8 if is_sample else 64
                for c0 in range(0, T, glen):
                    A("dve", Fk[0] + Fk[2], B["KHTk"], lambda e, c0=c0: e.tensor_scalar(out=B["KHT"][:, c0:c0 + glen], in0=F[0][:, c0:c0 + glen], scalar1=F[2][:, c0 + glen - 1:c0 + glen], scalar2=None, op0=ALU.mult))

            def hg_tile(hh, i, B):
                F, Fk = B["F"], B["Fk"]
                po, pu = B["po"], B["pu"]
                vsl = slice(hh * 128, (hh + 1) * 128)
                tsl = slice(i * 128, (i + 1) * 128)
                A("pe", B["KHTk"] + ["identb"], ["psb"], lambda e: e.transpose(out=psb[:, 0:128], in_=B["KHT"][:, tsl], identity=identb[:]))
                if not prefix:
                    A("pe", B["KTk"] + B["QTk"], [("ps", pu)], lambda e: e.matmul(ps[pu][:, 0:128], lhsT=B["KT"][:, tsl], rhs=B["QT"][:, tsl], start=True, stop=True))
                    A("dve", [("ps", pu), "mch", "msm"], B["ATk"], lambda e: e.tensor_tensor(out=B["AT"], in0=ps[pu][:, 0:128], in1=msk[:, :], op=ALU.mult))
                    A("pe", [("vr", i)] + B["ATk"], [("ps", po)], lambda e: e.matmul(ps[po][:, 0:128], lhsT=vr[:, i, vsl], rhs=B["AT"], start=True, stop=False))
                if not is_sample:
                    for ch in range(2):
                        A("dve", ["psb", "pm"], B["KHk"][ch], lambda e, ch=ch: e.tensor_scalar(out=B["KH"][ch], in0=psb[:, 0:128], scalar1=pm[:, ch:ch + 1], scalar2=None, op0=ALU.mult))
                    for ch in range(2):
                        cs = i * 128 + ch * 64
                        if not prefix:
                            A("pe", [("Sb", hh)] + B["QTk"], [("ps", po)], lambda e, cs=cs, ch=ch: e.matmul(ps[po][:, ch * 64:(ch + 1) * 64], lhsT=Sb[:, hh, :], rhs=B["QT"][:, cs:cs + 64], start=False, stop=(ch == 1)))
                        A("pe", B["KHk"][ch] + [("vr", i)], [("ps", pu)], lambda e, ch=ch: e.matmul(ps[pu][:, 128:256], lhsT=B["KH"][ch], rhs=vr[:, i, vsl], start=True, stop=True))
                        A("dve", [("ps", pu), ("S", hh)] + Fk[2], [("S", hh)], lambda e, cs=cs: e.scalar_tensor_tensor(out=S[:, hh, :], in0=S[:, hh, :], scalar=F[2][:, cs + 63:cs + 64], in1=ps[pu][:, 128:256], op0=ALU.mult, op1=ALU.add))
                        if (not prefix) or (prefix_last and i == nt - 1 and ch == 1):
                            A("act", [("S", hh)], [("Sb", hh)], lambda e: e.activation(out=Sb[:, hh, :], in_=S[:, hh, :], func=AF.Copy))
                else:
                    A("act", ["psb"], B["KHk"][0], lambda e: e.activation(out=B["KH"][0], in_=psb[:, 0:128], func=AF.Copy))
                    for b in range(16):
                        A("pe", B["S0bk"] + B["QTk"], [("ps", po)], lambda e, b=b: e.matmul(ps[po][:, b * 8:(b + 1) * 8], lhsT=B["S0b"][:, b, :], rhs=B["QT"][:, i * 128 + b * 8:i * 128 + b * 8 + 8], start=False, stop=(b == 15)))
                    for b in range(16):
                        A("dve", [("vr", i), "selm"], [("Vm", b)], lambda e, b=b: e.tensor_scalar(out=Vm[:, b, :], in0=vr[:, i, vsl], scalar1=selm[:, b:b + 1], scalar2=None, op0=ALU.mult))
                    for q4 in range(4):
                        A("pe", B["KHk"][0] + ["Vm"], [("ps", pu)], lambda e, q4=q4: e.matmul(ps[pu][:, :], lhsT=B["KH"][0], rhs=Vm[:, q4 * 4:(q4 + 1) * 4, :], start=True, stop=True))
                        for bb in range(4):
                            b = q4 * 4 + bb
                            col = i * 128 + b * 8 + 7
                            A("dve", [("ps", pu)] + B["S0k"] + Fk[2], B["S0k"], lambda e, b=b, bb=bb, col=col: e.scalar_tensor_tensor(out=B["S0"][:, b, :], in0=B["S0"][:, b, :], scalar=F[2][:, col:col + 1], in1=ps[pu][:, bb * 128:(bb + 1) * 128], op0=ALU.mult, op1=ALU.add))
                    A("sp", B["S0k"], [], lambda e: e.dma_start(out=sS[:, hh, :, :].rearrange("b k v -> k b v"), in_=B["S0"]), dma=B["S0g"])
                if not prefix:
                    A("act", [("ps", po)], B["FOk"], lambda e: e.activation(out=B["FO"][:, tsl], in_=ps[po][:, 0:128], func=AF.Copy))

            def hg_stage_c(hh, B):
                F, Fk = B["F"], B["Fk"]
                sg_, gv = wload("win", 8, C_GR + hh * 128, 128)
                pc, pg = B["po"], B["pu"]
                A("act", B["FOk"], B["KTk"], lambda e: e.activation(out=B["KT"], in_=B["FO"], func=AF.Square))
                A("pe", B["KTk"] + ["onesb"], [("ps", pc)], lambda e: e.matmul(ps[pc][:, 0:T], lhsT=onesb[:], rhs=B["KT"], start=True, stop=True))
                yield
                A("act", [("ps", pc)], Fk[1], lambda e: e.activation(out=F[1], in_=ps[pc][:, 0:T], func=AF.Ln, bias=EPS, scale=1.0 / 128))
                A("act", Fk[1], Fk[1], lambda e: e.activation(out=F[1], in_=F[1], func=AF.Exp, scale=-0.5))
                proj_fm(T, sg_, gv, 0, 128, pg)
                yield
                A("act", [("ps", pg)], Fk[3], lambda e: e.activation(out=F[3], in_=ps[pg][:, 0:T], func=AF.Silu))
                A("dve", B["FOk"] + Fk[1] + ["hgn"], B["FOk"], lambda e: e.scalar_tensor_tensor(out=B["FO"], in0=B["FO"], scalar=hgn[:, 0:1], in1=F[1], op0=ALU.mult, op1=ALU.mult))
                A("dve", B["FOk"] + Fk[3], [("orT", hh)], lambda e: e.tensor_tensor(out=orT[:, hh, 0:T], in0=B["FO"], in1=F[3], op=ALU.mult))

            def hg_prefix_a(hh, B):
                F, Fk = B["F"], B["Fk"]
                sf_, fv = wload("win", 8, C_FR + hh * 128, 128)
                proj_fm(T, sf_, fv, 0, 128, 0)
                A("act", [("ps", 0)], Fk[1], lambda e: e.activation(out=F[1], in_=ps[0][:, 0:T], func=AF.Sigmoid))
                A("dve", Fk[1] + ["lb", "oml"], Fk[1], lambda e: e.tensor_scalar(out=F[1], in0=F[1], scalar1=oml[:, hh:hh + 1], scalar2=lb[:, hh:hh + 1], op0=ALU.mult, op1=ALU.add))
                A("act", Fk[1], Fk[2], lambda e: e.activation(out=F[2], in_=F[1], func=AF.Ln))
                A("dve", Fk[1], Fk[1], lambda e: e.tensor_scalar(out=F[1], in0=F[1], scalar1=-1.0, scalar2=1.0, op0=ALU.mult, op1=ALU.add))
                A("dve", Fk[2] + [("actT", 0), ("actT", 1)], Fk[3], lambda e: e.tensor_tensor_scan(out=F[3], data0=onesT[:, 0:T], data1=F[2], initial=0.0, op0=ALU.mult, op1=ALU.add))
                A("act", Fk[3], Fk[0], lambda e: e.activation(out=F[0], in_=F[3], func=AF.Exp, scale=-1.0, bias=F[3][:, T - 1:T]))
                A("act", Fk[3], Fk[2], lambda e: e.activation(out=F[2][:, 0:1], in_=F[3][:, T - 1:T], func=AF.Exp))
                A("dve", Fk[1] + Fk[0], B["KHTk"], lambda e: e.tensor_tensor(out=B["KHT"], in0=F[1], in1=F[0], op=ALU.mult))

            def hg_prefix_b(hh, B):
                F, Fk = B["F"], B["Fk"]
                pu = B["pu"]
                vsl = slice(hh * 128, (hh + 1) * 128)
                for i in range(nt):
                    tsl = slice(i * 128, (i + 1) * 128)
                    A("pe", B["KHTk"] + ["identb"], ["psb"], lambda e, tsl=tsl: e.transpose(out=psb[:, 0:128], in_=B["KHT"][:, tsl], identity=identb[:]))
                    kh, khk = B["KH"][i % 2], B["KHk"][i % 2]
                    if i % 2 == 0:
                        A("act", ["psb"], khk, lambda e, kh=kh: e.activation(out=kh, in_=psb[:, 0:128], func=AF.Copy))
                    else:
                        A("dve", ["psb"], khk, lambda e, kh=kh: e.tensor_copy(out=kh, in_=psb[:, 0:128]))
                    A("pe", khk + [("vr", i)], [("ps", pu)], lambda e, i=i, kh=kh: e.matmul(ps[pu][:, 0:128], lhsT=kh, rhs=vr[:, i, vsl], start=(i == 0), stop=(i == nt - 1)))
                A("dve", [("ps", pu), ("S", hh)] + Fk[2], [("S", hh)], lambda e: e.scalar_tensor_tensor(out=S[:, hh, :], in0=S[:, hh, :], scalar=F[2][:, 0:1], in1=ps[pu][:, 0:128], op0=ALU.mult, op1=ALU.add))
                if prefix_last:
                    A("act", [("S", hh)], [("Sb", hh)], lambda e: e.activation(out=Sb[:, hh, :], in_=S[:, hh, :], func=AF.Copy))

            if prefix:
                onesT = actT[:, 0:2, :].rearrange("p c t -> p (c t)").bitcast(F32)
                A("dve", [], [("actT", 0), ("actT", 1)], lambda e: e.memset(onesT[:, 0:T], 1.0))
                B0, B1 = hg_set(0), hg_set(1)
                for h0 in (0, 2):
                    hg_prefix_a(h0, B0)
                    hg_prefix_a(h0 + 1, B1)
                    hg_prefix_b(h0, B0)
                    hg_prefix_b(h0 + 1, B1)
            elif is_sample:
                B0 = hg_set(0)
                S0alt = actT[:, 8:16, :].rearrange("p c t -> p (c t)").bitcast(F32).rearrange("p (b v) -> p b v", b=16)
                S0balt = actT[:, 16:20, :].rearrange("p c t -> p (c t)").rearrange("p (b v) -> p b v", b=16)
                Bh = [dict(B0, S0=S0[:, :, :], S0k=["S0"], S0b=S0b[:, :, :], S0bk=["S0b"], S0g="s0"),
                      dict(B0, S0=S0alt, S0k=[("actT", c) for c in range(8, 16)], S0b=S0balt, S0bk=[("actT", c) for c in range(16, 20)], S0g="s0b")]

                def s0_load(hh):
                    Bx = Bh[hh % 2]
                    A("sp", [], Bx["S0k"], lambda e: e.dma_start(out=Bx["S0"], in_=s0[:, hh, :, :].rearrange("b k v -> k b v")), dma=Bx["S0g"])
                    A("act", Bx["S0k"], Bx["S0bk"], lambda e: e.activation(out=Bx["S0b"], in_=Bx["S0"], func=AF.Copy))

                s0_load(0)
                for hh in range(4):
                    for _ in hg_stage_a(hh, Bh[hh % 2]):
                        pass
                    if hh + 1 < 4:
                        s0_load(hh + 1)
                    for i in range(nt):
                        hg_tile(hh, i, Bh[hh % 2])
                    for _ in hg_stage_c(hh, Bh[hh % 2]):
                        pass
            else:
                B0, B1 = hg_set(0), hg_set(1)
                def zipgen(*gens):
                    gens = list(gens)
                    while gens:
                        for gn in list(gens):
                            try:
                                next(gn)
                            except StopIteration:
                                gens.remove(gn)

                for h0 in (0, 2):
                    zipgen(hg_stage_a(h0, B0), hg_stage_a(h0 + 1, B1))
                    for i in range(nt):
                        hg_tile(h0, i, B0)
                        hg_tile(h0 + 1, i, B1)
                    zipgen(hg_stage_c(h0, B0), hg_stage_c(h0 + 1, B1))
            if STAGE <= 3 or prefix:
                return
            for n4 in range(2):
                sga_, gav = wload("win", 8, C_GA + n4 * 512, 512)
                sA_, wA4 = wload("wupa", 4, n4 * 512, 512)
                for nn in range(4):
                    n = n4 * 4 + nn
                    p2 = 2 * (nn % 2)
                    nsl = slice(nn * 128, (nn + 1) * 128)
                    fsg, fsgk = fs[p2], "fs%d" % p2
                    for c in range(4):
                        A("pe", [("w", sA_), ("attT", c)], [("ps", p2)], lambda e, c=c, nsl=nsl, wA4=wA4, p2=p2: e.matmul(ps[p2][:, 0:T], lhsT=wA4[:, c, nsl], rhs=attT[:, c, 0:T], start=(c == 0), stop=(c == 3)))
                    proj_fm(T, sga_, gav, nn * 128, 128, p2 + 1)
                    A("act", [("ps", p2 + 1)], [fsgk], lambda e, fsg=fsg, p2=p2: e.activation(out=fsg[:, 0:T], in_=ps[p2 + 1][:, 0:T], func=AF.Sigmoid))
                    A("dve", [("ps", p2), fsgk], m1k(nn), lambda e, nn=nn, fsg=fsg, p2=p2: e.tensor_tensor(out=m1v(nn)[:, 0:T], in0=ps[p2][:, 0:T], in1=fsg[:, 0:T], op=ALU.mult))
                sgr_, grv = wload("win", 8, C_GTR + n4 * 512, 512)
                sR_, wR4 = wload("wupr", 4, n4 * 512, 512)
                for nn in range(4):
                    n = n4 * 4 + nn
                    p2 = 2 * (nn % 2)
                    nsl = slice(nn * 128, (nn + 1) * 128)
                    fsg, fsgk = fs[p2 + 1], "fs%d" % (p2 + 1)
                    for c in range(4):
                        A("pe", [("w", sR_), ("orT", c)], [("ps", p2)], lambda e, c=c, nsl=nsl, wR4=wR4, p2=p2: e.matmul(ps[p2][:, 0:T], lhsT=wR4[:, c, nsl], rhs=orT[:, c, 0:T], start=(c == 0), stop=(c == 3)))
                    proj_fm(T, sgr_, grv, nn * 128, 128, p2 + 1)
                    A("act", [("ps", p2 + 1)], [fsgk], lambda e, fsg=fsg, p2=p2: e.activation(out=fsg[:, 0:T], in_=ps[p2 + 1][:, 0:T], func=AF.Sigmoid))
                    A("dve", [("ps", p2), fsgk], ["fs4"], lambda e, fsg=fsg, p2=p2: e.tensor_tensor(out=fs[4][:, 0:T], in0=ps[p2][:, 0:T], in1=fsg[:, 0:T], op=ALU.mult))
                    A("dve", m1k(nn) + ["fs4"], [("actT", n)], lambda e, n=n, nn=nn: e.tensor_tensor(out=actT[:, n, 0:T], in0=m1v(nn)[:, 0:T], in1=fs[4][:, 0:T], op=ALU.add))
            for n4 in range(2):
                so_, ov = wload("wout", 8, n4 * 512, 512)
                for nn in range(4):
                    n = n4 * 4 + nn
                    pb = n % 2
                    for c in range(8):
                        A("pe", [("w", so_), ("actT", c)], [("ps", pb)], lambda e, c=c, nn=nn, pb=pb, ov=ov: e.matmul(ps[pb][:, 0:T], lhsT=ov[:, c, nn * 128:(nn + 1) * 128], rhs=actT[:, c, 0:T], start=(c == 0), stop=(c == 7)))
                    A("dve", [("ps", pb), ("hT", n)], [("hT", n)], lambda e, n=n, pb=pb: e.tensor_tensor(out=hT[:, n, 0:T], in0=ps[pb][:, 0:T], in1=hT[:, n, 0:T], op=ALU.add))
            rmsnorm(T, g2, "g2")
            ffn(T, "w1b", "w3b", "w2b")
            for i in range(nt):
                sl = xctr[0] % 2
                xctr[0] += 1
                for c in range(8):
                    A("pe", [("hT", c), "ident"], [("ps", 2 + c // 4)], lambda e, c=c, i=i: e.transpose(out=ps[2 + c // 4][:, (c % 4) * 128:(c % 4 + 1) * 128], in_=hT[:, c, i * 128:(i + 1) * 128], identity=ident[:]))
                A("act", [("ps", 2)], [("xst", sl, 0)], lambda e, sl=sl: e.activation(out=xst[sl][:, 0:512], in_=ps[2][:, :], func=AF.Copy))
                A("dve", [("ps", 3)], [("xst", sl, 1)], lambda e, sl=sl: e.tensor_copy(out=xst[sl][:, 512:1024], in_=ps[3][:, :]))
                A("act", [("xst", sl)], [], lambda e, i=i, sl=sl: e.dma_start(out=ydst[i * 128:(i + 1) * 128, :], in_=xst[sl][:]), dma="x%d" % sl)

        def sample_prep():
            if True:
                A("sp", [], ["S0"], lambda e: e.dma_start(out=cst[:], in_=ck.rearrange("b j c -> j b c")), dma="cs")
            A("sp", ["S0"], [], lambda e: e.dma_start(out=sk.rearrange("b j c -> j b c")[0:120], in_=cst[8:128, :, :]), dma="o2")
            tmpv = Vm[:, :, :].rearrange("p b v -> p (b v)").bitcast(F32).rearrange("p (b g u d) -> p b g u d", b=4, g=2, u=2)
            for bq in range(4):
                for u in range(2):
                    A("act" if u == 0 else "dve", ["S0"], ["Vm"], (lambda e, bq=bq, u=u: e.activation(out=tmpv[:, :, :, u, :], in_=cst[:, 4 * bq:4 * bq + 4, :].rearrange("p b (g d) -> p b g d", g=2), func=AF.Copy)) if u == 0 else (lambda e, bq=bq, u=u: e.tensor_copy(out=tmpv[:, :, :, u, :], in_=cst[:, 4 * bq:4 * bq + 4, :].rearrange("p b (g d) -> p b g d", g=2))))
                for bb in range(4):
                    for g in range(2):
                        idx = bb * 2 + g
                        A("pe", ["Vm", "ident"], [("ps", 2 + idx // 4)], lambda e, bb=bb, g=g, idx=idx: e.transpose(out=ps[2 + idx // 4][:, (idx % 4) * 128:(idx % 4 + 1) * 128], in_=tmpv[:, bb, g, :, :].rearrange("p u d -> p (u d)"), identity=ident[:]))
                for half in range(2):
                    b0 = 4 * bq + 2 * half
                    A("act" if half == 0 else "dve", [("ps", 2 + half)], ["kcT"], (lambda e, b0=b0, half=half: e.activation(out=kcT[:, :, b0:b0 + 2, :], in_=ps[2 + half][:, :].rearrange("p (b g j) -> p g b j", b=2, g=2), func=AF.Copy)) if half == 0 else (lambda e, b0=b0, half=half: e.tensor_copy(out=kcT[:, :, b0:b0 + 2, :], in_=ps[2 + half][:, :].rearrange("p (b g j) -> p g b j", b=2, g=2))))
            A("sp", ["S0"], ["S0"], lambda e: e.dma_start(out=cst[:], in_=cv.rearrange("b j c -> j b c")), dma="cs")
            A("sp", ["S0"], [], lambda e: e.dma_start(out=sv.rearrange("b j c -> j b c")[0:120], in_=cst[8:128, :, :]), dma="o2")
            for dup in range(2):
                A("act", ["S0"], ["vcdup"], lambda e, dup=dup: e.activation(out=vcdup[:, :, :, dup * 64:(dup + 1) * 64], in_=cst[:].rearrange("j b (g d) -> j b g d", g=2), func=AF.Copy))

        nblk = NTP // TB
        for bi in range(nblk):
            rows = slice(bi * TB * 128, (bi + 1) * TB * 128)
            block(TB, False, None, xpre[rows, :], None, False, False, prefix=True, prefix_last=(bi == nblk - 1), after_x=(sample_prep if bi == 1 else None))
        for bi in range(nblk):
            rows = slice(bi * TB * 128, (bi + 1) * TB * 128)
            block(TB, False, None, xp[rows, :], yp[rows, :], False, bi == nblk - 1, halo_mask=(bi == 0))
        A("sp", [("S",)], [], lambda e: e.dma_start(out=pS.rearrange("h k v -> k h v"), in_=S[:]), dma="o2")
        import os
        if os.environ.get('NOSAMPLE'):
            nops = P.emit()
            return nc, nops
        if not os.environ.get('NOSBLOCK'):
            block(1, True, None, xs, ys, True, False)
        nops = P.emit()
    return nc, nops


def _t5_bucket(d):
    d = np.maximum(d, 0)
    large = 16 + (np.log(np.maximum(d, 1) / 16) / np.log(128 / 16) * 16).astype(np.int32)
    large = np.minimum(large, 31)
    return np.where(d < 16, d, large).astype(np.int32)


def _consts():
    c = {}
    c["ident"] = np.eye(128, dtype=np.float32)
    oh = np.zeros((33, 383), np.float32)
    for i in range(383):
        d = i - 127
        if 0 <= d <= 128:
            oh[int(_t5_bucket(np.array(d))), i] = 1.0
        else:
            oh[32, i] = 1.0
    c["oh"] = oh
    s = np.arange(128)[:, None]
    t = np.arange(128)[None, :]
    c["mchunk"] = ((s // 64 == t // 64) & (s <= t)).astype(np.float32)
    c["msamp"] = ((s // 8 == t // 8) & (s <= t)).astype(np.float32)
    c["bmask"] = np.where(s // 8 == t // 8, 0.0, NEG).astype(np.float32)
    c["rstp"] = np.broadcast_to((np.arange(128) % 64 != 0).astype(np.float32), (128, 128)).copy()
    c["rsts"] = np.broadcast_to((np.arange(128) % 8 != 0).astype(np.float32), (128, 128)).copy()
    c["selm"] = (np.arange(128)[:, None] // 8 == np.arange(16)[None, :]).astype(np.float32)
    pm = np.zeros((128, 2), np.float32)
    pm[:64, 0] = 1.0
    pm[64:, 1] = 1.0
    c["pm"] = pm
    c["bd"] = (np.arange(128)[:, None] // 64 == np.arange(128)[None, :] // 64).astype(np.float32)
    return c


_CACHE = {}


def kernel(x_prompt, x_sample, cache_win_k, cache_win_v, state_hgrn,
           ffn1_norm, ffn1_w1, ffn1_w3, ffn1_w2, mix_norm, w_in, q_norm, k_norm, sinks,
           rel_bias_table, hgrn_lb_logits, hg_norm, w_up_attn, w_up_hgrn, w_out,
           ffn2_norm, ffn2_w1, ffn2_w3, ffn2_w2):
    f = lambda a: np.ascontiguousarray(np.asarray(a, dtype=np.float32))
    NC = 8
    if "nc" not in _CACHE:
        _CACHE["nc"] = build_program(16, 4)[0]
    nc = _CACHE["nc"]
    x_prompt = f(x_prompt); x_sample = f(x_sample)
    ckf = f(cache_win_k)[0].reshape(128, 128, 128); cvf = f(cache_win_v)[0].reshape(128, 128, 128)
    s0f = f(state_hgrn)[0]
    g8 = lambda g: f(np.asarray(g)[0].reshape(8, 128).T)
    shared = dict(
        w1a=f(ffn1_w1)[0], w3a=f(ffn1_w3)[0], w2a=f(ffn1_w2)[0],
        w1b=f(ffn2_w1)[0], w3b=f(ffn2_w3)[0], w2b=f(ffn2_w2)[0],
        win=f(w_in)[0], wupa=f(w_up_attn)[0], wupr=f(w_up_hgrn)[0], wout=f(w_out)[0],
        g1=g8(ffn1_norm), gm=g8(mix_norm), g2=g8(ffn2_norm),
        qn=f(np.tile(np.asarray(q_norm)[0].reshape(64, 1), (2, 1))), kn=f(np.tile(np.asarray(k_norm)[0].reshape(64, 1), (2, 1))),
        hgn=f(np.asarray(hg_norm)[0].reshape(128, 1)),
        sinks=f(np.broadcast_to(np.asarray(sinks)[0].reshape(1, 8), (128, 8))),
        lba=f(np.asarray(hgrn_lb_logits)[0].reshape(4, 128).T), lbb=f(np.asarray(hgrn_lb_logits)[1].reshape(4, 128).T),
        tab=f(rel_bias_table),
    )
    shared.update(_consts())
    zeros_p = np.zeros((2048, D), np.float32)
    in_maps = []
    for c in range(NC):
        m = dict(shared)
        p, r = c // 2, c % 2
        m["xp"] = x_prompt[p, r * 2048:(r + 1) * 2048]
        m["xpre"] = x_prompt[p, 0:2048] if r == 1 else zeros_p
        m["pmask"] = np.full((128, 1), 0.0 if r == 1 else NEG, np.float32)
        m["xs"] = x_sample[16 * c:16 * (c + 1)].reshape(128, D)
        m["ck"] = ckf[16 * c:16 * (c + 1)]
        m["cv"] = cvf[16 * c:16 * (c + 1)]
        m["s0"] = s0f[16 * c:16 * (c + 1)]
        in_maps.append(m)
    res = run_bass_kernel_spmd(nc, in_maps, core_ids=list(range(NC))).results
    yp = np.stack([np.concatenate([res[2 * p]["yp"], res[2 * p + 1]["yp"]], 0) for p in range(4)]).astype(np.float32)
    ys = np.concatenate([res[c]["ys"].reshape(16, 8, D) for c in range(NC)]).astype(np.float32)
    pk = np.stack([res[2 * p + 1]["pk"].reshape(128, 2, 64) for p in range(4)])[None].astype(np.float32)
    pv = np.stack([res[2 * p + 1]["pv"].reshape(128, 2, 64) for p in range(4)])[None].astype(np.float32)
    pS = np.stack([res[2 * p + 1]["pS"] for p in range(4)])[None].astype(np.float32)
    sk = np.concatenate([res[c]["sk"].reshape(16, 128, 2, 64) for c in range(NC)])[None].astype(np.float32)
    sv = np.concatenate([res[c]["sv"].reshape(16, 128, 2, 64) for c in range(NC)])[None].astype(np.float32)
    sS = np.concatenate([res[c]["sS"] for c in range(NC)])[None].astype(np.float32)
    return (yp, ys, pk, pv, pS, sk, sv, sS)
```

```python
import bisect
import contextlib
import numpy as np
import concourse.bass as bass
import concourse.mybir as mybir
from concourse.bass_utils import run_bass_kernel_spmd

F32 = mybir.dt.float32
BF16 = mybir.dt.bfloat16
AF = mybir.ActivationFunctionType
ALU = mybir.AluOpType
ENGS = ("pe", "act", "dve", "pool", "sp")
import os as _os
SAME_ENGINE_NOSYNC = ("pe", "sp") if _os.environ.get("SAMESYNC") else ("pe", "sp", "act", "dve")

D = 1024
DFF = 2816
NFF = 22
EPS = 1e-6
NEG = -30000.0
C_QA, C_KA, C_VA, C_QR, C_FR, C_IR, C_GR, C_GA, C_GTR = 0, 512, 640, 768, 1280, 1792, 2304, 2816, 3840


class _Node:
    __slots__ = ("ch", "w", "r")

    def __init__(self):
        self.ch = {}
        self.w = None
        self.r = {}


class Prog:
    def __init__(self, nc):
        self.nc = nc
        self.ops = []
        self.root = _Node()

    def _walk(self, key):
        node = self.root
        path = [node]
        for k in key:
            nxt = node.ch.get(k)
            if nxt is None:
                nxt = _Node()
                node.ch[k] = nxt
            node = nxt
            path.append(node)
        return path, node

    def _subtree(self, node, out):
        for c in node.ch.values():
            out.append(c)
            self._subtree(c, out)

    def _deps_for(self, idx, reads, writes, rkey):
        deps = set()
        for key in reads:
            path, node = self._walk(key)
            rel = list(path)
            self._subtree(node, rel)
            for n in rel:
                if n.w is not None:
                    deps.add(n.w)
            node.r[rkey] = idx
        for key in writes:
            path, node = self._walk(key)
            rel = list(path)
            self._subtree(node, rel)
            for n in rel:
                if n.w is not None:
                    deps.add(n.w)
                deps.update(n.r.values())
            sub = []
            self._subtree(node, sub)
            for n in sub:
                n.w = None
                n.r = {}
            node.w = idx
            node.r = {}
        deps.discard(idx)
        return deps

    def add(self, eng, fn, reads=(), writes=(), dma=None):
        idx = len(self.ops)
        reads = [tuple(k) if isinstance(k, (tuple, list)) else (k,) for k in reads]
        writes = [tuple(k) if isinstance(k, (tuple, list)) else (k,) for k in writes]
        rkey = ("dma", dma, idx) if dma is not None else eng
        deps = self._deps_for(idx, reads, writes, rkey)
        best = {}
        red = set()
        for d in deps:
            p = self.ops[d]
            if p["dma"] is not None:
                red.add(d)
            elif d > best.get(p["eng"], -1):
                best[p["eng"]] = d
        red.update(best.values())
        self.ops.append(dict(eng=eng, fn=fn, deps=red, dma=dma, idx=idx))
        return idx

    def emit(self):
        nc = self.nc
        ops = self.ops

        def stream(o):
            return ("dma", o["dma"]) if o["dma"] is not None else ("eng", o["eng"])

        def skip(p, o):
            if p["dma"] is not None:
                return False
            if p["eng"] == o["eng"]:
                if o["dma"] is None and p["eng"] in SAME_ENGINE_NOSYNC:
                    return True
                if o["dma"] is not None and p["eng"] == "sp":
                    return True
            return False

        need_inc = [False] * len(ops)
        for o in ops:
            for d in o["deps"]:
                p = ops[d]
                if p["dma"] is None and not skip(p, o):
                    need_inc[d] = True
        cnt = {}
        val = [0] * len(ops)
        for o in ops:
            s = stream(o)
            if o["dma"] is not None:
                cnt[s] = cnt.get(s, 0) + 16
                val[o["idx"]] = cnt[s]
            elif need_inc[o["idx"]]:
                cnt[s] = cnt.get(s, 0) + 1
                val[o["idx"]] = cnt[s]
        dma_prefix = {}
        for o in ops:
            if o["dma"] is not None:
                dma_prefix.setdefault(o["dma"], []).append((o["idx"], val[o["idx"]]))
        dma_idx_lists = {g: [a for a, _ in l] for g, l in dma_prefix.items()}

        def dma_target(group, consumer_idx):
            k = bisect.bisect_left(dma_idx_lists[group], consumer_idx)
            return dma_prefix[group][k - 1][1]

        streams = sorted(set(stream(o) for o in ops))
        per_eng = {e: [] for e in ENGS}
        for o in ops:
            per_eng[o["eng"]].append(o)
        waited = {e: {} for e in ENGS}
        for o in ops:
            need = {}
            for d in o["deps"]:
                p = ops[d]
                if skip(p, o):
                    continue
                s = stream(p)
                v = val[d] if p["dma"] is None else dma_target(p["dma"], o["idx"])
                if v > need.get(s, 0):
                    need[s] = v
            w = []
            wd = waited[o["eng"]]
            for s, v in need.items():
                if wd.get(s, 0) >= v:
                    continue
                wd[s] = v
                w.append((s, v))
            o["waits"] = w
            o["inc"] = (stream(o), 16 if o["dma"] is not None else 1) if (
                o["dma"] is not None or need_inc[o["idx"]]) else None
        final = dict(cnt)
        with contextlib.ExitStack() as es:
            sems = {}
            for s in streams:
                sems[s] = es.enter_context(nc.semaphore("s_%s_%s" % s))
            block = es.enter_context(nc.Block())
            handles = {"pe": "tensor", "act": "scalar", "dve": "vector",
                       "pool": "gpsimd", "sp": "sync"}

            def make(engname):
                my_ops = per_eng[engname]

                def body(eng):
                    for o in my_ops:
                        for (s, v) in o["waits"]:
                            eng.wait_ge(sems[s], v)
                        ins = o["fn"](eng)
                        if o["inc"] is not None:
                            ins.then_inc(sems[o["inc"][0]], o["inc"][1])
                    if engname == "sp":
                        for s, c in final.items():
                            eng.wait_ge(sems[s], c)
                return body

            for engname in ENGS:
                getattr(block, handles[engname])(make(engname))
        return len(ops)


def build_program(NTP=32, TB=4):
    nc = bass.Bass("TRN2", target_bir_lowering=False)

    def din(name, shape):
        return nc.dram_tensor(name, shape, F32, kind="ExternalInput").ap()

    def dout(name, shape):
        return nc.dram_tensor(name, shape, F32, kind="ExternalOutput").ap()

    xp = din("xp", [NTP * 128, D]); xpre = din("xpre", [NTP * 128, D]); xs = din("xs", [128, D]); pmkd = din("pmask", [128, 1])
    ck = din("ck", [16, 128, 128]); cv = din("cv", [16, 128, 128]); s0 = din("s0", [16, 4, 128, 128])
    w1a = din("w1a", [D, DFF]); w3a = din("w3a", [D, DFF]); w2a = din("w2a", [DFF, D])
    w1b = din("w1b", [D, DFF]); w3b = din("w3b", [D, DFF]); w2b = din("w2b", [DFF, D])
    win = din("win", [D, 4864]); wupa = din("wupa", [512, D]); wupr = din("wupr", [512, D]); wout = din("wout", [D, D])
    g1d = din("g1", [128, 8]); gmd = din("gm", [128, 8]); g2d = din("g2", [128, 8])
    qnd = din("qn", [128, 1]); knd = din("kn", [128, 1]); bdd = din("bd", [128, 128]); hgnd = din("hgn", [128, 1])
    sinkd = din("sinks", [128, 8]); lbad = din("lba", [128, 4]); lbbd = din("lbb", [128, 4]); tabd = din("tab", [32, 8])
    identd = din("ident", [128, 128]); ohd = din("oh", [33, 383]); mchd = din("mchunk", [128, 128])
    msd = din("msamp", [128, 128]); bmd = din("bmask", [128, 128]); rstpd = din("rstp", [128, 128])
    rstsd = din("rsts", [128, 128]); selmd = din("selm", [128, 16]); pmd = din("pm", [128, 2])
    yp = dout("yp", [NTP * 128, D]); ys = dout("ys", [128, D])
    pk = dout("pk", [128, 128]); pv = dout("pv", [128, 128]); pS = dout("pS", [4, 128, 128])
    sk = dout("sk", [16, 128, 128]); sv = dout("sv", [16, 128, 128]); sS = dout("sS", [16, 4, 128, 128])
    scr = nc.dram_tensor("scr", [8, 128, 383], F32, kind="Internal")
    wsrc = dict(w1a=w1a, w3a=w3a, w2a=w2a, win=win, wupa=wupa, wupr=wupr, wout=wout, w1b=w1b, w3b=w3b, w2b=w2b)
    wbf = {k: nc.dram_tensor(k + "_bf", [128, (v.shape[0] // 128) * v.shape[1]], BF16, kind="Internal") for k, v in wsrc.items()}

    TM = TB * 128
    NSLOT = 6
    with contextlib.ExitStack() as es:
        def sb(name, shape, dt=F32):
            return es.enter_context(nc.sbuf_tensor("sb_" + name, shape, dt))

        def pst(name, shape, dt=F32):
            return es.enter_context(nc.psum_tensor(name, shape, dt))

        hT = sb("hT", [128, 8, TM]); xnT = sb("xnT", [128, 8, TM], BF16); actT = sb("actT", [128, NFF, TM], BF16)
        wsl = [sb("wsl%d" % i, [128, 4096], BF16) for i in range(NSLOT)]
        xst = [sb("xst%d" % i, [128, D]) for i in range(2)]
        rstd = sb("rstd", [128, TM])
        def m1v(nn):
            return actT[:, 8 + 2 * nn:10 + 2 * nn, :].rearrange("p c t -> p (c t)").bitcast(F32)

        def m1k(nn):
            return [("actT", 8 + 2 * nn), ("actT", 9 + 2 * nn)]
        fs = [sb("fs%d" % i, [128, TM]) for i in range(5)]
        fo = sb("fo", [128, TM]); sqb = sb("sqb", [128, TM], BF16)
        qT = sb("qT", [128, 4, TM], BF16); kT = sb("kT", [128, 2, TM], BF16)
        khalo = sb("khalo", [128, 2, 128], BF16); vhalo = sb("vhalo", [128, 2, 128], BF16)
        vdup = sb("vdup", [128, TB, 2, 128], BF16)
        attT = sb("attT", [128, 4, TM], BF16); orT = sb("orT", [128, 4, TM], BF16)
        qt = sb("qt", [128, TM], BF16); kt = sb("kt", [128, TM], BF16); kht = sb("kht", [128, TM], BF16)
        khtok = [sb("khtok%d" % i, [128, 128], BF16) for i in range(2)]
        AT = sb("AT", [128, 128], BF16)
        vr = sb("vr", [128, TB, 512], BF16)
        pT = [sb("pT%d" % i, [128, 512], BF16) for i in range(2)]
        sbf = [sb("sbf%d" % i, [128, 512]) for i in range(2)]
        rd = sb("rd", [128, 512])
        biasO = sb("biasO", [128, 8, 128]); biasP = sb("biasP", [128, 8, 128])
        biasS = sb("biasS", [128, 8, 128]); biasC = sb("biasC", [128, 8, 128])
        S = sb("S", [128, 4, 128]); Sb = sb("Sb", [128, 4, 128], BF16)
        kcT = sb("kcT", [128, 2, 16, 128], BF16); vcdup = sb("vcdup", [128, 16, 2, 128], BF16)
        S0 = sb("S0", [128, 16, 128]); cst = S0; S0b = sb("S0b", [128, 16, 128], BF16); Vm = sb("Vm", [128, 16, 128], BF16)
        kout = sb("kout", [128, 128]); vout = sb("vout", [128, 128]); bdf = sb("bdf", [128, 128]); bdb = sb("bdb", [128, 128], BF16)
        ident = sb("ident", [128, 128]); identb = sb("identb", [128, 128], BF16)
        onesb = sb("onesb", [128, 128], BF16); onesf = sb("onesf", [128, 128])
        g1 = sb("g1s", [128, 8]); gm = sb("gms", [128, 8]); g2 = sb("g2s", [128, 8])
        qn = sb("qns", [128, 1]); kn = sb("kns", [128, 1]); hgn = sb("hgns", [128, 1])
        esink = sb("esink", [128, 8]); lb = sb("lb", [128, 4]); oml = sb("oml", [128, 4]); lbt = sb("lbt", [128, 4])
        tabs = sb("tabs", [33, 8]); ohs = rstd[0:33, 0:383]; tabrep = sb("tabrep", [33, 128]); cb = fo
        mch = sb("mch", [128, 128]); msm = sb("msm", [128, 128]); bmk = sb("bmk", [128, 128])
        pmk = sb("pmk_s", [128, 1]); rstp = sb("rstp_s", [128, 128]); rsts = sb("rsts_s", [128, 128]); selm = sb("selm_s", [128, 16]); pm = sb("pm_s", [128, 2])
        ps = [pst("ps%d" % i, [128, 512]) for i in range(7)]
        psb = pst("psb", [128, 1024], BF16)

        P = Prog(nc)

        def A(eng, reads, writes, fn, dma=None):
            writes = list(writes)
            for k in reads:
                k0 = k[0] if isinstance(k, (tuple, list)) else k
                if k0 in ("ps", "psb"):
                    writes.append(k)
            P.add(eng, fn, reads, writes, dma)

        small = [(ident, identd, "ident"),  (mch, mchd, "mch"), (msm, msd, "msm"), (bmk, bmd, "bmk"),
                 (rstp, rstpd, "rstp"), (rsts, rstsd, "rsts"), (selm, selmd, "selm"), (pm, pmd, "pm"),
                 (g1, g1d, "g1"), (gm, gmd, "gm"), (g2, g2d, "g2"), (bdf, bdd, "bdf"), (qn, qnd, "qn"), (kn, knd, "kn"), (hgn, hgnd, "hgn"),
                 (esink, sinkd, "esink"), (pmk, pmkd, "pmk"), (lb, lbad, "lb"), (lbt, lbbd, "lbt")]
        for (t, d, key) in small:
            A("sp", [], [key], lambda e, t=t, d=d: e.dma_start(out=t[:], in_=d), dma="c")
        A("sp", [], ["rstd"], lambda e: e.dma_start(out=ohs, in_=ohd), dma="c")
        A("dve", [], ["tabs"], lambda e: e.memset(tabs[:], NEG))
        A("sp", [], ["tabs"], lambda e: e.dma_start(out=tabs[0:32, :], in_=tabd), dma="c")
        A("dve", [], ["onesb"], lambda e: e.memset(onesb[:], 1.0))
        A("dve", [], ["onesf"], lambda e: e.memset(onesf[:], 1.0))
        A("dve", [], ["S"], lambda e: e.memset(S[:], 0.0))
        A("dve", [], ["Sb"], lambda e: e.memset(Sb[:], 0.0))
        A("act", ["bdf"], ["bdb"], lambda e: e.activation(out=bdb[:], in_=bdf[:], func=AF.Copy))
        A("dve", [], ["Vm"], lambda e: e.memset(Vm[:], 0.0))
        A("act", ["ident"], ["identb"], lambda e: e.activation(out=identb[:], in_=ident[:], func=AF.Copy))
        A("dve", ["lb", "lbt"], ["lbt"], lambda e: e.tensor_tensor(out=lbt[:], in0=lb[:], in1=lbt[:], op=ALU.subtract))
        A("act", ["lbt"], ["lb"], lambda e: e.activation(out=lb[:], in_=lbt[:], func=AF.Sigmoid))
        A("dve", ["lb"], ["oml"], lambda e: e.tensor_scalar(out=oml[:], in0=lb[:], scalar1=-1.0, scalar2=1.0, op0=ALU.mult, op1=ALU.add))
        A("act", ["esink"], ["esink"], lambda e: e.activation(out=esink[:], in_=esink[:], func=AF.Exp))
        ffn_g = [(c0, min(512, DFF - c0)) for c0 in range(0, DFF, 512)]
        win_g = ([(C_QA, 256), (C_QA + 256, 256), (C_KA, 128), (C_VA, 128)] + [(C_QR + h * 128, 128) for h in range(4)]
                 + [(C_FR + h * 128, 128) for h in range(4)] + [(C_IR, 512)] + [(C_GR + h * 128, 128) for h in range(4)]
                 + [(C_GA, 512), (C_GA + 512, 512), (C_GTR, 512), (C_GTR + 512, 512)])
        GT = dict(w1a=ffn_g, w3a=ffn_g, w1b=ffn_g, w3b=ffn_g, w2a=[(c0, 256) for c0 in range(0, D, 256)], w2b=[(c0, 256) for c0 in range(0, D, 256)],
                  win=win_g, wupa=[(0, 512), (512, 512)], wupr=[(0, 512), (512, 512)], wout=[(0, 512), (512, 512)])
        KCW = {k: v.shape[0] // 128 for k, v in wsrc.items()}

        def conv_jobs(name):
            return [(name, gi) for gi in range(len(GT[name]))]

        def emit_conv(job):
            name, gi = job
            c0, w = GT[name][gi]
            kcw = KCW[name]
            srcf = wsrc[name][:, c0:c0 + w].rearrange("(k p) n -> p k n", p=128)
            dstf = wbf[name].ap()[:, kcw * c0:kcw * (c0 + w)].rearrange("p (k n) -> p k n", k=kcw)
            A("pool", [], [("wb", name, gi)], lambda e, srcf=srcf, dstf=dstf: e.dma_start(out=dstf, in_=srcf), dma="cv_%s_%d" % (name, gi))

        j1, j3 = conv_jobs("w1a"), conv_jobs("w3a")
        emit_conv(j1[0])
        emit_conv(j3[0])
        pending_conv = []
        for a_, b_ in zip(j1[1:], j3[1:]):
            pending_conv += [a_, b_]
        pending_conv += conv_jobs("w2a")
        for nm in ("win", "wupa", "wupr", "wout"):
            pending_conv += conv_jobs(nm)
        jb1, jb3 = conv_jobs("w1b"), conv_jobs("w3b")
        for a_, b_ in zip(jb1, jb3):
            pending_conv += [a_, b_]
        pending_conv += conv_jobs("w2b")
        for h in range(8):
            A("dve", ["onesf", "tabs"], ["tabrep"], lambda e, h=h: e.tensor_scalar(out=tabrep[:], in0=onesf[0:33, :], scalar1=tabs[:, h:h + 1], scalar2=None, op0=ALU.mult))
            pbk = h % 2
            cbh, cbk = (fo, "fo") if h % 2 == 0 else (fs[4], "fs4")
            A("pe", ["tabrep", "rstd"], [("ps", pbk)], lambda e, pbk=pbk: e.matmul(ps[pbk][:, 0:383], lhsT=tabrep[:], rhs=ohs, start=True, stop=True))
            A("act", [("ps", pbk)], [cbk], lambda e, pbk=pbk, cbh=cbh: e.activation(out=cbh[:, 0:383], in_=ps[pbk][:, 0:383], func=AF.Copy))
            A("act", [cbk], [("scr", h)], lambda e, h=h, cbh=cbh: e.dma_start(out=scr.ap()[h], in_=cbh[:, 0:383]), dma="b1%d" % (h % 2))
        for h in range(8):
            so = bass.AP(tensor=scr, offset=h * 128 * 383 + 127, ap=[[382, 128], [1, 128]])
            sp_ = bass.AP(tensor=scr, offset=h * 128 * 383 + 255, ap=[[382, 128], [1, 128]])
            A("act", [("scr", h)], [("biasO", h)], lambda e, h=h, so=so: e.dma_start(out=biasO[:, h, :], in_=so), dma="b2")
            A("act", [("scr", h)], [("biasP", h)], lambda e, h=h, sp_=sp_: e.dma_start(out=biasP[:, h, :], in_=sp_), dma="b2")
        A("dve", ["biasO", "bmk"], ["biasS"], lambda e: e.tensor_tensor(out=biasS[:], in0=biasO[:], in1=bmk[:].unsqueeze(1).to_broadcast([128, 8, 128]), op=ALU.add))
        A("dve", ["biasP"], ["biasC"], lambda e: e.tensor_copy(out=biasC[:].rearrange("p h (b q) -> p h b q", b=16), in_=biasP[:, :, 0:8].unsqueeze(2).to_broadcast([128, 8, 16, 8])))

        wctr = [0]

        def wload(wname, kc, c0, n, nsl=None, k0=0):
            gi = GT[wname].index((c0, n))
            need = [k for k, jb in enumerate(pending_conv) if jb == (wname, gi)]
            if need:
                for jb in pending_conv[:need[-1] + 1]:
                    emit_conv(jb)
                del pending_conv[:need[-1] + 1]
            s = wctr[0] % (nsl or NSLOT)
            wctr[0] += 1
            view = wsl[s][:, 0:kc * n].rearrange("p (k n) -> p k n", k=kc)
            base = KCW[wname] * c0 + k0 * n
            src = wbf[wname].ap()[:, base:base + kc * n]
            A("pool", [("wb", wname, gi)], [("w", s)], lambda e: e.dma_start(out=wsl[s][:, 0:kc * n], in_=src), dma="w%d" % s)
            if pending_conv:
                emit_conv(pending_conv.pop(0))
            return s, view

        def rmsnorm(T, gain, gkey):
            for c in range(8):
                A("act", [("hT", c)], [("xnT", c)], lambda e, c=c: e.activation(out=xnT[:, c, 0:T], in_=hT[:, c, 0:T], func=AF.Square))
            for c in range(8):
                A("pe", [("xnT", c), "onesb"], [("ps", 0)], lambda e, c=c: e.matmul(ps[0][:, 0:T], lhsT=onesb[:], rhs=xnT[:, c, 0:T], start=(c == 0), stop=(c == 7)))
            A("act", [("ps", 0)], ["rstd"], lambda e: e.activation(out=rstd[:, 0:T], in_=ps[0][:, 0:T], func=AF.Ln, bias=EPS, scale=1.0 / D))
            A("act", ["rstd"], ["rstd"], lambda e: e.activation(out=rstd[:, 0:T], in_=rstd[:, 0:T], func=AF.Exp, scale=-0.5))
            for c in range(8):
                A("dve", [("hT", c), "rstd", gkey], [("xnT", c)], lambda e, c=c: e.scalar_tensor_tensor(out=xnT[:, c, 0:T], in0=hT[:, c, 0:T], scalar=gain[:, c:c + 1], in1=rstd[:, 0:T], op0=ALU.mult, op1=ALU.mult))

        def proj_fm(T, s, view, c0, m, psi, kc=8):
            for c in range(kc):
                A("pe", [("w", s), ("xnT", c)], [("ps", psi)], lambda e, c=c: e.matmul(ps[psi][0:m, 0:T], lhsT=view[:, c, c0:c0 + m], rhs=xnT[:, c, 0:T], start=(c == 0), stop=(c == kc - 1)))

        def ffn(T, w1, w3, w2):
            for j0 in range(0, NFF, 4):
                gw = min(4, NFF - j0)
                s1, v1 = wload(w1, 8, j0 * 128, gw * 128)
                s3, v3 = wload(w3, 8, j0 * 128, gw * 128)
                for jj in range(gw):
                    j = j0 + jj
                    proj_fm(T, s1, v1, jj * 128, 128, 0)
                    proj_fm(T, s3, v3, jj * 128, 128, 1)
                    f = fs[j % 2]
                    fk = "fs%d" % (j % 2)
                    A("act", [("ps", 0)], [fk], lambda e, f=f: e.activation(out=f[:, 0:T], in_=ps[0][:, 0:T], func=AF.Silu))
                    A("dve", [("ps", 1), fk], [("actT", j)], lambda e, f=f, j=j: e.tensor_tensor(out=actT[:, j, 0:T], in0=ps[1][:, 0:T], in1=f[:, 0:T], op=ALU.mult))
            HK = NFF // 2
            for n2 in range(4):
                s2a, v2a = wload(w2, HK, n2 * 256, 256, k0=0)
                s2b, v2b = wload(w2, HK, n2 * 256, 256, k0=HK)
                for nn in range(2):
                    n = n2 * 2 + nn
                    pb = n % 2
                    for j in range(NFF):
                        s2, v2, jj = (s2a, v2a, j) if j < HK else (s2b, v2b, j - HK)
                        A("pe", [("w", s2), ("actT", j)], [("ps", pb)], lambda e, j=j, jj=jj, v2=v2, pb=pb, nn=nn: e.matmul(ps[pb][:, 0:T], lhsT=v2[:, jj, nn * 128:(nn + 1) * 128], rhs=actT[:, j, 0:T], start=(j == 0), stop=(j == NFF - 1)))
                    A("dve", [("ps", pb), ("hT", n)], [("hT", n)], lambda e, n=n, pb=pb: e.scalar_tensor_tensor(out=hT[:, n, 0:T], in0=ps[pb][:, 0:T], scalar=0.5, in1=hT[:, n, 0:T], op0=ALU.mult, op1=ALU.add))

        def qknorm(T, psi, gain, gkey, dst, dkey):
            A("act", [("ps", psi)], ["sqb"], lambda e: e.activation(out=sqb[:, 0:T], in_=ps[psi][:, 0:T], func=AF.Square))
            A("pe", ["sqb", "bdb"], [("ps", 6)], lambda e: e.matmul(ps[6][:, 0:T], lhsT=bdb[:], rhs=sqb[:, 0:T], start=True, stop=True))
            A("act", [("ps", 6)], ["fs4"], lambda e: e.activation(out=fs[4][:, 0:T], in_=ps[6][:, 0:T], func=AF.Ln, bias=EPS, scale=1.0 / 64))
            A("act", ["fs4"], ["fs4"], lambda e: e.activation(out=fs[4][:, 0:T], in_=fs[4][:, 0:T], func=AF.Exp, scale=-0.5))
            A("dve", [("ps", psi), "fs4", gkey], [dkey], lambda e: e.scalar_tensor_tensor(out=dst, in0=ps[psi][:, 0:T], scalar=gain[:, 0:1], in1=fs[4][:, 0:T], op0=ALU.mult, op1=ALU.mult))

        import os
        STAGE = float(os.environ.get('STAGE', '9'))
        SST = float(os.environ.get('SSTAGE', '9'))
        xctr = [0]

        def block(tiles, first_tile_of_seq, out_rows, xsrc, ydst, is_sample, last_prompt, prefix=False, prefix_last=False, halo_mask=False, after_x=None):
            nt = tiles
            T = nt * 128
            for i in range(nt):
                sl = xctr[0] % 2
                xctr[0] += 1
                A("sp", [], [("xst", sl)], lambda e, i=i, sl=sl: e.dma_start(out=xst[sl][:], in_=xsrc[i * 128:(i + 1) * 128, :]), dma="x%d" % sl)
                for c in range(8):
                    A("pe", [("xst", sl), "ident"], [("ps", 2 + c // 4)], lambda e, c=c, sl=sl: e.transpose(out=ps[2 + c // 4][:, (c % 4) * 128:(c % 4 + 1) * 128], in_=xst[sl][:, c * 128:(c + 1) * 128], identity=ident[:]))
                A("act", [("ps", 2)], [("hT", c_, i) for c_ in range(4)], lambda e, i=i: e.activation(out=hT[:, 0:4, i * 128:(i + 1) * 128], in_=ps[2][:, :].rearrange("p (c t) -> p c t", c=4), func=AF.Copy))
                A("dve", [("ps", 3)], [("hT", c_, i) for c_ in range(4, 8)], lambda e, i=i: e.tensor_copy(out=hT[:, 4:8, i * 128:(i + 1) * 128], in_=ps[3][:, :].rearrange("p (c t) -> p c t", c=4)))
            if after_x is not None:
                after_x()
            rmsnorm(T, g1, "g1")
            ffn(T, "w1a", "w3a", "w2a")
            if STAGE <= 1:
                return
            rmsnorm(T, gm, "gm")
            if is_sample and SST <= 1.1:
                return
            do_kv = (not prefix) or prefix_last
            if do_kv:
                sv_, vv = wload("win", 8, C_VA, 128)
            for i in (range(nt) if do_kv else []):
                for c in range(8):
                    A("pe", [("w", sv_), ("xnT", c)], [("ps", 2)], lambda e, c=c, i=i: e.matmul(ps[2][:, 0:128], lhsT=xnT[:, c, i * 128:(i + 1) * 128], rhs=vv[:, c, 0:128], start=(c == 0), stop=(c == 7)))
                for dup in range(2):
                    A("act", [("ps", 2)], [("vdup", i)], lambda e, i=i, dup=dup: e.activation(out=vdup[:, i, :, dup * 64:(dup + 1) * 64], in_=ps[2][:, 0:128].rearrange("p (g d) -> p g d", g=2), func=AF.Copy))
                if is_sample and SST <= 1.2:
                    return
                if i == nt - 1 and (is_sample or last_prompt):
                    A("act", [("ps", 2)], ["vout"], lambda e: e.activation(out=vout[:], in_=ps[2][:, 0:128], func=AF.Copy))
                    if is_sample:
                        NB = int(os.environ.get('NB', '16'))
                        for b in range(NB):
                            A("sp", ["vout"], [], lambda e, b=b: e.dma_start(out=sv[b, 120:128, :], in_=vout[b * 8:(b + 1) * 8, :]), dma="o1")
                    else:
                        A("sp", ["vout"], [], lambda e: e.dma_start(out=pv, in_=vout[:]), dma="o1")
            if is_sample and SST <= 1.3:
                return
            want_k = is_sample or last_prompt
            if do_kv:
                sk_, kv = wload("win", 8, C_KA, 128)
            for g in (range(2) if do_kv else []):
                pad = Vm[:, 8 * g:8 * g + 8, :]
                for dup in range(2):
                    A("dve", [("w", sk_)], ["Vm"], lambda e, g=g, dup=dup, pad=pad: e.tensor_copy(out=pad[:, :, dup * 64:(dup + 1) * 64], in_=kv[:, :, g * 64:(g + 1) * 64]))
                for c in range(8):
                    A("pe", ["Vm", ("xnT", c)], [("ps", g)], lambda e, c=c, g=g, pad=pad: e.matmul(ps[g][:, 0:T], lhsT=pad[:, c, :], rhs=xnT[:, c, 0:T], start=(c == 0), stop=(c == 7)))
                qknorm(T, g, kn, "kn", kT[:, g, 0:T], ("kT", g))
                if want_k:
                    t0 = (nt - 1) * 128
                    A("dve", [("ps", g), "fs4", "kn"], [("fs3", g)], lambda e, g=g, t0=t0: e.scalar_tensor_tensor(out=fs[3][:, g * 128:(g + 1) * 128], in0=ps[g][:, t0:t0 + 128], scalar=kn[:, 0:1], in1=fs[4][:, t0:t0 + 128], op0=ALU.mult, op1=ALU.mult))
                    A("pe", [("fs3", g), "ident"], [("ps", 2)], lambda e, g=g: e.transpose(out=ps[2][:, g * 128:(g + 1) * 128], in_=fs[3][:, g * 128:(g + 1) * 128], identity=ident[:]))
                    A("act", [("ps", 2)], [("kout", g)], lambda e, g=g: e.activation(out=kout[:, g * 64:(g + 1) * 64], in_=ps[2][:, g * 128:g * 128 + 64], func=AF.Copy))
            if is_sample and SST <= 1.4:
                return
            if want_k:
                if is_sample:
                    for b in range(16):
                        A("sp", ["kout"], [], lambda e, b=b: e.dma_start(out=sk[b, 120:128, :], in_=kout[b * 8:(b + 1) * 8, :]), dma="o1")
                else:
                    A("sp", ["kout"], [], lambda e: e.dma_start(out=pk, in_=kout[:]), dma="o1")
            if STAGE <= 1.5:
                return
            for g in (range(2) if not prefix else []):
                sq_, qv = wload("win", 8, C_QA + g * 256, 256)
                for c2 in range(2):
                    for c in range(8):
                        A("pe", [("w", sq_), ("xnT", c)], [("ps", c2)], lambda e, c=c, c2=c2, qv=qv: e.matmul(ps[c2][:, 0:T], lhsT=qv[:, c, c2 * 128:(c2 + 1) * 128], rhs=xnT[:, c, 0:T], start=(c == 0), stop=(c == 7)))
                    qknorm(T, c2, qn, "qn", qt[:, 0:T], "qt")
                    for par in range(2):
                        hl = 2 * c2 + par
                        A("dve", ["qt", "pm"], [("qT", hl)], lambda e, hl=hl, par=par: e.tensor_scalar(out=qT[:, hl, 0:T], in0=qt[:, 0:T], scalar1=pm[:, par:par + 1], scalar2=None, op0=ALU.mult))
                b_own = biasS if is_sample else biasO
                b_prev = biasC if is_sample else biasP

                def att_set(sid):
                    if sid == 0:
                        return dict(sc=(2, 3), po=4, pd=ps[5], pdk=("ps", 5), P=[pT[0][:, :], pT[1][:, :]], Pk=[["pT0"], ["pT1"]],
                                    SB=[sbf[0][:, :], sbf[1][:, :]], SBk=[["sbf0"], ["sbf1"]], RD=rd[:, :], RDk=["rd"])
                    f32v = lambda c: actT[:, c:c + 2, :].rearrange("p c t -> p (c t)").bitcast(F32)
                    return dict(sc=(0, 1), po=6, pd=psb[:, :].bitcast(F32), pdk="psb", P=[actT[:, 0, :], actT[:, 1, :]], Pk=[[("actT", 0)], [("actT", 1)]],
                                SB=[f32v(2), f32v(4)], SBk=[[("actT", 2), ("actT", 3)], [("actT", 4), ("actT", 5)]], RD=f32v(6), RDk=[("actT", 6), ("actT", 7)])

                def att_scores(i, W, g):
                    tsl = slice(i * 128, (i + 1) * 128)
                    has_prev = is_sample or not (first_tile_of_seq and i == 0)
                    so, sp2 = W["sc"]
                    A("pe", [("kT", g), "qT"], [("ps", so)], lambda e: e.matmul(ps[so][:, :], lhsT=kT[:, g, tsl], rhs=qT[:, :, tsl], start=True, stop=True))
                    A("dve", [("ps", so), "biasO", "biasS"], W["SBk"][0], lambda e: e.scalar_tensor_tensor(out=W["SB"][0], in0=ps[so][:, :], scalar=0.125, in1=b_own[:, 4 * g:4 * g + 4, :].rearrange("p h q -> p (h q)"), op0=ALU.mult, op1=ALU.add))
                    A("act", W["SBk"][0], W["Pk"][0], lambda e: e.activation(out=W["P"][0], in_=W["SB"][0], func=AF.Exp))
                    if has_prev:
                        if is_sample:
                            for b in range(16):
                                A("pe", ["kcT", "qT"], [("ps", sp2)], lambda e, b=b: e.matmul(ps[sp2][:, :].rearrange("p (h b q) -> p h b q", h=4, b=16)[:, :, b, :], lhsT=kcT[:, g, b, :], rhs=qT[:, :, i * 128 + b * 8:i * 128 + b * 8 + 8], start=True, stop=True))
                        elif i > 0:
                            psl = slice((i - 1) * 128, i * 128)
                            A("pe", [("kT", g), "qT"], [("ps", sp2)], lambda e: e.matmul(ps[sp2][:, :], lhsT=kT[:, g, psl], rhs=qT[:, :, tsl], start=True, stop=True))
                        else:
                            A("pe", [("khalo", g), "qT"], [("ps", sp2)], lambda e: e.matmul(ps[sp2][:, :], lhsT=khalo[:, g, :], rhs=qT[:, :, tsl], start=True, stop=True))
                        A("dve", [("ps", sp2), "biasP", "biasC"], W["SBk"][1], lambda e: e.scalar_tensor_tensor(out=W["SB"][1], in0=ps[sp2][:, :], scalar=0.125, in1=b_prev[:, 4 * g:4 * g + 4, :].rearrange("p h q -> p (h q)"), op0=ALU.mult, op1=ALU.add))
                        if halo_mask and i == 0:
                            A("dve", W["SBk"][1] + ["pmk"], W["SBk"][1], lambda e: e.tensor_scalar(out=W["SB"][1], in0=W["SB"][1], scalar1=pmk[:, 0:1], scalar2=None, op0=ALU.add))
                        A("act", W["SBk"][1], W["Pk"][1], lambda e: e.activation(out=W["P"][1], in_=W["SB"][1], func=AF.Exp))

                def att_pv(i, W, g):
                    tsl = slice(i * 128, (i + 1) * 128)
                    has_prev = is_sample or not (first_tile_of_seq and i == 0)
                    po, pd, pdk = W["po"], W["pd"], W["pdk"]
                    P0, P1 = W["P"]
                    A("pe", [("vdup", i)] + W["Pk"][0], [("ps", po)], lambda e: e.matmul(ps[po][:, :], lhsT=vdup[:, i, g, :], rhs=P0, start=True, stop=not has_prev))
                    A("pe", ["onesb"] + W["Pk"][0], [pdk], lambda e: e.matmul(pd[:, :], lhsT=onesb[:], rhs=P0, start=True, stop=not has_prev))
                    if has_prev:
                        if is_sample:
                            for b in range(16):
                                A("pe", ["vcdup"] + W["Pk"][1], [("ps", po)], lambda e, b=b: e.matmul(ps[po][:, :].rearrange("p (h b q) -> p h b q", h=4, b=16)[:, :, b, :], lhsT=vcdup[:, b, g, :], rhs=P1.rearrange("p (h b q) -> p h b q", h=4, b=16)[:, :, b, :], start=False, stop=(b == 15)))
                        elif i > 0:
                            A("pe", [("vdup", i - 1)] + W["Pk"][1], [("ps", po)], lambda e: e.matmul(ps[po][:, :], lhsT=vdup[:, i - 1, g, :], rhs=P1, start=False, stop=True))
                        else:
                            A("pe", [("vhalo", g)] + W["Pk"][1], [("ps", po)], lambda e: e.matmul(ps[po][:, :], lhsT=vhalo[:, g, :], rhs=P1, start=False, stop=True))
                        A("pe", ["onesb"] + W["Pk"][1], [pdk], lambda e: e.matmul(pd[:, :], lhsT=onesb[:], rhs=P1, start=False, stop=True))
                    RD = W["RD"]
                    for hl in range(4):
                        h = 4 * g + hl
                        A("act", [pdk, "esink"], W["RDk"], lambda e, hl=hl, h=h: e.activation(out=RD[:, hl * 128:(hl + 1) * 128], in_=pd[:, hl * 128:(hl + 1) * 128], func=AF.Ln, bias=esink[:, h:h + 1]))
                    A("act", W["RDk"], W["RDk"], lambda e: e.activation(out=RD, in_=RD, func=AF.Exp, scale=-1.0))
                    for hl in range(4):
                        h = 4 * g + hl
                        hp = (h % 2) * 64
                        A("dve", [("ps", po)] + W["RDk"], [("attT", h // 2, h % 2, i)], lambda e, hl=hl, h=h, hp=hp: e.tensor_tensor(out=attT[hp:hp + 64, h // 2, tsl], in0=ps[po][hp:hp + 64, hl * 128:(hl + 1) * 128], in1=RD[hp:hp + 64, hl * 128:(hl + 1) * 128], op=ALU.mult))

                W2 = [att_set(0), att_set(1)]
                for k in range(nt + 1):
                    if k < nt:
                        att_scores(k, W2[k % 2], g)
                    if k >= 1:
                        att_pv(k - 1, W2[(k - 1) % 2], g)
            if not is_sample and do_kv:
                A("act", ["kT"], ["khalo"], lambda e: e.activation(out=khalo[:, :, :], in_=kT[:, :, (nt - 1) * 128:nt * 128], func=AF.Copy))
                A("dve", [("vdup", nt - 1)], ["vhalo"], lambda e: e.tensor_copy(out=vhalo[:, :, :], in_=vdup[:, nt - 1, :, :]))
            if STAGE <= 2:
                return
            si_, iv = wload("win", 8, C_IR, 512)
            for i in range(nt):
                for c in range(8):
                    A("pe", [("w", si_), ("xnT", c)], [("ps", 2)], lambda e, c=c, i=i: e.matmul(ps[2][:, :], lhsT=xnT[:, c, i * 128:(i + 1) * 128], rhs=iv[:, c, :], start=(c == 0), stop=(c == 7)))
                A("act", [("ps", 2)], [("vr", i)], lambda e, i=i: e.activation(out=vr[:, i, :], in_=ps[2][:, :], func=AF.Copy))
            rst = rsts if is_sample else rstp
            msk = msm if is_sample else mch

            def hg_set(sid):
                if sid == 0:
                    F = [fs[0], fs[1], fs[2], fs[3]]
                    return dict(F=[f[:, 0:T] for f in F], Fk=[["fs0"], ["fs1"], ["fs2"], ["fs3"]], FO=fo[:, 0:T], FOk=["fo"],
                                QT=qt[:, 0:T], QTk=["qt"], KT=kt[:, 0:T], KTk=["kt"], KHT=kht[:, 0:T], KHTk=["kht"],
                                AT=AT[:, :], ATk=["AT"], KH=[khtok[0][:, :], khtok[1][:, :]], KHk=[[("khtok", 0)], [("khtok", 1)]],
                                po=4, pu=5)
                def f32v(c):
                    return actT[:, c:c + 2, :].rearrange("p c t -> p (c t)").bitcast(F32)[:, 0:T]
                return dict(F=[f32v(8), f32v(10), f32v(12), f32v(14)], Fk=[[("actT", 8), ("actT", 9)], [("actT", 10), ("actT", 11)], [("actT", 12), ("actT", 13)], [("actT", 14), ("actT", 15)]],
                            FO=f32v(16), FOk=[("actT", 16), ("actT", 17)],
                            QT=actT[:, 18, 0:T], QTk=[("actT", 18)], KT=actT[:, 19, 0:T], KTk=[("actT", 19)], KHT=actT[:, 20, 0:T], KHTk=[("actT", 20)],
                            AT=actT[:, 21, 0:128], ATk=[("actT", 21, 0)], KH=[actT[:, 21, 128:256], actT[:, 21, 256:384]], KHk=[[("actT", 21, 1)], [("actT", 21, 2)]],
                            po=6, pu=3)

            def hg_stage_a(hh, B):
                F, Fk = B["F"], B["Fk"]
                sf_, fv = wload("win", 8, C_FR + hh * 128, 128)
                if not prefix:
                    sqr_, qrv = wload("win", 8, C_QR + hh * 128, 128)
                pf = B["po"] if not is_sample else 0
                pq = B["pu"] if not is_sample else 1
                proj_fm(T, sf_, fv, 0, 128, pf)
                A("act", [("ps", pf)], Fk[1], lambda e: e.activation(out=F[1], in_=ps[pf][:, 0:T], func=AF.Sigmoid))
                A("dve", Fk[1] + ["lb", "oml"], Fk[1], lambda e, hh=hh: e.tensor_scalar(out=F[1], in0=F[1], scalar1=oml[:, hh:hh + 1], scalar2=lb[:, hh:hh + 1], op0=ALU.mult, op1=ALU.add))
                yield
                A("act", Fk[1], Fk[2], lambda e: e.activation(out=F[2], in_=F[1], func=AF.Ln))
                A("dve", Fk[1], Fk[1], lambda e: e.tensor_scalar(out=F[1], in0=F[1], scalar1=-1.0, scalar2=1.0, op0=ALU.mult, op1=ALU.add))
                for i in range(nt):
                    A("dve", Fk[2] + ["rstp", "rsts"], Fk[3], lambda e, i=i: e.tensor_tensor_scan(out=F[3][:, i * 128:(i + 1) * 128], data0=rst[:, :], data1=F[2][:, i * 128:(i + 1) * 128], initial=0.0, op0=ALU.mult, op1=ALU.add))
                yield
                A("act", Fk[3], Fk[2], lambda e: e.activation(out=F[2], in_=F[3], func=AF.Exp))
                A("act", Fk[3], Fk[0], lambda e: e.activation(out=F[0], in_=F[3], func=AF.Exp, scale=-1.0))
                yield
                if not prefix:
                    proj_fm(T, sqr_, qrv, 0, 128, pq)
                    A("act", [("ps", pq)], Fk[3], lambda e: e.activation(out=F[3], in_=ps[pq][:, 0:T], func=AF.Silu))
                    A("dve", Fk[3] + Fk[2], B["QTk"], lambda e: e.tensor_tensor(out=B["QT"], in0=F[3], in1=F[2], op=ALU.mult))
                A("dve", Fk[1] + Fk[0], Fk[0], lambda e: e.tensor_tensor(out=F[0], in0=F[1], in1=F[0], op=ALU.mult))
                yield
                if not prefix:
                    A("act", Fk[0], B["KTk"], lambda e: e.activation(out=B["KT"], in_=F[0], func=AF.Copy))
                glen = 8 if is_sample else 64
                for c0 in range(0, T, glen):
                    A("dve", Fk[0] + Fk[2], B["KHTk"], lambda e, c0=c0: e.tensor_scalar(out=B["KHT"][:, c0:c0 + glen], in0=F[0][:, c0:c0 + glen], scalar1=F[2][:, c0 + glen - 1:c0 + glen], scalar2=None, op0=ALU.mult))

            def hg_tile(hh, i, B):
                F, Fk = B["F"], B["Fk"]
                po, pu = B["po"], B["pu"]
                vsl = slice(hh * 128, (hh + 1) * 128)
                tsl = slice(i * 128, (i + 1) * 128)
                A("pe", B["KHTk"] + ["identb"], ["psb"], lambda e: e.transpose(out=psb[:, 0:128], in_=B["KHT"][:, tsl], identity=identb[:]))
                if not prefix:
                    A("pe", B["KTk"] + B["QTk"], [("ps", pu)], lambda e: e.matmul(ps[pu][:, 0:128], lhsT=B["KT"][:, tsl], rhs=B["QT"][:, tsl], start=True, stop=True))
                    A("dve", [("ps", pu), "mch", "msm"], B["ATk"], lambda e: e.tensor_tensor(out=B["AT"], in0=ps[pu][:, 0:128], in1=msk[:, :], op=ALU.mult))
                    A("pe", [("vr", i)] + B["ATk"], [("ps", po)], lambda e: e.matmul(ps[po][:, 0:128], lhsT=vr[:, i, vsl], rhs=B["AT"], start=True, stop=False))
                if not is_sample:
                    for ch in range(2):
                        A("dve", ["psb", "pm"], B["KHk"][ch], lambda e, ch=ch: e.tensor_scalar(out=B["KH"][ch], in0=psb[:, 0:128], scalar1=pm[:, ch:ch + 1], scalar2=None, op0=ALU.mult))
                    for ch in range(2):
                        cs = i * 128 + ch * 64
                        if not prefix:
                            A("pe", [("Sb", hh)] + B["QTk"], [("ps", po)], lambda e, cs=cs, ch=ch: e.matmul(ps[po][:, ch * 64:(ch + 1) * 64], lhsT=Sb[:, hh, :], rhs=B["QT"][:, cs:cs + 64], start=False, stop=(ch == 1)))
                        A("pe", B["KHk"][ch] + [("vr", i)], [("ps", pu)], lambda e, ch=ch: e.matmul(ps[pu][:, 128:256], lhsT=B["KH"][ch], rhs=vr[:, i, vsl], start=True, stop=True))
                        A("dve", [("ps", pu), ("S", hh)] + Fk[2], [("S", hh)], lambda e, cs=cs: e.scalar_tensor_tensor(out=S[:, hh, :], in0=S[:, hh, :], scalar=F[2][:, cs + 63:cs + 64], in1=ps[pu][:, 128:256], op0=ALU.mult, op1=ALU.add))
                        if (not prefix) or (prefix_last and i == nt - 1 and ch == 1):
                            A("act", [("S", hh)], [("Sb", hh)], lambda e: e.activation(out=Sb[:, hh, :], in_=S[:, hh, :], func=AF.Copy))
                else:
                    A("act", ["psb"], B["KHk"][0], lambda e: e.activation(out=B["KH"][0], in_=psb[:, 0:128], func=AF.Copy))
                    for b in range(16):
                        A("pe", B["S0bk"] + B["QTk"], [("ps", po)], lambda e, b=b: e.matmul(ps[po][:, b * 8:(b + 1) * 8], lhsT=B["S0b"][:, b, :], rhs=B["QT"][:, i * 128 + b * 8:i * 128 + b * 8 + 8], start=False, stop=(b == 15)))
                    for b in range(16):
                        A("dve", [("vr", i), "selm"], [("Vm", b)], lambda e, b=b: e.tensor_scalar(out=Vm[:, b, :], in0=vr[:, i, vsl], scalar1=selm[:, b:b + 1], scalar2=None, op0=ALU.mult))
                    for q4 in range(4):
                        A("pe", B["KHk"][0] + ["Vm"], [("ps", pu)], lambda e, q4=q4: e.matmul(ps[pu][:, :], lhsT=B["KH"][0], rhs=Vm[:, q4 * 4:(q4 + 1) * 4, :], start=True, stop=True))
                        for bb in range(4):
                            b = q4 * 4 + bb
                            col = i * 128 + b * 8 + 7
                            A("dve", [("ps", pu)] + B["S0k"] + Fk[2], B["S0k"], lambda e, b=b, bb=bb, col=col: e.scalar_tensor_tensor(out=B["S0"][:, b, :], in0=B["S0"][:, b, :], scalar=F[2][:, col:col + 1], in1=ps[pu][:, bb * 128:(bb + 1) * 128], op0=ALU.mult, op1=ALU.add))
                    A("sp", B["S0k"], [], lambda e: e.dma_start(out=sS[:, hh, :, :].rearrange("b k v -> k b v"), in_=B["S0"]), dma=B["S0g"])
                if not prefix:
                    A("act", [("ps", po)], B["FOk"], lambda e: e.activation(out=B["FO"][:, tsl], in_=ps[po][:, 0:128], func=AF.Copy))

            def hg_stage_c(hh, B):
                F, Fk = B["F"], B["Fk"]
                sg_, gv = wload("win", 8, C_GR + hh * 128, 128)
                pc, pg = B["po"], B["pu"]
                A("act", B["FOk"], B["KTk"], lambda e: e.activation(out=B["KT"], in_=B["FO"], func=AF.Square))
                A("pe", B["KTk"] + ["onesb"], [("ps", pc)], lambda e: e.matmul(ps[pc][:, 0:T], lhsT=onesb[:], rhs=B["KT"], start=True, stop=True))
                yield
                A("act", [("ps", pc)], Fk[1], lambda e: e.activation(out=F[1], in_=ps[pc][:, 0:T], func=AF.Ln, bias=EPS, scale=1.0 / 128))
                A("act", Fk[1], Fk[1], lambda e: e.activation(out=F[1], in_=F[1], func=AF.Exp, scale=-0.5))
                proj_fm(T, sg_, gv, 0, 128, pg)
                yield
                A("act", [("ps", pg)], Fk[3], lambda e: e.activation(out=F[3], in_=ps[pg][:, 0:T], func=AF.Silu))
                A("dve", B["FOk"] + Fk[1] + ["hgn"], B["FOk"], lambda e: e.scalar_tensor_tensor(out=B["FO"], in0=B["FO"], scalar=hgn[:, 0:1], in1=F[1], op0=ALU.mult, op1=ALU.mult))
                A("dve", B["FOk"] + Fk[3], [("orT", hh)], lambda e: e.tensor_tensor(out=orT[:, hh, 0:T], in0=B["FO"], in1=F[3], op=ALU.mult))

            def hg_prefix_a(hh, B):
                F, Fk = B["F"], B["Fk"]
                sf_, fv = wload("win", 8, C_FR + hh * 128, 128)
                pf = B["po"]
                proj_fm(T, sf_, fv, 0, 128, pf)
                A("act", [("ps", pf)], Fk[1], lambda e: e.activation(out=F[1], in_=ps[pf][:, 0:T], func=AF.Sigmoid))
                A("dve", Fk[1] + ["lb", "oml"], Fk[1], lambda e: e.tensor_scalar(out=F[1], in0=F[1], scalar1=oml[:, hh:hh + 1], scalar2=lb[:, hh:hh + 1], op0=ALU.mult, op1=ALU.add))
                yield
                A("act", Fk[1], Fk[2], lambda e: e.activation(out=F[2], in_=F[1], func=AF.Ln))
                A("dve", Fk[1], Fk[1], lambda e: e.tensor_scalar(out=F[1], in0=F[1], scalar1=-1.0, scalar2=1.0, op0=ALU.mult, op1=ALU.add))
                A("dve", Fk[2] + [("actT", 0), ("actT", 1)], Fk[3], lambda e: e.tensor_tensor_scan(out=F[3], data0=onesT[:, 0:T], data1=F[2], initial=0.0, op0=ALU.mult, op1=ALU.add))
                yield
                A("act", Fk[3], Fk[0], lambda e: e.activation(out=F[0], in_=F[3], func=AF.Exp, scale=-1.0, bias=F[3][:, T - 1:T]))
                A("act", Fk[3], Fk[2], lambda e: e.activation(out=F[2][:, 0:1], in_=F[3][:, T - 1:T], func=AF.Exp))
                A("dve", Fk[1] + Fk[0], B["KHTk"], lambda e: e.tensor_tensor(out=B["KHT"], in0=F[1], in1=F[0], op=ALU.mult))

            def hg_prefix_b(hh, B):
                F, Fk = B["F"], B["Fk"]
                pu = B["pu"]
                vsl = slice(hh * 128, (hh + 1) * 128)
                for i in range(nt):
                    tsl = slice(i * 128, (i + 1) * 128)
                    A("pe", B["KHTk"] + ["identb"], ["psb"], lambda e, tsl=tsl: e.transpose(out=psb[:, 0:128], in_=B["KHT"][:, tsl], identity=identb[:]))
                    kh, khk = B["KH"][i % 2], B["KHk"][i % 2]
                    if i % 2 == 0:
                        A("act", ["psb"], khk, lambda e, kh=kh: e.activation(out=kh, in_=psb[:, 0:128], func=AF.Copy))
                    else:
                        A("dve", ["psb"], khk, lambda e, kh=kh: e.tensor_copy(out=kh, in_=psb[:, 0:128]))
                    A("pe", khk + [("vr", i)], [("ps", pu)], lambda e, i=i, kh=kh: e.matmul(ps[pu][:, 0:128], lhsT=kh, rhs=vr[:, i, vsl], start=(i == 0), stop=(i == nt - 1)))
                A("dve", [("ps", pu), ("S", hh)] + Fk[2], [("S", hh)], lambda e: e.scalar_tensor_tensor(out=S[:, hh, :], in0=S[:, hh, :], scalar=F[2][:, 0:1], in1=ps[pu][:, 0:128], op0=ALU.mult, op1=ALU.add))
                if prefix_last:
                    A("act", [("S", hh)], [("Sb", hh)], lambda e: e.activation(out=Sb[:, hh, :], in_=S[:, hh, :], func=AF.Copy))

            if prefix:
                onesT = actT[:, 0:2, :].rearrange("p c t -> p (c t)").bitcast(F32)
                A("dve", [], [("actT", 0), ("actT", 1)], lambda e: e.memset(onesT[:, 0:T], 1.0))
                B0, B1 = hg_set(0), hg_set(1)
                def zipg(*gens):
                    gens = list(gens)
                    while gens:
                        for gn in list(gens):
                            try:
                                next(gn)
                            except StopIteration:
                                gens.remove(gn)

                for h0 in (0, 2):
                    zipg(hg_prefix_a(h0, B0), hg_prefix_a(h0 + 1, B1))
                    hg_prefix_b(h0, B0)
                    hg_prefix_b(h0 + 1, B1)
            elif is_sample:
                B0 = hg_set(0)
                S0alt = actT[:, 8:16, :].rearrange("p c t -> p (c t)").bitcast(F32).rearrange("p (b v) -> p b v", b=16)
                S0balt = actT[:, 16:20, :].rearrange("p c t -> p (c t)").rearrange("p (b v) -> p b v", b=16)
                Bh = [dict(B0, S0=S0[:, :, :], S0k=["S0"], S0b=S0b[:, :, :], S0bk=["S0b"], S0g="s0"),
                      dict(B0, S0=S0alt, S0k=[("actT", c) for c in range(8, 16)], S0b=S0balt, S0bk=[("actT", c) for c in range(16, 20)], S0g="s0b")]

                def s0_load(hh):
                    Bx = Bh[hh % 2]
                    A("sp", [], Bx["S0k"], lambda e: e.dma_start(out=Bx["S0"], in_=s0[:, hh, :, :].rearrange("b k v -> k b v")), dma=Bx["S0g"])
                    A("act", Bx["S0k"], Bx["S0bk"], lambda e: e.activation(out=Bx["S0b"], in_=Bx["S0"], func=AF.Copy))

                s0_load(0)
                for hh in range(4):
                    for _ in hg_stage_a(hh, Bh[hh % 2]):
                        pass
                    if hh + 1 < 4:
                        s0_load(hh + 1)
                    for i in range(nt):
                        hg_tile(hh, i, Bh[hh % 2])
                    for _ in hg_stage_c(hh, Bh[hh % 2]):
                        pass
            else:
                B0, B1 = hg_set(0), hg_set(1)
                def zipgen(*gens):
                    gens = list(gens)
                    while gens:
                        for gn in list(gens):
                            try:
                                next(gn)
                            except StopIteration:
                                gens.remove(gn)

                for h0 in (0, 2):
                    zipgen(hg_stage_a(h0, B0), hg_stage_a(h0 + 1, B1))
                    for i in range(nt):
                        hg_tile(h0, i, B0)
                        hg_tile(h0 + 1, i, B1)
                    zipgen(hg_stage_c(h0, B0), hg_stage_c(h0 + 1, B1))
            if STAGE <= 3 or prefix:
                return
            for n4 in range(2):
                sga_, gav = wload("win", 8, C_GA + n4 * 512, 512)
                sA_, wA4 = wload("wupa", 4, n4 * 512, 512)
                for nn in range(4):
                    n = n4 * 4 + nn
                    p2 = 2 * (nn % 2)
                    nsl = slice(nn * 128, (nn + 1) * 128)
                    fsg, fsgk = fs[p2], "fs%d" % p2
                    for c in range(4):
                        A("pe", [("w", sA_), ("attT", c)], [("ps", p2)], lambda e, c=c, nsl=nsl, wA4=wA4, p2=p2: e.matmul(ps[p2][:, 0:T], lhsT=wA4[:, c, nsl], rhs=attT[:, c, 0:T], start=(c == 0), stop=(c == 3)))
                    proj_fm(T, sga_, gav, nn * 128, 128, p2 + 1)
                    A("act", [("ps", p2 + 1)], [fsgk], lambda e, fsg=fsg, p2=p2: e.activation(out=fsg[:, 0:T], in_=ps[p2 + 1][:, 0:T], func=AF.Sigmoid))
                    A("dve", [("ps", p2), fsgk], m1k(nn), lambda e, nn=nn, fsg=fsg, p2=p2: e.tensor_tensor(out=m1v(nn)[:, 0:T], in0=ps[p2][:, 0:T], in1=fsg[:, 0:T], op=ALU.mult))
                sgr_, grv = wload("win", 8, C_GTR + n4 * 512, 512)
                sR_, wR4 = wload("wupr", 4, n4 * 512, 512)
                for nn in range(4):
                    n = n4 * 4 + nn
                    p2 = 2 * (nn % 2)
                    nsl = slice(nn * 128, (nn + 1) * 128)
                    fsg, fsgk = fs[p2 + 1], "fs%d" % (p2 + 1)
                    for c in range(4):
                        A("pe", [("w", sR_), ("orT", c)], [("ps", p2)], lambda e, c=c, nsl=nsl, wR4=wR4, p2=p2: e.matmul(ps[p2][:, 0:T], lhsT=wR4[:, c, nsl], rhs=orT[:, c, 0:T], start=(c == 0), stop=(c == 3)))
                    proj_fm(T, sgr_, grv, nn * 128, 128, p2 + 1)
                    A("act", [("ps", p2 + 1)], [fsgk], lambda e, fsg=fsg, p2=p2: e.activation(out=fsg[:, 0:T], in_=ps[p2 + 1][:, 0:T], func=AF.Sigmoid))
                    A("dve", [("ps", p2), fsgk], ["fs4"], lambda e, fsg=fsg, p2=p2: e.tensor_tensor(out=fs[4][:, 0:T], in0=ps[p2][:, 0:T], in1=fsg[:, 0:T], op=ALU.mult))
                    A("dve", m1k(nn) + ["fs4"], [("actT", n)], lambda e, n=n, nn=nn: e.tensor_tensor(out=actT[:, n, 0:T], in0=m1v(nn)[:, 0:T], in1=fs[4][:, 0:T], op=ALU.add))
            for n4 in range(2):
                so_, ov = wload("wout", 8, n4 * 512, 512)
                for nn in range(4):
                    n = n4 * 4 + nn
                    pb = n % 2
                    for c in range(8):
                        A("pe", [("w", so_), ("actT", c)], [("ps", pb)], lambda e, c=c, nn=nn, pb=pb, ov=ov: e.matmul(ps[pb][:, 0:T], lhsT=ov[:, c, nn * 128:(nn + 1) * 128], rhs=actT[:, c, 0:T], start=(c == 0), stop=(c == 7)))
                    A("dve", [("ps", pb), ("hT", n)], [("hT", n)], lambda e, n=n, pb=pb: e.tensor_tensor(out=hT[:, n, 0:T], in0=ps[pb][:, 0:T], in1=hT[:, n, 0:T], op=ALU.add))
            rmsnorm(T, g2, "g2")
            ffn(T, "w1b", "w3b", "w2b")
            for i in range(nt):
                sl = xctr[0] % 2
                xctr[0] += 1
                for c in range(8):
                    A("pe", [("hT", c), "ident"], [("ps", 2 + c // 4)], lambda e, c=c, i=i: e.transpose(out=ps[2 + c // 4][:, (c % 4) * 128:(c % 4 + 1) * 128], in_=hT[:, c, i * 128:(i + 1) * 128], identity=ident[:]))
                A("act", [("ps", 2)], [("xst", sl, 0)], lambda e, sl=sl: e.activation(out=xst[sl][:, 0:512], in_=ps[2][:, :], func=AF.Copy))
                A("dve", [("ps", 3)], [("xst", sl, 1)], lambda e, sl=sl: e.tensor_copy(out=xst[sl][:, 512:1024], in_=ps[3][:, :]))
                A("sp", [("xst", sl)], [], lambda e, i=i, sl=sl: e.dma_start(out=ydst[i * 128:(i + 1) * 128, :], in_=xst[sl][:]), dma="x%d" % sl)

        def sample_prep():
            if True:
                A("sp", [], ["S0"], lambda e: e.dma_start(out=cst[:], in_=ck.rearrange("b j c -> j b c")), dma="cs")
            A("sp", ["S0"], [], lambda e: e.dma_start(out=sk.rearrange("b j c -> j b c")[0:120], in_=cst[8:128, :, :]), dma="o2")
            tmpv = Vm[:, :, :].rearrange("p b v -> p (b v)").bitcast(F32).rearrange("p (b g u d) -> p b g u d", b=4, g=2, u=2)
            for bq in range(4):
                for u in range(2):
                    A("act" if u == 0 else "dve", ["S0"], ["Vm"], (lambda e, bq=bq, u=u: e.activation(out=tmpv[:, :, :, u, :], in_=cst[:, 4 * bq:4 * bq + 4, :].rearrange("p b (g d) -> p b g d", g=2), func=AF.Copy)) if u == 0 else (lambda e, bq=bq, u=u: e.tensor_copy(out=tmpv[:, :, :, u, :], in_=cst[:, 4 * bq:4 * bq + 4, :].rearrange("p b (g d) -> p b g d", g=2))))
                for bb in range(4):
                    for g in range(2):
                        idx = bb * 2 + g
                        A("pe", ["Vm", "ident"], [("ps", 2 + idx // 4)], lambda e, bb=bb, g=g, idx=idx: e.transpose(out=ps[2 + idx // 4][:, (idx % 4) * 128:(idx % 4 + 1) * 128], in_=tmpv[:, bb, g, :, :].rearrange("p u d -> p (u d)"), identity=ident[:]))
                for half in range(2):
                    b0 = 4 * bq + 2 * half
                    A("act" if half == 0 else "dve", [("ps", 2 + half)], ["kcT"], (lambda e, b0=b0, half=half: e.activation(out=kcT[:, :, b0:b0 + 2, :], in_=ps[2 + half][:, :].rearrange("p (b g j) -> p g b j", b=2, g=2), func=AF.Copy)) if half == 0 else (lambda e, b0=b0, half=half: e.tensor_copy(out=kcT[:, :, b0:b0 + 2, :], in_=ps[2 + half][:, :].rearrange("p (b g j) -> p g b j", b=2, g=2))))
            A("sp", ["S0"], ["S0"], lambda e: e.dma_start(out=cst[:], in_=cv.rearrange("b j c -> j b c")), dma="cs")
            A("sp", ["S0"], [], lambda e: e.dma_start(out=sv.rearrange("b j c -> j b c")[0:120], in_=cst[8:128, :, :]), dma="o2")
            for dup in range(2):
                A("act", ["S0"], ["vcdup"], lambda e, dup=dup: e.activation(out=vcdup[:, :, :, dup * 64:(dup + 1) * 64], in_=cst[:].rearrange("j b (g d) -> j b g d", g=2), func=AF.Copy))

        nblk = NTP // TB
        for bi in range(nblk):
            rows = slice(bi * TB * 128, (bi + 1) * TB * 128)
            block(TB, False, None, xpre[rows, :], None, False, False, prefix=True, prefix_last=(bi == nblk - 1), after_x=(sample_prep if bi == 1 else None))
        for bi in range(nblk):
            rows = slice(bi * TB * 128, (bi + 1) * TB * 128)
            block(TB, False, None, xp[rows, :], yp[rows, :], False, bi == nblk - 1, halo_mask=(bi == 0))
        A("sp", [("S",)], [], lambda e: e.dma_start(out=pS.rearrange("h k v -> k h v"), in_=S[:]), dma="o2")
        import os
        if os.environ.get('NOSAMPLE'):
            nops = P.emit()
            return nc, nops
        if not os.environ.get('NOSBLOCK'):
            block(1, True, None, xs, ys, True, False)
        nops = P.emit()
    return nc, nops


def _t5_bucket(d):
    d = np.maximum(d, 0)
    large = 16 + (np.log(np.maximum(d, 1) / 16) / np.log(128 / 16) * 16).astype(np.int32)
    large = np.minimum(large, 31)
    return np.where(d < 16, d, large).astype(np.int32)


def _consts():
    c = {}
    c["ident"] = np.eye(128, dtype=np.float32)
    oh = np.zeros((33, 383), np.float32)
    for i in range(383):
        d = i - 127
        if 0 <= d <= 128:
            oh[int(_t5_bucket(np.array(d))), i] = 1.0
        else:
            oh[32, i] = 1.0
    c["oh"] = oh
    s = np.arange(128)[:, None]
    t = np.arange(128)[None, :]
    c["mchunk"] = ((s // 64 == t // 64) & (s <= t)).astype(np.float32)
    c["msamp"] = ((s // 8 == t // 8) & (s <= t)).astype(np.float32)
    c["bmask"] = np.where(s // 8 == t // 8, 0.0, NEG).astype(np.float32)
    c["rstp"] = np.broadcast_to((np.arange(128) % 64 != 0).astype(np.float32), (128, 128)).copy()
    c["rsts"] = np.broadcast_to((np.arange(128) % 8 != 0).astype(np.float32), (128, 128)).copy()
    c["selm"] = (np.arange(128)[:, None] // 8 == np.arange(16)[None, :]).astype(np.float32)
    pm = np.zeros((128, 2), np.float32)
    pm[:64, 0] = 1.0
    pm[64:, 1] = 1.0
    c["pm"] = pm
    c["bd"] = (np.arange(128)[:, None] // 64 == np.arange(128)[None, :] // 64).astype(np.float32)
    return c


_CACHE = {}


def kernel(x_prompt, x_sample, cache_win_k, cache_win_v, state_hgrn,
           ffn1_norm, ffn1_w1, ffn1_w3, ffn1_w2, mix_norm, w_in, q_norm, k_norm, sinks,
           rel_bias_table, hgrn_lb_logits, hg_norm, w_up_attn, w_up_hgrn, w_out,
           ffn2_norm, ffn2_w1, ffn2_w3, ffn2_w2):
    f = lambda a: np.ascontiguousarray(np.asarray(a, dtype=np.float32))
    NC = 8
    if "nc" not in _CACHE:
        _CACHE["nc"] = build_program(16, 4)[0]
    nc = _CACHE["nc"]
    x_prompt = f(x_prompt); x_sample = f(x_sample)
    ckf = f(cache_win_k)[0].reshape(128, 128, 128); cvf = f(cache_win_v)[0].reshape(128, 128, 128)
    s0f = f(state_hgrn)[0]
    g8 = lambda g: f(np.asarray(g)[0].reshape(8, 128).T)
    shared = dict(
        w1a=f(ffn1_w1)[0], w3a=f(ffn1_w3)[0], w2a=f(ffn1_w2)[0],
        w1b=f(ffn2_w1)[0], w3b=f(ffn2_w3)[0], w2b=f(ffn2_w2)[0],
        win=f(w_in)[0], wupa=f(w_up_attn)[0], wupr=f(w_up_hgrn)[0], wout=f(w_out)[0],
        g1=g8(ffn1_norm), gm=g8(mix_norm), g2=g8(ffn2_norm),
        qn=f(np.tile(np.asarray(q_norm)[0].reshape(64, 1), (2, 1))), kn=f(np.tile(np.asarray(k_norm)[0].reshape(64, 1), (2, 1))),
        hgn=f(np.asarray(hg_norm)[0].reshape(128, 1)),
        sinks=f(np.broadcast_to(np.asarray(sinks)[0].reshape(1, 8), (128, 8))),
        lba=f(np.asarray(hgrn_lb_logits)[0].reshape(4, 128).T), lbb=f(np.asarray(hgrn_lb_logits)[1].reshape(4, 128).T),
        tab=f(rel_bias_table),
    )
    shared.update(_consts())
    zeros_p = np.zeros((2048, D), np.float32)
    in_maps = []
    for c in range(NC):
        m = dict(shared)
        p, r = c // 2, c % 2
        m["xp"] = x_prompt[p, r * 2048:(r + 1) * 2048]
        m["xpre"] = x_prompt[p, 0:2048] if r == 1 else zeros_p
        m["pmask"] = np.full((128, 1), 0.0 if r == 1 else NEG, np.float32)
        m["xs"] = x_sample[16 * c:16 * (c + 1)].reshape(128, D)
        m["ck"] = ckf[16 * c:16 * (c + 1)]
        m["cv"] = cvf[16 * c:16 * (c + 1)]
        m["s0"] = s0f[16 * c:16 * (c + 1)]
        in_maps.append(m)
    res = run_bass_kernel_spmd(nc, in_maps, core_ids=list(range(NC))).results
    yp = np.stack([np.concatenate([res[2 * p]["yp"], res[2 * p + 1]["yp"]], 0) for p in range(4)]).astype(np.float32)
    ys = np.concatenate([res[c]["ys"].reshape(16, 8, D) for c in range(NC)]).astype(np.float32)
    pk = np.stack([res[2 * p + 1]["pk"].reshape(128, 2, 64) for p in range(4)])[None].astype(np.float32)
    pv = np.stack([res[2 * p + 1]["pv"].reshape(128, 2, 64) for p in range(4)])[None].astype(np.float32)
    pS = np.stack([res[2 * p + 1]["pS"] for p in range(4)])[None].astype(np.float32)
    sk = np.concatenate([res[c]["sk"].reshape(16, 128, 2, 64) for c in range(NC)])[None].astype(np.float32)
    sv = np.concatenate([res[c]["sv"].reshape(16, 128, 2, 64) for c in range(NC)])[None].astype(np.float32)
    sS = np.concatenate([res[c]["sS"] for c in range(NC)])[None].astype(np.float32)
    return (yp, ys, pk, pv, pS, sk, sv, sS)
```
